# Optimizing a Trainium2 kernel written in Bass

```python
import math
import jax
import jax.numpy as jnp
from jax import lax
import numpy as np

D_MODEL = 2048
BATCH = 1
SEQ = 16384
DEPTH = 1
DEC_BATCH = 2
DEC_SEQ = 16384
PAST_LEN = 128

HEAD_DIM = 128
N_QK_HEADS = 16
N_V_HEADS = 32
QK_WIDTH = N_QK_HEADS * HEAD_DIM
V_WIDTH = N_V_HEADS * HEAD_DIM
SHORT_CONV = 5
CHUNK = 64
N_DIR = 2
CONF_WIDTH = D_MODEL
CONF_KERNEL = 31
D_FF = 5632
FFN_CONV = 3
RMS_EPS = 1e-6
LN_EPS = 1e-5
L2_EPS = 1e-6
IN_COLS = 2 * QK_WIDTH + 2 * V_WIDTH + 2 * N_DIR * N_V_HEADS + 2 * CONF_WIDTH + 2 * D_MODEL

kernel_name = 'bidir_gated_deltanet_conformer_encoder'

F32 = jnp.float32


def rms_norm(x, w):
    xf = x.astype(F32)
    y = xf * lax.rsqrt(jnp.mean(xf * xf, axis=-1, keepdims=True) + RMS_EPS)
    return (y * w.astype(F32)).astype(x.dtype)


def layer_norm(x, w, b):
    xf = x.astype(F32)
    mu = jnp.mean(xf, axis=-1, keepdims=True)
    xc = xf - mu
    var = jnp.mean(xc * xc, axis=-1, keepdims=True)
    return (xc * lax.rsqrt(var + LN_EPS) * w.astype(F32) + b.astype(F32)).astype(x.dtype)


def l2_normalize(x):
    return x * lax.rsqrt(jnp.sum(x * x, axis=-1, keepdims=True) + L2_EPS)


def depthwise_conv(x, w):
    pad = (w.shape[0] - 1) // 2
    return lax.conv_general_dilated(
        x, w[:, None, :].astype(x.dtype), (1,), [(pad, pad)],
        dimension_numbers=('NWC', 'WIO', 'NWC'), feature_group_count=x.shape[-1])


def split_in_proj(proj):
    sizes = (QK_WIDTH, QK_WIDTH, V_WIDTH, V_WIDTH, N_DIR * N_V_HEADS, N_DIR * N_V_HEADS,
             2 * CONF_WIDTH, 2 * D_MODEL)
    return jnp.split(proj, np.cumsum(sizes)[:-1].tolist(), axis=-1)


def chunk_gated_delta_rule(q, k, v, g, beta):
    bsz, seq, nh, dk = q.shape
    dv = v.shape[-1]
    n_chunks = seq // CHUNK

    def to_chunks(t):
        t = t.reshape((bsz, n_chunks, CHUNK, nh) + t.shape[3:])
        return jnp.moveaxis(t, 3, 1)

    q, k, v, g, beta = (to_chunks(t) for t in (q, k, v, g, beta))
    q = q * (dk ** -0.5)
    g = jnp.cumsum(g, axis=-1)
    idx = jnp.arange(CHUNK)
    incl = idx[:, None] >= idx[None, :]
    strict = idx[:, None] > idx[None, :]
    diff = g[..., :, None] - g[..., None, :]
    decay = jnp.where(incl, jnp.exp(jnp.where(incl, diff, 0.0)), 0.0)

    k_beta = k * beta[..., None]
    v_beta = v * beta[..., None]
    lmat = jnp.where(strict, jnp.einsum('bhncd,bhnsd->bhncs', k_beta, k) * decay, 0.0)
    rhs = jnp.concatenate([v_beta, k_beta * jnp.exp(g)[..., None]], axis=-1)
    sol = lax.linalg.triangular_solve(lmat, rhs, left_side=True, lower=True, unit_diagonal=True)
    u, w = sol[..., :dv], sol[..., dv:]

    qk_intra = jnp.where(incl, jnp.einsum('bhncd,bhnsd->bhncs', q, k) * decay, 0.0)
    g_last = g[..., -1]
    q_dec = q * jnp.exp(g)[..., None]
    k_dec = k * jnp.exp(g_last[..., None] - g)[..., None]

    def step(state, inp):
        qd, kd, wc, uc, ac, gl = inp
        v_new = uc - jnp.einsum('bhck,bhkv->bhcv', wc, state)
        out = jnp.einsum('bhck,bhkv->bhcv', qd, state) + jnp.einsum('bhcs,bhsv->bhcv', ac, v_new)
        state = state * jnp.exp(gl)[..., None, None] + jnp.einsum('bhck,bhcv->bhkv', kd, v_new)
        return state, out

    xs = tuple(jnp.moveaxis(t, 2, 0) for t in (q_dec, k_dec, w, u, qk_intra, g_last))
    state0 = jnp.zeros((bsz, nh, dk, dv), F32)
    _, out = lax.scan(step, state0, xs)
    out = jnp.moveaxis(jnp.moveaxis(out, 0, 2), 1, 3)
    return out.reshape(bsz, seq, nh, dv)


def gated_deltanet_bidir(q, k, v, z, a, b, conv_w, a_log, dt_bias, norm_w, w_out):
    bsz, seq, _ = q.shape
    qkv = jax.nn.silu(depthwise_conv(jnp.concatenate([q, k, v], axis=-1), conv_w))
    q, k, v = jnp.split(qkv, [QK_WIDTH, 2 * QK_WIDTH], axis=-1)
    rep = N_V_HEADS // N_QK_HEADS
    q = jnp.repeat(l2_normalize(q.astype(F32).reshape(bsz, seq, N_QK_HEADS, HEAD_DIM)), rep, axis=2)
    k = jnp.repeat(l2_normalize(k.astype(F32).reshape(bsz, seq, N_QK_HEADS, HEAD_DIM)), rep, axis=2)
    v = v.astype(F32).reshape(bsz, seq, N_V_HEADS, HEAD_DIM)
    a = a.astype(F32).reshape(bsz, seq, N_DIR, N_V_HEADS)
    g = -jnp.exp(a_log.astype(F32)) * jax.nn.softplus(a + dt_bias.astype(F32))
    beta = jax.nn.sigmoid(b.astype(F32).reshape(bsz, seq, N_DIR, N_V_HEADS))
    o_fwd = chunk_gated_delta_rule(q, k, v, g[:, :, 0], beta[:, :, 0])
    flip = lambda t: jnp.flip(t, axis=1)
    o_bwd = flip(chunk_gated_delta_rule(flip(q), flip(k), flip(v), flip(g[:, :, 1]), flip(beta[:, :, 1])))
    o = rms_norm(o_fwd + o_bwd, norm_w) * jax.nn.silu(z.astype(F32).reshape(bsz, seq, N_V_HEADS, HEAD_DIM))
    return o.reshape(bsz, seq, V_WIDTH).astype(z.dtype) @ w_out


def conformer_conv(glu_in, conv_w, conv_b, ln_w, ln_b, w_out):
    val, gate = jnp.split(glu_in, 2, axis=-1)
    hcv = val * jax.nn.sigmoid(gate)
    hcv = depthwise_conv(hcv, conv_w) + conv_b.astype(hcv.dtype)
    hcv = jax.nn.silu(layer_norm(hcv, ln_w, ln_b))
    return hcv @ w_out


def encoder_layer(x, mix_norm_pre, w_in, short_conv_w, a_log, dt_bias, delta_norm_w, w_delta_out,
                  conf_conv_w, conf_conv_b, conf_ln_w, conf_ln_b, w_conf_out, w_mix_out,
                  mix_norm_post, ffn_norm_pre, w_up, ffn_conv_w, w_down, ffn_norm_post):
    h = rms_norm(x, mix_norm_pre)
    q, k, v, z, a, b, glu_in, gates = split_in_proj(h @ w_in)
    y_a = gated_deltanet_bidir(q, k, v, z, a, b, short_conv_w, a_log, dt_bias, delta_norm_w, w_delta_out)
    y_b = conformer_conv(glu_in, conf_conv_w, conf_conv_b, conf_ln_w, conf_ln_b, w_conf_out)
    gate_a, gate_b = jnp.split(jax.nn.sigmoid(gates), 2, axis=-1)
    merged = gate_a * y_a + gate_b * y_b
    x = x + rms_norm(merged @ w_mix_out, mix_norm_post)
    h = rms_norm(x, ffn_norm_pre)
    up = depthwise_conv(h @ w_up, ffn_conv_w)
    gt, val = jnp.split(up, 2, axis=-1)
    f = (jax.nn.silu(gt) * val) @ w_down
    return x + rms_norm(f, ffn_norm_post)


def run_trunk(x, params):
    for i in range(DEPTH):
        x = encoder_layer(x, *(p[i] for p in params))
    return x


def setup_inputs(seed: int = 0) -> dict:
    key = jax.random.key(seed)
    ks = jax.random.split(key, 24)

    def normal(k, shape, scale):
        return scale * jax.random.normal(k, shape, F32)

    def gain(k, shape):
        return 1.0 + 0.02 * jax.random.normal(k, shape, F32)

    dt = jnp.exp(jax.random.uniform(ks[5], (DEPTH, N_DIR, N_V_HEADS), F32, math.log(1e-3), math.log(1e-1)))
    return {
        'x_prompt': jax.random.normal(ks[0], (BATCH, SEQ, D_MODEL), F32),
        'x_sample': jax.random.normal(ks[1], (DEC_BATCH, DEC_SEQ, D_MODEL), F32),
        'mix_norm_pre': gain(ks[2], (DEPTH, D_MODEL)),
        'w_in': normal(ks[3], (DEPTH, D_MODEL, IN_COLS), D_MODEL ** -0.5),
        'short_conv_w': normal(ks[4], (DEPTH, SHORT_CONV, 2 * QK_WIDTH + V_WIDTH), SHORT_CONV ** -0.5),
        'a_log': jnp.log(jax.random.uniform(ks[6], (DEPTH, N_DIR, N_V_HEADS), F32, 1.0, 16.0)),
        'dt_bias': dt + jnp.log(-jnp.expm1(-dt)),
        'delta_norm_w': gain(ks[7], (DEPTH, HEAD_DIM)),
        'w_delta_out': normal(ks[8], (DEPTH, V_WIDTH, D_MODEL), V_WIDTH ** -0.5),
        'conf_conv_w': normal(ks[9], (DEPTH, CONF_KERNEL, CONF_WIDTH), CONF_KERNEL ** -0.5),
        'conf_conv_b': normal(ks[10], (DEPTH, CONF_WIDTH), 0.02),
        'conf_ln_w': gain(ks[11], (DEPTH, CONF_WIDTH)),
        'conf_ln_b': normal(ks[12], (DEPTH, CONF_WIDTH), 0.02),
        'w_conf_out': normal(ks[13], (DEPTH, CONF_WIDTH, D_MODEL), CONF_WIDTH ** -0.5),
        'w_mix_out': normal(ks[14], (DEPTH, D_MODEL, D_MODEL), D_MODEL ** -0.5),
        'mix_norm_post': gain(ks[15], (DEPTH, D_MODEL)),
        'ffn_norm_pre': gain(ks[16], (DEPTH, D_MODEL)),
        'w_up': normal(ks[17], (DEPTH, D_MODEL, 2 * D_FF), D_MODEL ** -0.5),
        'ffn_conv_w': normal(ks[18], (DEPTH, FFN_CONV, 2 * D_FF), FFN_CONV ** -0.5),
        'w_down': normal(ks[19], (DEPTH, D_FF, D_MODEL), D_FF ** -0.5),
        'ffn_norm_post': gain(ks[20], (DEPTH, D_MODEL)),
    }


def reference(x_prompt, x_sample, mix_norm_pre, w_in, short_conv_w, a_log, dt_bias, delta_norm_w,
              w_delta_out, conf_conv_w, conf_conv_b, conf_ln_w, conf_ln_b, w_conf_out, w_mix_out,
              mix_norm_post, ffn_norm_pre, w_up, ffn_conv_w, w_down, ffn_norm_post):
    params = (mix_norm_pre, w_in, short_conv_w, a_log, dt_bias, delta_norm_w, w_delta_out,
              conf_conv_w, conf_conv_b, conf_ln_w, conf_ln_b, w_conf_out, w_mix_out,
              mix_norm_post, ffn_norm_pre, w_up, ffn_conv_w, w_down, ffn_norm_post)
    y_prompt = run_trunk(x_prompt, params)
    y_sample = run_trunk(x_sample, params)
    return (y_prompt, y_sample)
```

```python
import math
import numpy as np
import concourse.bass as bass
import concourse.mybir as mybir
from concourse.bass_utils import run_bass_kernel_spmd

F32 = mybir.dt.float32
BF16 = mybir.dt.bfloat16
AF = mybir.ActivationFunctionType
ALU = mybir.AluOpType
AX = mybir.AxisListType

D = 2048
NCORE = 8
SEQ = 16384
NSEQ = 3
SEG = SEQ // NCORE
TO = 410
WX = TO + 2
WH = TO + 32
NT = 5
HALO = 16
DFF = 5632
RMS_EPS = 1e-6
LN_EPS = 1e-5


class Buf:
    __slots__ = ("name", "w", "r", "dsem", "dcnt", "psum")

    def __init__(self, name="", psum=False):
        self.name = name
        self.psum = psum
        self.w = {}
        self.r = {}
        self.dsem = None
        self.dcnt = 0


class FW:
    LIMIT = 30000
    SAME_ENGINE_SYNC = True

    def __init__(self, nc, sems):
        self.nc = nc
        self.free = list(sems)
        self.eng = {"pe": nc.tensor, "dve": nc.vector, "act": nc.scalar,
                    "pool": nc.gpsimd, "sp": nc.sync}
        self.csem = {}
        self.ccnt = {}
        for e in ("pe", "dve", "act", "pool"):
            self.csem[e] = self.free.pop()
            self.ccnt[e] = 0
        self.seen = {e: {} for e in self.eng}
        self.ninstr = 0
        self.alld = {}

    def _collect(self, reads, writes):
        need = {}
        for b in reads:
            for s, v in b.w.items():
                if need.get(s, 0) < v:
                    need[s] = v
        for b in writes:
            for s, v in b.w.items():
                if need.get(s, 0) < v:
                    need[s] = v
            for s, v in b.r.items():
                if need.get(s, 0) < v:
                    need[s] = v
        return need

    def _wait(self, e, need):
        seen = self.seen[e]
        own = self.csem.get(e)
        for s, v in need.items():
            if seen.get(s, 0) >= v:
                continue
            if s is own and (e == "pe" or not self.SAME_ENGINE_SYNC):
                continue
            self.eng[e].wait_ge(s, v)
            seen[s] = v
            self.ninstr += 1

    def _mark(self, tok, reads, writes, excl=True):
        s, v = tok
        for b in reads:
            if b.r.get(s, 0) < v:
                b.r[s] = v
        for b in writes:
            if excl:
                b.w = {s: v}
                b.r = {}
            else:
                if b.w.get(s, 0) < v:
                    b.w[s] = v

    def op(self, e, fn, reads=(), writes=(), excl=True):
        pr = [b for b in reads if b.psum]
        if pr:
            reads = [b for b in reads if not b.psum]
            if excl:
                writes = list(writes) + pr
            else:
                self._wait(e, self._collect((), pr))
        self._wait(e, self._collect(reads, writes))
        ins = fn(self.eng[e])
        if self.ccnt[e] >= self.LIMIT:
            self.csem[e] = self.free.pop()
            self.ccnt[e] = 0
        self.ccnt[e] += 1
        tok = (self.csem[e], self.ccnt[e])
        ins.then_inc(tok[0], 1)
        self._mark(tok, reads, writes, excl)
        if pr and not excl:
            self._mark(tok, (), pr, True)
        self.ninstr += 1
        return tok

    def dma(self, q, out, in_, reads=(), writes=(), owner=None, excl=True, **kw):
        self._wait(q, self._collect(reads, writes))
        ins = self.eng[q].dma_start(out=out, in_=in_, **kw)
        o = owner or (writes[0] if writes else reads[0])
        if o.dsem is None or o.dcnt + 16 > self.LIMIT:
            o.dsem = self.free.pop()
            o.dcnt = 0
        o.dcnt += 16
        tok = (o.dsem, o.dcnt)
        ins.then_inc(tok[0], 16)
        self.alld[tok[0]] = tok[1]
        self._mark(tok, reads, writes, excl)
        self.ninstr += 1
        return tok

    def barrier(self):
        need = dict(self.alld)
        for e in ("pe", "dve", "act", "pool"):
            if self.ccnt[e] > 0:
                need[self.csem[e]] = self.ccnt[e]
        for e in ("pe", "dve", "act", "pool", "sp"):
            seen = self.seen[e]
            for s_, v in need.items():
                if seen.get(s_, 0) >= v:
                    continue
                self.eng[e].wait_ge(s_, v)
                seen[s_] = v
                self.ninstr += 1

    def wait_all(self, e, bufs):
        self._wait(e, self._collect((), bufs))


class T:
    def __init__(self, t, nchunk=1, name=""):
        self.t = t
        self.bufs = [Buf(f"{name}{i}") for i in range(nchunk)]

    @property
    def buf(self):
        return self.bufs[0]


def build(cfg):
    nseg = cfg.get("nseg", NSEQ)
    tiles = cfg.get("tiles", list(range(NT)))
    dbg = cfg.get("dbg", [])
    nc = bass.Bass("TRN2", target_bir_lowering=False)
    from contextlib import ExitStack
    es = ExitStack()

    def din(name, shape, dt=F32):
        return nc.dram_tensor(name, list(shape), dt, kind="ExternalInput")

    def dint(name, shape, dt):
        return nc.dram_tensor(name, list(shape), dt, kind="Internal")

    def dout(name, shape, dt=F32):
        return nc.dram_tensor(name, list(shape), dt, kind="ExternalOutput")

    xp = din("xp", [NSEQ, SEG + 2 * HALO, D])
    msk = din("msk", [NSEQ * NT, 128, WX])
    wz_f = din("wz", [32, 128, D])
    wglu_f = din("wglu", [32, 128, D])
    wgt_f = din("wgt", [32, 128, D])
    wdo_f = din("wdo", [16, 128, 4096])
    wco_f = din("wco", [16, 128, D])
    wmo_f = din("wmo", [16, 128, D])
    wup_f = din("wup", [88, 128, D])
    wdn_f = din("wdn", [32, 128, DFF // 2])
    vecs = din("vecs", [128, 16, 8])
    cconv = din("cconv", [128, 16, 31])
    fconv = din("fconv", [128, 88, 3])
    dnw = din("dnw", [128, 1])
    ident_d = din("ident", [128, 128])
    o_ext = din("o_ext", [4096, NSEQ, SEG + 2], BF16) if cfg.get("ext_o") else None
    y = dout("y", [NSEQ, SEG, D])

    wz_b = dint("wz_b", [32, 128, D], BF16)
    wglu_b = dint("wglu_b", [32, 128, D], BF16)
    wgt_b = dint("wgt_b", [32, 128, D], BF16)
    wdo_b = dint("wdo_b", [16, 128, 4096], BF16)
    wco_b = dint("wco_b", [16, 128, D], BF16)
    wmo_b = dint("wmo_b", [16, 128, D], BF16)
    wup_b = dint("wup_b", [88, 128, D], BF16)
    wdn_b = dint("wdn_b", [32, 128, DFF // 2], BF16)

    dbg_out = {}
    dbg_bufs = []
    outb = Buf("out")
    o_buf_dram = Buf("o_dram")
    if o_ext is not None:
        o_src = lambda m, s_, c0: o_ext[m * 128:(m + 1) * 128, s_, c0:c0 + WX]

    sems = [es.enter_context(nc.semaphore(f"s{i}")) for i in range(100)]
    fw = FW(nc, sems)

    def sb(name, shape, dt, nchunk=1):
        return T(es.enter_context(nc.sbuf_tensor(name, list(shape), dt)), nchunk, name)

    def ps(name, shape, dt):
        t_ = T(es.enter_context(nc.psum_tensor(name, list(shape), dt)), 1, name)
        t_.bufs[0].psum = True
        return t_

    F1 = sb("F1", [128, 16, WX], F32, 16)
    F2 = sb("F2", [128, 16, WX], F32, 16)
    B1 = sb("B1", [128, 16, WH], BF16, 16)
    B2 = sb("B2", [128, 16, WH], BF16, 16)
    B3 = sb("B3", [128, 16, WX], BF16, 16)
    B4 = sb("B4", [128, 16, WX], BF16, 16)
    BF = sb("BF", [128, 44, WX], BF16, 44)
    xt = [sb(f"xt{i}", [128, D], F32) for i in range(2)]
    xn = sb("xn", [128, D], BF16)
    NWS = 3
    wsl = [sb(f"w{i}", [128, 4096], BF16) for i in range(NWS)]
    tmpA = [sb(f"tA{i}", [128, WH], F32) for i in range(3)]
    tmpB = [sb(f"tB{i}", [128, WX], F32) for i in range(3)]
    obuf = [sb(f"ob{i}", [128, WX], BF16) for i in range(2)]
    rowA = sb("rowA", [128, WH], F32)
    rowB = sb("rowB", [128, WH], F32)
    mskt = sb("mskt", [128, WX], F32)
    ss = sb("ss", [128, 4], F32)
    identf = sb("identf", [128, 128], F32)
    identb = sb("identb", [128, 128], BF16)
    onesf = sb("onesf", [128, 128], F32)
    vec = sb("vec", [128, 16, 8], F32)
    gpre = sb("gpre", [128, 16], F32)
    gpost = sb("gpost", [128, 16], F32)
    gffn = sb("gffn", [128, 16], F32)
    gfpost = sb("gfpost", [128, 16], F32)
    cbias = sb("cbias", [128, 16], F32)
    lnw = sb("lnw", [128, 16], F32)
    lnb = sb("lnb", [128, 16], F32)
    ccw = sb("ccw", [128, 16, 31], F32)
    fcw = sb("fcw", [128, 88, 3], F32)

    NPS = 4
    mmps = [ps(f"mm{i}", [128, 512], F32) for i in range(NPS)]
    tpb = ps("tpb", [128, 8, 128], BF16)
    tpf = [ps(f"tpf{i}", [128, 4, 128], F32) for i in range(2)]
    stp = ps("stp", [128, 512], F32)

    st = {"wi": 0, "pi": 0, "ti": 0, "xi": 0, "oi": 0, "fi": 0}

    fw.dma("sp", identf.t[:, :], ident_d[:, :], writes=[identf.buf])
    fw.dma("sp", vec.t[:, :, :], vecs[:, :, :], writes=[vec.buf])
    fw.dma("sp", ccw.t[:, :, :], cconv[:, :, :], writes=[ccw.buf])
    fw.dma("sp", fcw.t[:, :, :], fconv[:, :, :], writes=[fcw.buf])
    fw.op("dve", lambda e: e.tensor_copy(out=identb.t[:, :], in_=identf.t[:, :]),
          reads=[identf.buf], writes=[identb.buf])
    fw.op("dve", lambda e: e.memset(onesf.t[:, :], 1.0), writes=[onesf.buf])
    for i, gt_ in enumerate((gpre, gpost, gffn, gfpost, cbias, lnw, lnb)):
        fw.op("dve", lambda e, i=i, gt_=gt_: e.tensor_copy(out=gt_.t[:, :], in_=vec.t[:, :, i]),
              reads=[vec.buf], writes=[gt_.buf])

    wbufs = {}
    for name, src, dst, mcn in (("wglu", wglu_f, wglu_b, 32), ("wgt", wgt_f, wgt_b, 32), ("wz", wz_f, wz_b, 32),
                                ("wdo", wdo_f, wdo_b, 16), ("wco", wco_f, wco_b, 16), ("wmo", wmo_f, wmo_b, 16),
                                ("wup", wup_f, wup_b, 88), ("wdn", wdn_f, wdn_b, 32)):
        wb_ = Buf(name)
        wbufs[name] = [wb_] * mcn
        if cfg.get("skip_" + name):
            continue
        G = 4
        for m0 in range(0, mcn, G):
            fw.dma("pool", dst[m0:m0 + G, :, :], src[m0:m0 + G, :, :], writes=[wb_], excl=False)

    def dump(name, ap, shape, bufs, dt=F32):
        if name not in dbg:
            return
        if name not in dbg_out:
            dbg_out[name] = dout("dbg_" + name, shape, dt)
        b = Buf("dbg")
        dbg_bufs.append(b)
        fw.dma("sp", dbg_out[name][tuple(slice(None) for _ in shape)], ap, reads=bufs, writes=[b])

    def linear(wd, wname, order, KC, rhs, rbufs, N, epi, ksplit=1):
        KP = KC // ksplit
        for mc in order:
            pt = mmps[st["pi"] % NPS]
            st["pi"] += 1
            for part in range(ksplit):
                slot = wsl[st["wi"] % NWS]
                st["wi"] += 1
                wi = mc * ksplit + part
                fw.dma("sp", slot.t[:, 0:KP * 128], wd[wi, :, :], reads=[wbufs[wname][wi]], writes=[slot.buf])
                for k in range(KP):
                    kc = part * KP + k
                    fw.op("pe", lambda e, kc=kc, k=k, slot=slot: e.matmul(
                        pt.t[:, 0:N], lhsT=slot.t[:, k * 128:(k + 1) * 128], rhs=rhs(kc),
                        start=(kc == 0), stop=(kc == KC - 1)),
                          reads=[slot.buf, rbufs(kc)], writes=[pt.buf])
            epi(mc, pt)

    def bcast_sum(src_ap_fn, src_bufs, nch, N):
        for c in range(nch):
            fw.op("pe", lambda e, c=c: e.matmul(stp.t[:, 0:N], lhsT=onesf.t[:, :], rhs=src_ap_fn(c),
                                                start=(c == 0), stop=(c == nch - 1)),
                  reads=[onesf.buf, src_bufs(c)], writes=[stp.buf])

    def tile(s, ti):
        a = min(TO * ti, SEG - TO)
        r0 = 0
        while r0 < WH:
            n = min(128, WH - r0)
            xb = xt[st["xi"] % 2]
            st["xi"] += 1
            fw.dma("sp", xb.t[0:n, :], xp[s, a + r0:a + r0 + n, :], writes=[xb.buf])
            fw.op("dve", lambda e: e.memset(ss.t[:, 0:1], 0.0), writes=[ss.buf])
            fw.op("act", lambda e: e.activation(out=xn.t[0:n, :], in_=xb.t[0:n, :], func=AF.Square,
                                                accum_out=ss.t[0:n, 0:1]),
                  reads=[xb.buf], writes=[xn.buf, ss.buf])
            fw.op("act", lambda e: e.activation(out=ss.t[0:n, 1:2], in_=ss.t[0:n, 0:1], func=AF.Sqrt,
                                                bias=RMS_EPS, scale=1.0 / D),
                  reads=[ss.buf], writes=[ss.buf])
            fw.op("dve", lambda e: e.reciprocal(out=ss.t[0:n, 1:2], in_=ss.t[0:n, 1:2]),
                  reads=[ss.buf], writes=[ss.buf])
            fw.op("dve", lambda e: e.tensor_scalar(out=xn.t[0:n, :], in0=xb.t[0:n, :], scalar1=ss.t[0:n, 1:2],
                                                   scalar2=None, op0=ALU.mult),
                  reads=[xb.buf, ss.buf], writes=[xn.buf])
            for k4 in range(4):
                for j in range(4):
                    kc = k4 * 4 + j
                    fw.op("pe", lambda e, kc=kc, j=j: e.transpose(tpb.t[:, j, 0:n], xn.t[0:n, kc * 128:(kc + 1) * 128],
                                                                  identb.t[0:n, 0:n]),
                          reads=[xn.buf, identb.buf], writes=[tpb.buf])
                for j in range(4):
                    kc = k4 * 4 + j
                    fw.op("act", lambda e, kc=kc, j=j: e.activation(out=B1.t[:, kc, r0:r0 + n], in_=tpb.t[:, j, 0:n],
                                                                    func=AF.Copy, scale=gpre.t[:, kc:kc + 1]),
                          reads=[tpb.buf, gpre.buf], writes=[B1.bufs[kc]], excl=(r0 == 0))
            lo = max(r0, HALO - 1)
            hi = min(r0 + n, HALO - 1 + WX)
            if hi > lo:
                for k4 in range(4):
                    tp = tpf[k4 % 2]
                    for j in range(4):
                        kc = k4 * 4 + j
                        fw.op("pe", lambda e, kc=kc, j=j, tp=tp: e.transpose(tp.t[:, j, 0:n],
                                                                             xb.t[0:n, kc * 128:(kc + 1) * 128],
                                                                             identf.t[0:n, 0:n]),
                              reads=[xb.buf, identf.buf], writes=[tp.buf])
                    fw.op("dve", lambda e, k4=k4, tp=tp: e.tensor_copy(
                        out=F1.t[:, k4 * 4:k4 * 4 + 4, lo - (HALO - 1):hi - (HALO - 1)],
                        in_=tp.t[:, :, lo - r0:hi - r0]),
                          reads=[tp.buf], writes=F1.bufs[k4 * 4:k4 * 4 + 4], excl=(lo == HALO - 1))
            r0 += n
        dump("hT", B1.t[:, :, :], [128, 16, WH], B1.bufs, BF16)
        dump("xT", F1.t[:, :, :], [128, 16, WX], F1.bufs)

        hrhs = lambda kc: B1.t[:, kc, 0:WH]
        hbuf = lambda kc: B1.bufs[kc]
        gl_order = []
        for i in range(16):
            gl_order += [16 + i, i]
        sgt = {}

        def epi_glu(mc, pt):
            if mc >= 16:
                tb = tmpA[st["ti"] % 3]
                st["ti"] += 1
                fw.op("act", lambda e: e.activation(out=tb.t[:, 0:WH], in_=pt.t[:, 0:WH], func=AF.Sigmoid),
                      reads=[pt.buf], writes=[tb.buf])
                sgt["t"] = tb
            else:
                tb = sgt["t"]
                fw.op("dve", lambda e: e.tensor_tensor(out=B2.t[:, mc, 0:WH], in0=pt.t[:, 0:WH], in1=tb.t[:, 0:WH],
                                                       op=ALU.mult),
                      reads=[pt.buf, tb.buf], writes=[B2.bufs[mc]])
        linear(wglu_b, "wglu", gl_order, 16, hrhs, hbuf, WH, epi_glu)
        dump("hcv", B2.t[:, :, :], [128, 16, WH], B2.bufs, BF16)

        for kc in range(16):
            ce = "dve"
            fw.op(ce, lambda e, kc=kc: e.tensor_scalar(out=F2.t[:, kc, 0:WX], in0=B2.t[:, kc, 0:WX],
                                                       scalar1=ccw.t[:, kc, 0:1], scalar2=cbias.t[:, kc:kc + 1],
                                                       op0=ALU.mult, op1=ALU.add),
                  reads=[B2.bufs[kc], ccw.buf, cbias.buf], writes=[F2.bufs[kc]])
            for j in range(1, 31):
                fw.op(ce, lambda e, kc=kc, j=j: e.scalar_tensor_tensor(out=F2.t[:, kc, 0:WX], in0=B2.t[:, kc, j:j + WX],
                                                                       scalar=ccw.t[:, kc, j:j + 1], in1=F2.t[:, kc, 0:WX],
                                                                       op0=ALU.mult, op1=ALU.add),
                      reads=[B2.bufs[kc], ccw.buf], writes=[F2.bufs[kc]])
        dump("cv", F2.t[:, :, :], [128, 16, WX], F2.bufs)
        bcast_sum(lambda c: F2.t[:, c, 0:WX], lambda c: F2.bufs[c], 16, WX)
        fw.op("act", lambda e: e.activation(out=rowA.t[:, 0:WX], in_=stp.t[:, 0:WX], func=AF.Copy, scale=1.0 / D),
              reads=[stp.buf], writes=[rowA.buf])
        sqb = []
        for kc in range(16):
            tb = tmpA[st["ti"] % 3]
            st["ti"] += 1
            fw.op("act", lambda e, kc=kc, tb=tb: e.activation(out=tb.t[:, 0:WX], in_=F2.t[:, kc, 0:WX], func=AF.Square),
                  reads=[F2.bufs[kc]], writes=[tb.buf])
            fw.op("pe", lambda e, kc=kc, tb=tb: e.matmul(stp.t[:, 0:WX], lhsT=onesf.t[:, :], rhs=tb.t[:, 0:WX],
                                                         start=(kc == 0), stop=(kc == 15)),
                  reads=[onesf.buf, tb.buf], writes=[stp.buf])
        tb = tmpB[0]
        fw.op("dve", lambda e: e.tensor_tensor(out=tb.t[:, 0:WX], in0=rowA.t[:, 0:WX], in1=rowA.t[:, 0:WX], op=ALU.mult),
              reads=[rowA.buf], writes=[tb.buf])
        fw.op("dve", lambda e: e.scalar_tensor_tensor(out=rowB.t[:, 0:WX], in0=stp.t[:, 0:WX], scalar=1.0 / D,
                                                      in1=tb.t[:, 0:WX], op0=ALU.mult, op1=ALU.subtract),
              reads=[stp.buf, tb.buf], writes=[rowB.buf])
        fw.op("act", lambda e: e.activation(out=rowB.t[:, 0:WX], in_=rowB.t[:, 0:WX], func=AF.Sqrt, bias=LN_EPS, scale=1.0),
              reads=[rowB.buf], writes=[rowB.buf])
        fw.op("dve", lambda e: e.reciprocal(out=rowB.t[:, 0:WX], in_=rowB.t[:, 0:WX]), reads=[rowB.buf], writes=[rowB.buf])
        for kc in range(16):
            tb = tmpB[1 + kc % 2]
            ce = "dve"
            fw.op(ce, lambda e, kc=kc, tb=tb: e.tensor_tensor(out=tb.t[:, 0:WX], in0=F2.t[:, kc, 0:WX], in1=rowA.t[:, 0:WX],
                                                              op=ALU.subtract),
                  reads=[F2.bufs[kc], rowA.buf], writes=[tb.buf])
            fw.op(ce, lambda e, kc=kc, tb=tb: e.tensor_tensor(out=tb.t[:, 0:WX], in0=tb.t[:, 0:WX], in1=rowB.t[:, 0:WX],
                                                              op=ALU.mult),
                  reads=[tb.buf, rowB.buf], writes=[tb.buf])
            fw.op("act", lambda e, kc=kc, tb=tb: e.activation(out=B3.t[:, kc, 0:WX], in_=tb.t[:, 0:WX], func=AF.Silu,
                                                              bias=lnb.t[:, kc:kc + 1], scale=lnw.t[:, kc:kc + 1]),
                  reads=[tb.buf, lnb.buf, lnw.buf], writes=[B3.bufs[kc]])
        dump("cb", B3.t[:, :, :], [128, 16, WX], B3.bufs, BF16)

        xrhs = lambda kc: B1.t[:, kc, HALO - 1:HALO - 1 + WX]
        cbrhs = lambda kc: B3.t[:, kc, 0:WX]
        cbbuf = lambda kc: B3.bufs[kc]
        for mc in range(16):
            def epi_gb(m, pt):
                tb = tmpA[st["ti"] % 3]
                st["ti"] += 1
                fw.op("act", lambda e: e.activation(out=tb.t[:, 0:WX], in_=pt.t[:, 0:WX], func=AF.Sigmoid),
                      reads=[pt.buf], writes=[tb.buf])
                sgt["t"] = tb
            linear(wgt_b, "wgt", [16 + mc], 16, xrhs, hbuf, WX, epi_gb)

            def epi_yb(m, pt):
                tb = sgt["t"]
                fw.op("dve", lambda e: e.tensor_tensor(out=B2.t[:, m, 0:WX], in0=pt.t[:, 0:WX], in1=tb.t[:, 0:WX],
                                                       op=ALU.mult),
                      reads=[pt.buf, tb.buf], writes=[B2.bufs[m]])
            linear(wco_b, "wco", [mc], 16, cbrhs, cbbuf, WX, epi_yb)

        def epi_ga(m, pt):
            fw.op("act", lambda e: e.activation(out=B4.t[:, m, 0:WX], in_=pt.t[:, 0:WX], func=AF.Sigmoid),
                  reads=[pt.buf], writes=[B4.bufs[m]])
        linear(wgt_b, "wgt", list(range(16)), 16, xrhs, hbuf, WX, epi_ga)
        dump("gbyb", B2.t[:, :, :], [128, 16, WH], B2.bufs, BF16)
        dump("ga", B4.t[:, :, :], [128, 16, WX], B4.bufs, BF16)

        tok0 = a - 1 + 1
        def epi_z(m, pt):
            tb = tmpA[st["ti"] % 3]
            st["ti"] += 1
            ob = obuf[st["oi"] % 2]
            st["oi"] += 1
            fw.dma("sp", ob.t[:, 0:WX], o_src(m, s, tok0), reads=[o_buf_dram], writes=[ob.buf])
            fw.op("act", lambda e: e.activation(out=tb.t[:, 0:WX], in_=pt.t[:, 0:WX], func=AF.Silu),
                  reads=[pt.buf], writes=[tb.buf])
            fw.op("dve", lambda e: e.tensor_tensor(out=BF.t[:, m, 0:WX], in0=tb.t[:, 0:WX], in1=ob.t[:, 0:WX], op=ALU.mult),
                  reads=[tb.buf, ob.buf], writes=[BF.bufs[m]])
        linear(wz_b, "wz", list(range(32)), 16, xrhs, hbuf, WX, epi_z)
        dump("og", BF.t[:, 0:32, :], [128, 32, WX], BF.bufs[0:32], BF16)

        def epi_ya(m, pt):
            tb = tmpA[st["ti"] % 3]
            st["ti"] += 1
            fw.op("dve", lambda e: e.tensor_tensor(out=tb.t[:, 0:WX], in0=pt.t[:, 0:WX], in1=B4.t[:, m, 0:WX], op=ALU.mult),
                  reads=[pt.buf, B4.bufs[m]], writes=[tb.buf])
            fw.op("pool", lambda e: e.tensor_tensor(out=B3.t[:, m, 0:WX], in0=tb.t[:, 0:WX], in1=B2.t[:, m, 0:WX], op=ALU.add),
                  reads=[tb.buf, B2.bufs[m]], writes=[B3.bufs[m]])
        linear(wdo_b, "wdo", list(range(16)), 32, lambda kc: BF.t[:, kc, 0:WX], lambda kc: BF.bufs[kc], WX, epi_ya)
        dump("merged", B3.t[:, :, :], [128, 16, WX], B3.bufs, BF16)

        def epi_m2(m, pt):
            fw.op("act", lambda e: e.activation(out=F2.t[:, m, 0:WX], in_=pt.t[:, 0:WX], func=AF.Copy),
                  reads=[pt.buf], writes=[F2.bufs[m]])
        linear(wmo_b, "wmo", list(range(16)), 16, lambda kc: B3.t[:, kc, 0:WX], lambda kc: B3.bufs[kc], WX, epi_m2)

        def rms_row(src, n, dst_row):
            for kc in range(16):
                tb = tmpA[st["ti"] % 3]
                st["ti"] += 1
                fw.op("act", lambda e, kc=kc, tb=tb: e.activation(out=tb.t[:, 0:n], in_=src(kc), func=AF.Square),
                      reads=[srcb(src, kc)], writes=[tb.buf])
                fw.op("pe", lambda e, kc=kc, tb=tb: e.matmul(stp.t[:, 0:n], lhsT=onesf.t[:, :], rhs=tb.t[:, 0:n],
                                                             start=(kc == 0), stop=(kc == 15)),
                      reads=[onesf.buf, tb.buf], writes=[stp.buf])
            fw.op("act", lambda e: e.activation(out=dst_row.t[:, 0:n], in_=stp.t[:, 0:n], func=AF.Sqrt, bias=RMS_EPS,
                                                scale=1.0 / D),
                  reads=[stp.buf], writes=[dst_row.buf])
            fw.op("dve", lambda e: e.reciprocal(out=dst_row.t[:, 0:n], in_=dst_row.t[:, 0:n]),
                  reads=[dst_row.buf], writes=[dst_row.buf])

        srcmap = {}

        def srcb(src, kc):
            return srcmap[src][kc]
        m2src = lambda kc: F2.t[:, kc, 0:WX]
        srcmap[m2src] = F2.bufs
        rms_row(m2src, WX, rowA)
        for kc in range(16):
            tb = tmpB[kc % 3]
            ce = "dve"
            fw.op(ce, lambda e, kc=kc, tb=tb: e.tensor_tensor(out=tb.t[:, 0:WX], in0=F2.t[:, kc, 0:WX], in1=rowA.t[:, 0:WX],
                                                              op=ALU.mult),
                  reads=[F2.bufs[kc], rowA.buf], writes=[tb.buf])
            fw.op(ce, lambda e, kc=kc, tb=tb: e.scalar_tensor_tensor(out=F1.t[:, kc, 0:WX], in0=tb.t[:, 0:WX],
                                                                     scalar=gpost.t[:, kc:kc + 1], in1=F1.t[:, kc, 0:WX],
                                                                     op0=ALU.mult, op1=ALU.add),
                  reads=[tb.buf, gpost.buf], writes=[F1.bufs[kc]])
        dump("x1", F1.t[:, :, :], [128, 16, WX], F1.bufs)

        x1src = lambda kc: F1.t[:, kc, 0:WX]
        srcmap[x1src] = F1.bufs
        rms_row(x1src, WX, rowB)
        fw.dma("sp", mskt.t[:, :], msk[s * NT + ti, :, :], writes=[mskt.buf])
        fw.op("dve", lambda e: e.tensor_tensor(out=rowB.t[:, 0:WX], in0=rowB.t[:, 0:WX], in1=mskt.t[:, 0:WX], op=ALU.mult),
              reads=[mskt.buf], writes=[rowB.buf])
        for kc in range(16):
            ce = "dve"
            fw.op(ce, lambda e, kc=kc: e.scalar_tensor_tensor(out=B1.t[:, kc, 0:WX], in0=F1.t[:, kc, 0:WX],
                                                              scalar=gffn.t[:, kc:kc + 1], in1=rowB.t[:, 0:WX],
                                                              op0=ALU.mult, op1=ALU.mult),
                  reads=[F1.bufs[kc], gffn.buf, rowB.buf], writes=[B1.bufs[kc]])
        dump("h2", B1.t[:, :, :], [128, 16, WH], B1.bufs, BF16)

        h2rhs = lambda kc: B1.t[:, kc, 0:WX]
        cres = {}

        def conv3(m, pt, dst):
            fw.op("act", lambda e: e.activation(out=dst.t[:, 0:TO], in_=pt.t[:, 0:TO], func=AF.Copy,
                                                scale=fcw.t[:, m, 0:1]),
                  reads=[pt.buf, fcw.buf], writes=[dst.buf])
            for j in (1, 2):
                fw.op("dve", lambda e, j=j: e.scalar_tensor_tensor(out=dst.t[:, 0:TO], in0=pt.t[:, j:j + TO],
                                                                   scalar=fcw.t[:, m, j:j + 1], in1=dst.t[:, 0:TO],
                                                                   op0=ALU.mult, op1=ALU.add),
                      reads=[pt.buf, fcw.buf], writes=[dst.buf])

        def epi_up(m, pt):
            if m < 44:
                tb = tmpA[st["ti"] % 3]
                st["ti"] += 1
                conv3(m, pt, tb)
                fw.op("act", lambda e: e.activation(out=tb.t[:, 0:TO], in_=tb.t[:, 0:TO], func=AF.Silu),
                      reads=[tb.buf], writes=[tb.buf])
                cres["g"] = tb
            else:
                tb = tmpB[st["fi"] % 3]
                st["fi"] += 1
                conv3(m, pt, tb)
                tg = cres["g"]
                fw.op("pool", lambda e: e.tensor_tensor(out=BF.t[:, m - 44, 0:TO], in0=tb.t[:, 0:TO], in1=tg.t[:, 0:TO],
                                                        op=ALU.mult),
                      reads=[tb.buf, tg.buf], writes=[BF.bufs[m - 44]])
        up_order = []
        for i in range(44):
            up_order += [i, 44 + i]
        linear(wup_b, "wup", up_order, 16, h2rhs, hbuf, WX, epi_up)
        dump("f", BF.t[:, :, :], [128, 44, WX], BF.bufs, BF16)

        def epi_d(m, pt):
            fw.op("act", lambda e: e.activation(out=F2.t[:, m, 0:TO], in_=pt.t[:, 0:TO], func=AF.Copy),
                  reads=[pt.buf], writes=[F2.bufs[m]])
        linear(wdn_b, "wdn", list(range(16)), 44, lambda kc: BF.t[:, kc, 0:TO], lambda kc: BF.bufs[kc], TO, epi_d, ksplit=2)
        dsrc = lambda kc: F2.t[:, kc, 0:TO]
        srcmap[dsrc] = F2.bufs
        rms_row(dsrc, TO, rowA)
        for kc in range(16):
            tb = tmpB[kc % 3]
            ce = "dve"
            fw.op(ce, lambda e, kc=kc, tb=tb: e.tensor_tensor(out=tb.t[:, 0:TO], in0=F2.t[:, kc, 0:TO], in1=rowA.t[:, 0:TO],
                                                              op=ALU.mult),
                  reads=[F2.bufs[kc], rowA.buf], writes=[tb.buf])
            fw.op(ce, lambda e, kc=kc, tb=tb: e.scalar_tensor_tensor(out=F2.t[:, kc, 0:TO], in0=tb.t[:, 0:TO],
                                                                     scalar=gfpost.t[:, kc:kc + 1], in1=F1.t[:, kc, 1:1 + TO],
                                                                     op0=ALU.mult, op1=ALU.add),
                  reads=[tb.buf, gfpost.buf, F1.bufs[kc]], writes=[F2.bufs[kc]])
        dump("yT", F2.t[:, :, :], [128, 16, WX], F2.bufs)
        t0 = 0
        while t0 < TO:
            n = min(128, TO - t0)
            xb = xt[st["xi"] % 2]
            st["xi"] += 1
            for k4 in range(4):
                tp = tpf[k4 % 2]
                for j in range(4):
                    kc = k4 * 4 + j
                    fw.op("pe", lambda e, kc=kc, j=j, tp=tp: e.transpose(tp.t[0:n, j, :], F2.t[:, kc, t0:t0 + n],
                                                                         identf.t[:, :]),
                          reads=[F2.bufs[kc], identf.buf], writes=[tp.buf])
                ce = "dve" if k4 % 2 == 0 else "act"
                if ce == "dve":
                    fw.op("dve", lambda e, k4=k4, tp=tp: e.tensor_copy(out=xb.t[0:n, k4 * 512:(k4 + 1) * 512].rearrange("p (j c) -> p j c", c=128),
                                                                       in_=tp.t[0:n, :, :]),
                          reads=[tp.buf], writes=[xb.buf], excl=(k4 == 0))
                else:
                    fw.op("act", lambda e, k4=k4, tp=tp: e.activation(out=xb.t[0:n, k4 * 512:(k4 + 1) * 512].rearrange("p (j c) -> p j c", c=128),
                                                                      in_=tp.t[0:n, :, :], func=AF.Copy),
                          reads=[tp.buf], writes=[xb.buf], excl=(k4 == 0))
            fw.dma("act", y[s, a + t0:a + t0 + n, :], xb.t[0:n, :], reads=[xb.buf], writes=[outb], excl=False)
            t0 += n

    for s in range(nseg):
        for ti in tiles:
            tile(s, ti)

    fw.wait_all("sp", [outb] + dbg_bufs)
    es.close()
    return nc, fw, dbg_out


def _wtile(W):
    K, M = W.shape
    KC, MC = K // 128, M // 128
    return np.ascontiguousarray(W.reshape(KC, 128, MC, 128).transpose(2, 1, 0, 3).reshape(MC, 128, KC * 128))


def _fm(v):
    return np.ascontiguousarray(v.reshape(-1, 128).T)


def prep_common(inp):
    f = lambda k: np.asarray(inp[k], dtype=np.float32)[0]
    w_in = f("w_in")
    com = {}
    com["wz"] = _wtile(w_in[:, 8192:12288])
    com["wglu"] = _wtile(w_in[:, 12416:16512])
    com["wgt"] = _wtile(w_in[:, 16512:20608])
    com["wdo"] = _wtile(f("w_delta_out"))
    com["wco"] = _wtile(f("w_conf_out"))
    com["wmo"] = _wtile(f("w_mix_out"))
    com["wup"] = _wtile(f("w_up"))
    wd = _wtile(f("w_down"))
    com["wdn"] = np.ascontiguousarray(wd.reshape(16, 128, 2, DFF // 2).transpose(0, 2, 1, 3).reshape(32, 128, DFF // 2))
    vecs = np.zeros((128, 16, 8), np.float32)
    for i, k in enumerate(("mix_norm_pre", "mix_norm_post", "ffn_norm_pre", "ffn_norm_post", "conf_conv_b",
                           "conf_ln_w", "conf_ln_b")):
        vecs[:, :, i] = _fm(f(k))
    com["vecs"] = vecs
    cw = f("conf_conv_w")
    com["cconv"] = np.ascontiguousarray(cw.T.reshape(16, 128, 31).transpose(1, 0, 2))
    fc = f("ffn_conv_w")
    com["fconv"] = np.ascontiguousarray(fc.T.reshape(88, 128, 3).transpose(1, 0, 2))
    com["dnw"] = np.ascontiguousarray(f("delta_norm_w").reshape(128, 1))
    com["ident"] = np.eye(128, dtype=np.float32)
    return com


def prep_core(inp, c, com):
    xs = [np.asarray(inp["x_prompt"], np.float32)[0], np.asarray(inp["x_sample"], np.float32)[0],
          np.asarray(inp["x_sample"], np.float32)[1]]
    xp = np.zeros((NSEQ, SEG + 2 * HALO, D), np.float32)
    for s in range(NSEQ):
        lo, hi = c * SEG - HALO, (c + 1) * SEG + HALO
        l2, h2 = max(lo, 0), min(hi, SEQ)
        xp[s, l2 - lo:h2 - lo] = xs[s][l2:h2]
    msk = np.ones((NSEQ * NT, 128, WX), np.float32)
    for s in range(NSEQ):
        for ti in range(NT):
            a = min(TO * ti, SEG - TO)
            g = c * SEG + a - 1 + np.arange(WX)
            msk[s * NT + ti, :, :] = ((g >= 0) & (g < SEQ)).astype(np.float32)[None, :]
    d = dict(com)
    d["xp"] = xp
    d["msk"] = msk
    return d


WIN = 256
NWIN = SEQ // WIN
NBLK = SEQ // 128
L2_EPS = 1e-6


def build_p1(cfg):
    nseq = cfg.get("nseq", NSEQ)
    nwin = cfg.get("nwin", NWIN)
    dbg = cfg.get("dbg", [])
    do_scan = cfg.get("scan", True)
    nc = bass.Bass("TRN2", target_bir_lowering=False)
    from contextlib import ExitStack
    es = ExitStack()
    nblk = nwin * 2

    def din(name, shape, dt=F32):
        return nc.dram_tensor(name, list(shape), dt, kind="ExternalInput")

    def dint(name, shape, dt):
        return nc.dram_tensor(name, list(shape), dt, kind="Internal")

    def dout(name, shape, dt=F32):
        return nc.dram_tensor(name, list(shape), dt, kind="ExternalOutput")

    xq = din("xq", [NSEQ, SEQ + 4, D])
    wqkv_f = din("wqkv_in", [8, 128, D])
    wab_f = din("wab_in", [128, 16, 16])
    gpre_d = din("gpre_in", [128, 16])
    scw_d = din("scw_in", [128, 8, 5])
    abv_d = din("abv_in", [16, 2])
    dnw_d = din("dnwb", [128, 128])
    ident_d = din("ident", [128, 128])
    masks_d = din("masks_in", [128, 8, 128])
    ind_d = din("ind_in", [128, 2, 128])
    esel_d = din("esel_in", [4, 8, 128])
    oT = dout("oT", [512, NSEQ, SEQ + 2], BF16)

    base = dint("base", [nseq, nblk, 128, 10, 128], F32)
    gbd = dint("gbd", [nseq, nblk, 128, 16], F32)
    ofw = dint("ofw", [nseq, nblk, 128, 4, 128], F32)

    dbg_out = {}
    dbg_bufs = []
    outb = Buf("out")
    sems = [es.enter_context(nc.semaphore(f"s{i}")) for i in range(100)]
    fw = FW(nc, sems)

    def sb(name, shape, dt, nchunk=1):
        return T(es.enter_context(nc.sbuf_tensor(name, list(shape), dt)), nchunk, name)

    def ps(name, shape, dt):
        t_ = T(es.enter_context(nc.psum_tensor(name, list(shape), dt)), 1, name)
        t_.bufs[0].psum = True
        return t_

    def dump(name, ap, shape, bufs, dt=F32):
        if name not in dbg:
            return
        if name not in dbg_out:
            dbg_out[name] = dout("dbg_" + name, shape, dt)
        b = Buf("dbg")
        dbg_bufs.append(b)
        fw.dma("sp", dbg_out[name][tuple(slice(None) for _ in shape)], ap, reads=bufs, writes=[b])

    identf = sb("identf", [128, 128], F32)
    identb = sb("identb", [128, 128], BF16)
    onesf = sb("onesf", [128, 128], F32)
    gpre = sb("gpre", [128, 16], F32)
    scw = sb("scw", [128, 8, 5], F32)
    abv = sb("abv", [16, 4], F32)
    dnw = sb("dnw", [128, 128], F32)
    masks = sb("masks", [128, 8, 128], F32)
    ind = sb("ind", [128, 2, 128], F32)
    esel = sb("esel", [4, 8, 128], F32)
    wqkv = sb("wqkv", [128, 8, D], BF16)
    wab = sb("wab", [128, 16, 16], BF16)
    for t_, d_ in ((identf, ident_d), (gpre, gpre_d), (dnw, dnw_d)):
        fw.dma("sp", t_.t[:, :], d_[:, :], writes=[t_.buf])
    fw.dma("sp", abv.t[:, 0:2], abv_d[:, :], writes=[abv.buf])
    for t_, d_ in ((scw, scw_d), (masks, masks_d), (ind, ind_d), (esel, esel_d)):
        fw.dma("sp", t_.t[:, :, :], d_[:, :, :], writes=[t_.buf])
    for mc in range(8):
        fw.dma("pool", wqkv.t[:, mc, :], wqkv_f[mc, :, :], writes=[wqkv.buf], excl=False)
    fw.dma("pool", wab.t[:, :, :], wab_f[:, :, :], writes=[wab.buf])
    fw.op("dve", lambda e: e.tensor_copy(out=identb.t[:, :], in_=identf.t[:, :]), reads=[identf.buf], writes=[identb.buf])
    fw.op("dve", lambda e: e.memset(onesf.t[:, :], 1.0), writes=[onesf.buf])
    fw.op("act", lambda e: e.activation(out=abv.t[:, 2:3], in_=abv.t[:, 1:2], func=AF.Exp), reads=[abv.buf], writes=[abv.buf])
    fw.op("dve", lambda e: e.tensor_scalar(out=abv.t[:, 2:3], in0=abv.t[:, 2:3], scalar1=-1.0, scalar2=None, op0=ALU.mult),
          reads=[abv.buf], writes=[abv.buf])

    WW = WIN + 4
    xt = [sb(f"xt{i}", [128, D], F32) for i in range(2)]
    xn = sb("xn", [128, D], BF16)
    ss = sb("ss", [128, 4], F32)
    hT = sb("hT", [128, 16, WW], BF16, 16)
    pre = sb("pre", [128, 8, WW], F32, 8)
    qkvc = [sb(f"qkvc{i}", [128, 8, WIN], F32, 8) for i in range(2)]
    tmpw = [sb(f"tw{i}", [128, WIN], F32) for i in range(2)]
    rows = sb("rows", [16, 2, WIN], F32, 2)
    stg = [sb(f"stg{i}", [128, 6, 128], F32) for i in range(2)]
    gbt = [sb(f"gbt{i}", [128, 16], F32) for i in range(2)]
    mm = [ps(f"mm{i}", [128, 512], F32) for i in range(3)]
    tpb = ps("tpb", [128, 8, 128], BF16)
    tpf = [ps(f"tpf{i}", [128, 4, 128], F32) for i in range(2)]
    stp = ps("stp", [128, 512], F32)
    rwp = ps("rwp", [128, 512], F32)
    st = {"xi": 0, "pi": 0, "qi": 0, "ti": 0, "si": 0}
    base_bufs = [[Buf(f"base{s}_{b}") for b in range(nblk)] for s in range(nseq)]
    gb_bufs = [[Buf(f"gb{s}_{b}") for b in range(nblk)] for s in range(nseq)]

    def window(s, w):
        t0 = w * WIN
        r0 = 0
        while r0 < WW:
            n = min(128, WW - r0)
            xb = xt[st["xi"] % 2]
            st["xi"] += 1
            fw.dma("sp", xb.t[0:n, :], xq[s, t0 + r0:t0 + r0 + n, :], writes=[xb.buf])
            fw.op("dve", lambda e: e.memset(ss.t[:, 0:1], 0.0), writes=[ss.buf])
            fw.op("act", lambda e: e.activation(out=xn.t[0:n, :], in_=xb.t[0:n, :], func=AF.Square, accum_out=ss.t[0:n, 0:1]),
                  reads=[xb.buf], writes=[xn.buf, ss.buf])
            fw.op("act", lambda e: e.activation(out=ss.t[0:n, 1:2], in_=ss.t[0:n, 0:1], func=AF.Sqrt, bias=RMS_EPS, scale=1.0 / D),
                  reads=[ss.buf], writes=[ss.buf])
            fw.op("dve", lambda e: e.reciprocal(out=ss.t[0:n, 1:2], in_=ss.t[0:n, 1:2]), reads=[ss.buf], writes=[ss.buf])
            fw.op("dve", lambda e: e.tensor_scalar(out=xn.t[0:n, :], in0=xb.t[0:n, :], scalar1=ss.t[0:n, 1:2], scalar2=None,
                                                   op0=ALU.mult), reads=[xb.buf, ss.buf], writes=[xn.buf])
            for k4 in range(4):
                for j in range(4):
                    kc = k4 * 4 + j
                    fw.op("pe", lambda e, kc=kc, j=j: e.transpose(tpb.t[:, j, 0:n], xn.t[0:n, kc * 128:(kc + 1) * 128],
                                                                  identb.t[0:n, 0:n]),
                          reads=[xn.buf, identb.buf], writes=[tpb.buf])
                for j in range(4):
                    kc = k4 * 4 + j
                    fw.op("act", lambda e, kc=kc, j=j: e.activation(out=hT.t[:, kc, r0:r0 + n], in_=tpb.t[:, j, 0:n],
                                                                    func=AF.Copy, scale=gpre.t[:, kc:kc + 1]),
                          reads=[tpb.buf, gpre.buf], writes=[hT.bufs[kc]], excl=(r0 == 0))
            r0 += n
        for mc in range(8):
            pt = mm[st["pi"] % 3]
            st["pi"] += 1
            for kc in range(16):
                fw.op("pe", lambda e, kc=kc, mc=mc: e.matmul(pt.t[:, 0:WW], lhsT=wqkv.t[:, mc, kc * 128:(kc + 1) * 128],
                                                             rhs=hT.t[:, kc, 0:WW], start=(kc == 0), stop=(kc == 15)),
                      reads=[wqkv.buf, hT.bufs[kc]], writes=[pt.buf])
            fw.op("act", lambda e, mc=mc: e.activation(out=pre.t[:, mc, 0:WW], in_=pt.t[:, 0:WW], func=AF.Copy),
                  reads=[pt.buf], writes=[pre.bufs[mc]])
        for kc in range(16):
            fw.op("pe", lambda e, kc=kc: e.matmul(rwp.t[0:16, 0:WIN], lhsT=wab.t[:, kc, :], rhs=hT.t[:, kc, 2:2 + WIN],
                                                  start=(kc == 0), stop=(kc == 15)),
                  reads=[wab.buf, hT.bufs[kc]], writes=[rwp.buf])
        fw.op("act", lambda e: e.activation(out=rows.t[:, 0, :], in_=rwp.t[0:16, 0:WIN], func=AF.Exp, bias=abv.t[:, 0:1], scale=1.0),
              reads=[rwp.buf, abv.buf], writes=[rows.bufs[0]])
        fw.op("act", lambda e: e.activation(out=rows.t[:, 0, :], in_=rows.t[:, 0, :], func=AF.Ln, bias=1.0, scale=1.0),
              reads=[rows.bufs[0]], writes=[rows.bufs[0]])
        fw.op("dve", lambda e: e.tensor_scalar(out=rows.t[:, 0, :], in0=rows.t[:, 0, :], scalar1=abv.t[:, 2:3], scalar2=None,
                                               op0=ALU.mult), reads=[rows.bufs[0], abv.buf], writes=[rows.bufs[0]])
        fw.op("act", lambda e: e.activation(out=rows.t[:, 1, :], in_=rwp.t[0:16, 0:WIN], func=AF.Sigmoid),
              reads=[rwp.buf], writes=[rows.bufs[1]])
        qc = qkvc[st["qi"] % 2]
        st["qi"] += 1
        for mc in range(8):
            tw = tmpw[st["ti"] % 2]
            st["ti"] += 1
            fw.op("dve", lambda e, mc=mc, tw=tw: e.tensor_scalar(out=tw.t[:, 0:WIN], in0=pre.t[:, mc, 0:WIN], scalar1=scw.t[:, mc, 0:1],
                                                                scalar2=None, op0=ALU.mult),
                  reads=[pre.bufs[mc], scw.buf], writes=[tw.buf])
            for j in range(1, 5):
                fw.op("dve", lambda e, mc=mc, j=j, tw=tw: e.scalar_tensor_tensor(out=tw.t[:, 0:WIN], in0=pre.t[:, mc, j:j + WIN],
                                                                                 scalar=scw.t[:, mc, j:j + 1], in1=tw.t[:, 0:WIN],
                                                                                 op0=ALU.mult, op1=ALU.add),
                      reads=[pre.bufs[mc], scw.buf], writes=[tw.buf])
            fw.op("act", lambda e, mc=mc, tw=tw: e.activation(out=qc.t[:, mc, :], in_=tw.t[:, 0:WIN], func=AF.Silu),
                  reads=[tw.buf], writes=[qc.bufs[mc]])
        for mc in range(4):
            tw = tmpw[st["ti"] % 2]
            st["ti"] += 1
            fw.op("act", lambda e, mc=mc, tw=tw: e.activation(out=tw.t[:, 0:WIN], in_=qc.t[:, mc, :], func=AF.Square),
                  reads=[qc.bufs[mc]], writes=[tw.buf])
            fw.op("pe", lambda e, tw=tw: e.matmul(stp.t[:, 0:WIN], lhsT=onesf.t[:, :], rhs=tw.t[:, 0:WIN], start=True, stop=True),
                  reads=[onesf.buf, tw.buf], writes=[stp.buf])
            fw.op("act", lambda e, tw=tw: e.activation(out=tw.t[:, 0:WIN], in_=stp.t[:, 0:WIN], func=AF.Sqrt, bias=L2_EPS, scale=1.0),
                  reads=[stp.buf], writes=[tw.buf])
            fw.op("dve", lambda e, tw=tw: e.reciprocal(out=tw.t[:, 0:WIN], in_=tw.t[:, 0:WIN]), reads=[tw.buf], writes=[tw.buf])
            sc = (128.0 ** -0.5) if mc < 2 else 1.0
            fw.op("dve", lambda e, mc=mc, tw=tw, sc=sc: e.scalar_tensor_tensor(out=qc.t[:, mc, :], in0=qc.t[:, mc, :], scalar=sc,
                                                                               in1=tw.t[:, 0:WIN], op0=ALU.mult, op1=ALU.mult),
                  reads=[tw.buf], writes=[qc.bufs[mc]])
        dump("qkvc", qc.t[:, :, :], [128, 8, WIN], qc.bufs)
        dump("rows", rows.t[:, :, :], [16, 2, WIN], rows.bufs)
        for hb in range(2):
            blk = w * 2 + hb
            sg = stg[st["si"] % 2]
            gt_ = gbt[st["si"] % 2]
            st["si"] += 1
            c0 = hb * 128
            tp = tpf[0]
            for i, mc in enumerate((2, 3, 4, 5)):
                fw.op("pe", lambda e, i=i, mc=mc: e.transpose(tp.t[:, i, :], qc.t[:, mc, c0:c0 + 128], identf.t[:, :]),
                      reads=[qc.bufs[mc], identf.buf], writes=[tp.buf])
            fw.op("act", lambda e: e.activation(out=sg.t[:, 0:4, :], in_=tp.t[:, :, :], func=AF.Copy), reads=[tp.buf], writes=[sg.buf])
            tp2 = tpf[1]
            for i, mc in enumerate((6, 7)):
                fw.op("pe", lambda e, i=i, mc=mc: e.transpose(tp2.t[:, i, :], qc.t[:, mc, c0:c0 + 128], identf.t[:, :]),
                      reads=[qc.bufs[mc], identf.buf], writes=[tp2.buf])
            for i in range(2):
                fw.op("pe", lambda e, i=i: e.transpose(tp2.t[:, 2 + i, 0:16], rows.t[:, i, c0:c0 + 128], identf.t[0:16, 0:16]),
                      reads=[rows.bufs[i], identf.buf], writes=[tp2.buf])
            fw.op("dve", lambda e: e.tensor_copy(out=sg.t[:, 4:6, :], in_=tp2.t[:, 0:2, :]), reads=[tp2.buf], writes=[sg.buf], excl=False)
            fw.op("dve", lambda e: e.tensor_copy(out=gt_.t[:, 0:8], in_=tp2.t[:, 2, 0:8]), reads=[tp2.buf], writes=[gt_.buf])
            fw.op("dve", lambda e: e.tensor_copy(out=gt_.t[:, 8:16], in_=tp2.t[:, 3, 8:16]), reads=[tp2.buf], writes=[gt_.buf], excl=False)
            fw.dma("act", base[s, blk, :, 0:4, :], qc.t[:, 0:4, c0:c0 + 128], reads=qc.bufs[0:4], writes=[base_bufs[s][blk]],
                   owner=base_bufs[s][0], excl=False)
            fw.dma("act", base[s, blk, :, 4:10, :], sg.t[:, :, :], reads=[sg.buf], writes=[base_bufs[s][blk]],
                   owner=base_bufs[s][0], excl=False)
            fw.dma("act", gbd[s, blk, :, :], gt_.t[:, :], reads=[gt_.buf], writes=[gb_bufs[s][blk]], owner=base_bufs[s][0])
            if "stg" in dbg:
                dump("stg", sg.t[:, :, :], [128, 6, 128], [sg.buf])
                dump("gbt", gt_.t[:, :], [128, 16], [gt_.buf])

    for s in range(nseq):
        for w in range(nwin):
            window(s, w)

    if do_scan:
        fw.barrier()
        _scan_pass(nc, fw, es, locals())

    fw.wait_all("sp", [outb] + dbg_bufs)
    fw.wait_all("act", [outb] + dbg_bufs)
    es.close()
    return nc, fw, dbg_out


def prep_p1_common(inp):
    f = lambda k: np.asarray(inp[k], dtype=np.float32)[0]
    com = {}
    com["gpre_in"] = _fm(f("mix_norm_pre"))
    com["dnwb"] = np.ascontiguousarray(np.broadcast_to(f("delta_norm_w")[None, :], (128, 128))).astype(np.float32)
    com["ident"] = np.eye(128, dtype=np.float32)
    t = np.arange(128)
    same = (t[:, None] // 64) == (t[None, :] // 64)
    s_, c_ = t[:, None], t[None, :]
    m = np.zeros((128, 8, 128), np.float32)
    m[:, 0] = (same & (c_ >= s_))
    m[:, 1] = (same & (c_ <= s_))
    m[:, 2] = (same & (c_ > s_))
    m[:, 3] = (same & (c_ < s_))
    m[:, 4] = np.where(m[:, 0] > 0, 0.0, -30000.0)
    m[:, 5] = np.where(m[:, 1] > 0, 0.0, -30000.0)
    m[:, 6] = same
    com["masks_in"] = m
    ind = np.zeros((128, 2, 128), np.float32)
    ind[0:64, 0, :] = 1.0
    ind[64:128, 1, :] = 1.0
    com["ind_in"] = ind
    es_ = np.zeros((4, 8, 128), np.float32)
    for j in range(4):
        es_[j, j, :] = 1.0
        es_[j, 4 + j, :] = -1.0
    com["esel_in"] = es_
    return com


def prep_p1_core(inp, c, com, xq):
    f = lambda k: np.asarray(inp[k], dtype=np.float32)[0]
    w_in = f("w_in")
    cols = np.concatenate([np.arange(256 * c, 256 * c + 256), 2048 + np.arange(256 * c, 256 * c + 256),
                           4096 + np.arange(512 * c, 512 * c + 512)])
    d = dict(com)
    d["wqkv_in"] = _wtile(w_in[:, cols])
    acols = np.concatenate([12288 + dd * 32 + 4 * c + np.arange(4) for dd in range(2)])
    bcols = acols + 64
    wab = w_in[:, np.concatenate([acols, bcols])]
    d["wab_in"] = np.ascontiguousarray(wab.reshape(16, 128, 16).transpose(1, 0, 2))
    scw = f("short_conv_w")[:, cols]
    d["scw_in"] = np.ascontiguousarray(scw.T.reshape(8, 128, 5).transpose(1, 0, 2))
    abv = np.zeros((16, 2), np.float32)
    abv[0:8, 0] = f("dt_bias")[:, 4 * c:4 * c + 4].reshape(-1)
    abv[0:8, 1] = f("a_log")[:, 4 * c:4 * c + 4].reshape(-1)
    d["abv_in"] = abv
    d["xq"] = xq
    return d


def prep_xq(inp):
    xs = [np.asarray(inp["x_prompt"], np.float32)[0], np.asarray(inp["x_sample"], np.float32)[0],
          np.asarray(inp["x_sample"], np.float32)[1]]
    xq = np.zeros((NSEQ, SEQ + 4, D), np.float32)
    for s in range(NSEQ):
        xq[s, 2:2 + SEQ] = xs[s]
    return xq


def _scan_pass(nc, fw, es, L):
    sb, ps, dump = L["sb"], L["ps"], L["dump"]
    masks, ind, esel, identf, identb, dnw = L["masks"], L["ind"], L["esel"], L["identf"], L["identb"], L["dnw"]
    base, gbd, ofw, oT = L["base"], L["gbd"], L["ofw"], L["oT"]
    base_bufs, gb_bufs, nseq, nblk, outb, dbg = L["base_bufs"], L["gb_bufs"], L["nseq"], L["nblk"], L["outb"], L["dbg"]
    mmp, stp, rwp, tpf, tpb = L["mm"], L["stp"], L["rwp"], L["tpf"], L["tpb"]
    lvl = float(L["cfg"].get("lvl", 4))

    class PS_:
        def __init__(self, ap, bank):
            self.ap = ap
            self.buf = bank.buf
    pG = PS_(mmp[0].t[:, 0:64], mmp[0])
    pGr = PS_(mmp[0].t[0:4, 128:256], mmp[0])
    pGram = [PS_(mmp[1].t[:, i * 128:(i + 1) * 128], mmp[1]) for i in range(4)]
    pDiff = [PS_(mmp[2].t[:, i * 128:(i + 1) * 128], mmp[2]) for i in range(2)]
    pT = [PS_(mmp[2].t[:, 256 + i * 128:256 + (i + 1) * 128], mmp[2]) for i in range(2)]
    pPow = [PS_(stp.t[:, i * 256:(i + 1) * 256], stp) for i in range(2)]
    pApp = [PS_(rwp.t[:, i * 256:(i + 1) * 256], rwp) for i in range(2)]
    pS = [[PS_(tpf[i].t[:, k, :], tpf[i]) for k in range(4)] for i in range(2)]
    pO = PS_(tpb.t[:, 0:4, :], tpb)

    T32 = [sb(f"T32_{i}", [128, 10, 128], F32) for i in range(2)]
    QB = [sb(f"QB{i}", [128, 2, 128], BF16) for i in range(2)]
    GB = [sb(f"GB{i}", [128, 16], F32) for i in range(2)]
    sm = [sb(f"sm{i}", [128, 64], F32) for i in range(2)]
    sm2 = [sb(f"sm2_{i}", [128, 16], F32) for i in range(2)]
    Grs = [sb(f"Grs{i}", [4, 128], F32) for i in range(2)]
    KKs = [sb(f"KKs{i}", [128, 2, 128], F32, 2) for i in range(2)]
    KQs = [sb(f"KQs{i}", [128, 2, 128], F32, 2) for i in range(2)]

    class Unit:
        pass
    units = []
    for k in range(8):
        U = Unit()
        U.Dm = sb(f"uDm{k}", [128, 128], F32)
        U.YZ = [sb(f"uYZ{k}_{i}", [128, 2, 128], F32) for i in range(2)]
        U.R = sb(f"uR{k}", [128, 256], F32)
        U.At = sb(f"uAt{k}", [128, 128], BF16)
        U.u = sb(f"uu{k}", [128, 128], F32)
        U.w = sb(f"uw{k}", [128, 128], F32)
        U.wT = sb(f"uwT{k}", [128, 128], BF16)
        U.kd = sb(f"ukd{k}", [128, 128], BF16)
        U.vn = sb(f"uvn{k}", [128, 128], BF16)
        U.p3s = sb(f"up3{k}", [128, 128], F32)
        units.append(U)
    S = [sb(f"S{j}", [128, 128], F32) for j in range(4)]
    Sb = [sb(f"Sb{j}", [128, 128], BF16) for j in range(4)]
    oacc = [sb(f"oacc{i}", [128, 4, 128], F32) for i in range(2)]
    of32 = [sb(f"of32_{i}", [128, 4, 128], F32) for i in range(2)]
    onb = sb("onb", [128, 4, 128], BF16)
    ostg = [sb(f"ostg{i}", [128, 4, 130], BF16) for i in range(2)]
    ssq = sb("ssq", [128, 8], F32)
    ofw_bufs = [[Buf(f"ofw{s}_{b}") for b in range(nblk)] for s in range(nseq)]
    ofw_own = Buf("ofw_own")
    ld_own = [Buf("ld0"), Buf("ld1")]

    def mmf(pt, lhsT, rhs, reads, start=True, stop=True):
        fw.op("pe", lambda e: e.matmul(pt.ap, lhsT=lhsT, rhs=rhs, start=start, stop=stop), reads=reads, writes=[pt.buf])

    zf = sb("zf", [128, 4, 2], F32)
    fw.op("dve", lambda e: e.memset(zf.t[:, :, :], 0.0), writes=[zf.buf])
    for og0 in ostg:
        for cc in (0, 129):
            fw.op("dve", lambda e, og0=og0, cc=cc: e.tensor_copy(out=og0.t[:, :, cc:cc + 1], in_=zf.t[:, :, 0:1]), reads=[zf.buf],
                  writes=[og0.buf], excl=False)
    cnt = {"u": 0, "e": 0}
    for s in range(nseq):
        for dirn in (0, 1):
            for j in range(4):
                fw.op("dve", lambda e, j=j: e.memset(S[j].t[:, :], 0.0), writes=[S[j].buf])
                fw.op("dve", lambda e, j=j: e.tensor_copy(out=Sb[j].t[:, :], in_=S[j].t[:, :]), reads=[S[j].buf], writes=[Sb[j].buf])
            order = list(range(nblk)) if dirn == 0 else list(reversed(range(nblk)))
            for bi, blk in enumerate(order):
                par = bi % 2
                t32, qb, gb = T32[par], QB[par], GB[par]
                fw.dma("sp", t32.t[:, :, :], base[s, blk, :, :, :], reads=[base_bufs[s][blk]], writes=[t32.buf], owner=ld_own[par])
                fw.dma("pool", qb.t[:, :, :], base[s, blk, :, 0:2, :], reads=[base_bufs[s][blk]], writes=[qb.buf])
                fw.dma("sp", gb.t[:, :], gbd[s, blk, :, :], reads=[gb_bufs[s][blk]], writes=[gb.buf])
                g4 = gb.t[:, dirn * 4:dirn * 4 + 4]
                b4 = gb.t[:, 8 + dirn * 4:8 + dirn * 4 + 4]
                Mi, Ms, Mn, same = masks.t[:, dirn, :], masks.t[:, 2 + dirn, :], masks.t[:, 4 + dirn, :], masks.t[:, 6, :]
                a_, a2, gr = sm[par], sm2[par], Grs[par]
                gall = gb.t[:, 0:16]
                d4 = dirn * 4
                if lvl <= 0:
                    continue
                fw.op("pe", lambda e: e.matmul(mmp[0].t[:, 0:16], lhsT=Mi, rhs=gall, start=True, stop=True),
                      reads=[masks.buf, gb.buf], writes=[pG.buf])
                fw.op("pe", lambda e: e.matmul(mmp[0].t[:, 16:32], lhsT=same, rhs=gall, start=True, stop=True),
                      reads=[masks.buf, gb.buf], writes=[pG.buf])
                for h in range(2):
                    fw.op("pe", lambda e, h=h: e.matmul(mmp[0].t[:, 32 + 16 * h:48 + 16 * h], lhsT=ind.t[:, h, :], rhs=gall, start=True, stop=True),
                          reads=[ind.buf, gb.buf], writes=[pG.buf])
                mmf(pGr, g4, Mi, [masks.buf, gb.buf])
                fw.op("act", lambda e: e.activation(out=a_.t[:, :], in_=pG.ap, func=AF.Copy), reads=[pG.buf], writes=[a_.buf])
                fw.op("act", lambda e: e.activation(out=a2.t[:, 0:4], in_=a_.t[:, d4:d4 + 4], func=AF.Exp), reads=[a_.buf], writes=[a2.buf])
                fw.op("dve", lambda e: e.tensor_tensor(out=a2.t[:, 4:8], in0=a_.t[:, 16 + d4:20 + d4], in1=a_.t[:, d4:d4 + 4], op=ALU.subtract),
                      reads=[a_.buf], writes=[a2.buf], excl=False)
                fw.op("act", lambda e: e.activation(out=a2.t[:, 4:8], in_=a2.t[:, 4:8], func=AF.Exp), reads=[a2.buf], writes=[a2.buf])
                for h in range(2):
                    fw.op("act", lambda e, h=h: e.activation(out=a2.t[:, 8 + 4 * h:12 + 4 * h], in_=a_.t[:, 32 + 16 * h + d4:36 + 16 * h + d4], func=AF.Exp),
                          reads=[a_.buf], writes=[a2.buf], excl=False)
                fw.op("dve", lambda e: e.tensor_copy(out=gr.t[:, :], in_=pGr.ap), reads=[pGr.buf], writes=[gr.buf])
                if lvl <= 0.5:
                    continue
                Gam = lambda j: a2.t[:, j:j + 1]
                E2 = lambda j: a2.t[:, 4 + j:5 + j]
                gbc = lambda h, j: a2.t[:, 8 + 4 * h + j:9 + 4 * h + j]
                kks, kqs = KKs[par], KQs[par]
                for q in range(2):
                    mmf(pGram[2 * q], t32.t[:, 2 + q, :], t32.t[:, 2 + q, :], [t32.buf])
                    mmf(pGram[2 * q + 1], t32.t[:, 2 + q, :], t32.t[:, q, :], [t32.buf])
                    fw.op("dve", lambda e, q=q: e.tensor_tensor(out=kks.t[:, q, :], in0=pGram[2 * q].ap, in1=Ms, op=ALU.mult),
                          reads=[pGram[2 * q].buf, masks.buf], writes=[kks.bufs[q]])
                    fw.op("act", lambda e, q=q: e.activation(out=kqs.t[:, q, :], in_=pGram[2 * q + 1].ap, func=AF.Copy),
                          reads=[pGram[2 * q + 1].buf], writes=[kqs.bufs[q]])
                us = [units[par * 4 + j] for j in range(4)]
                for j in range(4 if lvl >= 2 else 0):
                    q = j // 2
                    U = us[j]
                    k = cnt["u"] % 2
                    cnt["u"] += 1
                    pd, pt_, pp, pa = pDiff[k], pT[k], pPow[k], pApp[k]
                    mmf(pd, esel.t[:, j, :], gr.t[:, :], [esel.buf, gr.buf], True, False)
                    mmf(pd, gr.t[:, :], esel.t[:, 4 + j, :], [esel.buf, gr.buf], False, True)
                    fw.op("dve", lambda e: e.scalar_tensor_tensor(out=U.Dm.t[:, :], in0=pd.ap, scalar=0.0, in1=Mn, op0=ALU.min, op1=ALU.add),
                          reads=[pd.buf, masks.buf], writes=[U.Dm.buf])
                    fw.op("act", lambda e: e.activation(out=U.Dm.t[:, :], in_=U.Dm.t[:, :], func=AF.Exp), reads=[U.Dm.buf], writes=[U.Dm.buf])
                    Y0 = U.YZ[0]
                    fw.op("dve", lambda e: e.scalar_tensor_tensor(out=Y0.t[:, 0, :], in0=kks.t[:, q, :], scalar=b4[:, j:j + 1], in1=U.Dm.t[:, :],
                                                                  op0=ALU.mult, op1=ALU.mult),
                          reads=[kks.bufs[q], gb.buf, U.Dm.buf], writes=[Y0.buf])
                    fw.op("pool", lambda e: e.tensor_tensor(out=U.At.t[:, :], in0=kqs.t[:, q, :], in1=U.Dm.t[:, :], op=ALU.mult),
                          reads=[kqs.bufs[q], U.Dm.buf], writes=[U.At.buf])
                    fw.op("pe", lambda e: e.transpose(pt_.ap, Y0.t[:, 0, :], identf.t[:, :]), reads=[Y0.buf, identf.buf], writes=[pt_.buf])
                    fw.op("act", lambda e: e.activation(out=Y0.t[:, 1, :], in_=pt_.ap, func=AF.Copy), reads=[pt_.buf], writes=[Y0.buf], excl=False)
                    fw.op("act", lambda e: e.activation(out=U.R.t[:, 0:128], in_=t32.t[:, 6 + j, :], func=AF.Copy), reads=[t32.buf], writes=[U.R.buf])
                    fw.op("dve", lambda e: e.tensor_scalar(out=U.R.t[:, 128:256], in0=t32.t[:, 4 + q, :], scalar1=Gam(j), scalar2=None, op0=ALU.mult),
                          reads=[t32.buf, a2.buf], writes=[U.R.buf], excl=False)
                    mmf(pa, Y0.t[:, 0, :], U.R.t[:, :], [Y0.buf, U.R.buf])
                    fw.op("dve", lambda e: e.tensor_tensor(out=U.R.t[:, :], in0=U.R.t[:, :], in1=pa.ap, op=ALU.subtract),
                          reads=[pa.buf], writes=[U.R.buf])
                    for lvl in range(1, 6):
                        prev, cur = U.YZ[(lvl - 1) % 2], U.YZ[lvl % 2]
                        fw.op("pe", lambda e: e.matmul(pp.ap[:, 0:128], lhsT=prev.t[:, 1, :], rhs=prev.t[:, 0, :], start=True, stop=True),
                              reads=[prev.buf], writes=[pp.buf])
                        nz = 2 if lvl < 5 else 1
                        if lvl < 5:
                            fw.op("pe", lambda e: e.matmul(pp.ap[:, 128:256], lhsT=prev.t[:, 0, :], rhs=prev.t[:, 1, :], start=True, stop=True),
                                  reads=[prev.buf], writes=[pp.buf])
                        ee = "act" if cnt["e"] % 2 == 0 else "dve"
                        cnt["e"] += 1
                        if ee == "act":
                            fw.op("act", lambda e: e.activation(out=cur.t[:, 0:nz, :], in_=pp.ap[:, 0:nz * 128].rearrange("p (a b) -> p a b", b=128),
                                                                func=AF.Copy), reads=[pp.buf], writes=[cur.buf])
                        else:
                            fw.op("dve", lambda e: e.tensor_copy(out=cur.t[:, 0:nz, :], in_=pp.ap[:, 0:nz * 128].rearrange("p (a b) -> p a b", b=128)),
                                  reads=[pp.buf], writes=[cur.buf])
                        mmf(pa, cur.t[:, 0, :], U.R.t[:, :], [cur.buf, U.R.buf])
                        fw.op("dve", lambda e: e.tensor_tensor(out=U.R.t[:, :], in0=U.R.t[:, :], in1=pa.ap, op=ALU.add),
                              reads=[pa.buf], writes=[U.R.buf])
                    fw.op("dve", lambda e: e.tensor_scalar(out=U.u.t[:, :], in0=U.R.t[:, 0:128], scalar1=b4[:, j:j + 1], scalar2=None, op0=ALU.mult),
                          reads=[U.R.buf, gb.buf], writes=[U.u.buf])
                    fw.op("pool", lambda e: e.tensor_scalar(out=U.w.t[:, :], in0=U.R.t[:, 128:256], scalar1=b4[:, j:j + 1], scalar2=None, op0=ALU.mult),
                          reads=[U.R.buf, gb.buf], writes=[U.w.buf])
                    fw.op("pe", lambda e: e.transpose(pt_.ap, U.w.t[:, :], identf.t[:, :]), reads=[U.w.buf, identf.buf], writes=[pt_.buf])
                    fw.op("act", lambda e: e.activation(out=U.wT.t[:, :], in_=pt_.ap, func=AF.Copy), reads=[pt_.buf], writes=[U.wT.buf])
                    fw.op("pool", lambda e: e.tensor_scalar(out=U.kd.t[:, :], in0=t32.t[:, 4 + q, :], scalar1=E2(j), scalar2=None, op0=ALU.mult),
                          reads=[t32.buf, a2.buf], writes=[U.kd.buf])
                oa = oacc[par]
                first = True
                for h in (((0, 1) if dirn == 0 else (1, 0)) if lvl >= 3 else ()):
                    r = slice(64 * h, 64 * h + 64)
                    for j in range(4):
                        U, P = us[j], pS[j % 2]
                        mmf(P[0], U.wT.t[:, :], Sb[j].t[:, :], [U.wT.buf, Sb[j].buf])
                        fw.op("dve", lambda e, U=U, P=P: e.tensor_tensor(out=U.vn.t[r, :], in0=U.u.t[r, :], in1=P[0].ap[r, :], op=ALU.subtract),
                              reads=[U.u.buf, P[0].buf], writes=[U.vn.buf])
                        q = j // 2
                        mmf(P[1], qb.t[:, q, :], Sb[j].t[:, :], [qb.buf, Sb[j].buf])
                        mmf(P[2], U.At.t[r, :], U.vn.t[r, :], [U.At.buf, U.vn.buf])
                        mmf(P[3], U.kd.t[r, :], U.vn.t[r, :], [U.kd.buf, U.vn.buf])
                        fw.op("act", lambda e, U=U, P=P: e.activation(out=U.p3s.t[r, :], in_=P[2].ap[r, :], func=AF.Copy),
                              reads=[P[2].buf], writes=[U.p3s.buf])
                        fw.op("dve", lambda e, U=U, P=P, j=j: e.scalar_tensor_tensor(out=oa.t[r, j, :], in0=P[1].ap[r, :], scalar=a2.t[r, j:j + 1],
                                                                                     in1=U.p3s.t[r, :], op0=ALU.mult, op1=ALU.add),
                              reads=[P[1].buf, a2.buf, U.p3s.buf], writes=[oa.buf], excl=(first and j == 0))
                        fw.op("dve", lambda e, P=P, j=j, h=h: e.scalar_tensor_tensor(out=S[j].t[:, :], in0=S[j].t[:, :], scalar=gbc(h, j), in1=P[3].ap,
                                                                                     op0=ALU.mult, op1=ALU.add),
                              reads=[a2.buf, P[3].buf], writes=[S[j].buf])
                        fw.op("act", lambda e, j=j: e.activation(out=Sb[j].t[:, :], in_=S[j].t[:, :], func=AF.Copy), reads=[S[j].buf], writes=[Sb[j].buf])
                    first = False
                if lvl < 4:
                    continue
                if dirn == 0:
                    fw.dma("act", ofw[s, blk, :, :, :], oa.t[:, :, :], reads=[oa.buf], writes=[ofw_bufs[s][blk]], owner=ofw_own)
                else:
                    of = of32[par]
                    fw.dma("sp", of.t[:, :, :], ofw[s, blk, :, :, :], reads=[ofw_bufs[s][blk]], writes=[of.buf])
                    fw.op("dve", lambda e: e.tensor_tensor(out=oa.t[:, :, :], in0=oa.t[:, :, :], in1=of.t[:, :, :], op=ALU.add),
                          reads=[of.buf], writes=[oa.buf])
                    if "osum" in dbg and blk == 0:
                        dump("osum", oa.t[:, :, :], [128, 4, 128], [oa.buf])
                    fw.op("dve", lambda e: e.memset(ssq.t[:, 0:4], 0.0), writes=[ssq.buf])
                    for j in range(4):
                        fw.op("act", lambda e, j=j: e.activation(out=of.t[:, j, :], in_=oa.t[:, j, :], func=AF.Square, accum_out=ssq.t[:, j:j + 1]),
                              reads=[oa.buf], writes=[of.buf, ssq.buf])
                    fw.op("act", lambda e: e.activation(out=ssq.t[:, 4:8], in_=ssq.t[:, 0:4], func=AF.Sqrt, bias=RMS_EPS, scale=1.0 / 128),
                          reads=[ssq.buf], writes=[ssq.buf])
                    fw.op("dve", lambda e: e.reciprocal(out=ssq.t[:, 4:8], in_=ssq.t[:, 4:8]), reads=[ssq.buf], writes=[ssq.buf])
                    for j in range(4):
                        fw.op("dve", lambda e, j=j: e.scalar_tensor_tensor(out=onb.t[:, j, :], in0=oa.t[:, j, :], scalar=ssq.t[:, 4 + j:5 + j],
                                                                           in1=dnw.t[:, :], op0=ALU.mult, op1=ALU.mult),
                              reads=[oa.buf, ssq.buf, dnw.buf], writes=[onb.buf], excl=(j == 0))
                    for j in range(4):
                        fw.op("pe", lambda e, j=j: e.transpose(tpb.t[:, j, :], onb.t[:, j, :], identb.t[:, :]),
                              reads=[onb.buf, identb.buf], writes=[pO.buf])
                    og_ = ostg[par]
                    fw.op("act", lambda e: e.activation(out=og_.t[:, :, 1:129], in_=tpb.t[:, 0:4, :], func=AF.Copy), reads=[pO.buf], writes=[og_.buf])
                    c0 = 1 + blk * 128
                    lo_ = 0 if blk == 0 else 1
                    hi_ = 130 if blk == nblk - 1 else 129
                    fw.dma("act", oT[:, s, c0 - 1 + lo_:c0 - 1 + hi_].rearrange("(j p) t -> p j t", p=128), og_.t[:, :, lo_:hi_], reads=[og_.buf],
                           writes=[outb], excl=False)


def kernel(**inputs):
    import ml_dtypes
    inp = {k: np.asarray(v) for k, v in inputs.items()}
    cores = list(range(NCORE))
    com1 = prep_p1_common(inp)
    xq = prep_xq(inp)
    maps1 = [prep_p1_core(inp, c, com1, xq) for c in cores]
    nc1, _, _ = build_p1({})
    res1 = run_bass_kernel_spmd(nc1, maps1, core_ids=cores)
    o_full = np.concatenate([np.asarray(res1.results[c]["oT"]) for c in cores], axis=0)
    del maps1, xq, res1
    com2 = prep_common(inp)
    maps2 = []
    for c in cores:
        d = prep_core(inp, c, com2)
        d["o_ext"] = np.ascontiguousarray(o_full[:, :, c * SEG:c * SEG + SEG + 2])
        maps2.append(d)
    nc2, _, _ = build({"ext_o": True})
    res2 = run_bass_kernel_spmd(nc2, maps2, core_ids=cores)
    ys = [np.asarray(res2.results[c]["y"]) for c in cores]
    full = np.concatenate(ys, axis=1)
    y_prompt = np.ascontiguousarray(full[0:1]).astype(np.float32)
    y_sample = np.ascontiguousarray(full[1:3]).astype(np.float32)
    return (y_prompt, y_sample)
```

```python
import math
import numpy as np
import concourse.bass as bass
import concourse.mybir as mybir
from concourse.bass_utils import run_bass_kernel_spmd

F32 = mybir.dt.float32
BF16 = mybir.dt.bfloat16
AF = mybir.ActivationFunctionType
ALU = mybir.AluOpType
AX = mybir.AxisListType

D = 2048
NCORE = 8
SEQ = 16384
NSEQ = 3
SEG = SEQ // NCORE
TO = 410
WX = TO + 2
WH = TO + 32
NT = 5
HALO = 16
DFF = 5632
RMS_EPS = 1e-6
LN_EPS = 1e-5


class Buf:
    __slots__ = ("name", "w", "r", "dsem", "dcnt", "psum")

    def __init__(self, name="", psum=False):
        self.name = name
        self.psum = psum
        self.w = {}
        self.r = {}
        self.dsem = None
        self.dcnt = 0


class FW:
    LIMIT = 30000
    SAME_ENGINE_SYNC = True

    def __init__(self, nc, sems):
        self.nc = nc
        self.free = list(sems)
        self.eng = {"pe": nc.tensor, "dve": nc.vector, "act": nc.scalar,
                    "pool": nc.gpsimd, "sp": nc.sync}
        self.csem = {}
        self.ccnt = {}
        for e in ("pe", "dve", "act", "pool"):
            self.csem[e] = self.free.pop()
            self.ccnt[e] = 0
        self.seen = {e: {} for e in self.eng}
        self.ninstr = 0
        self.alld = {}

    def _collect(self, reads, writes):
        need = {}
        for b in reads:
            for s, v in b.w.items():
                if need.get(s, 0) < v:
                    need[s] = v
        for b in writes:
            for s, v in b.w.items():
                if need.get(s, 0) < v:
                    need[s] = v
            for s, v in b.r.items():
                if need.get(s, 0) < v:
                    need[s] = v
        return need

    def _wait(self, e, need):
        seen = self.seen[e]
        own = self.csem.get(e)
        for s, v in need.items():
            if seen.get(s, 0) >= v:
                continue
            if s is own and (e == "pe" or not self.SAME_ENGINE_SYNC):
                continue
            self.eng[e].wait_ge(s, v)
            seen[s] = v
            self.ninstr += 1

    def _mark(self, tok, reads, writes, excl=True):
        s, v = tok
        for b in reads:
            if b.r.get(s, 0) < v:
                b.r[s] = v
        for b in writes:
            if excl:
                b.w = {s: v}
                b.r = {}
            else:
                if b.w.get(s, 0) < v:
                    b.w[s] = v

    def op(self, e, fn, reads=(), writes=(), excl=True):
        pr = [b for b in reads if b.psum]
        if pr:
            reads = [b for b in reads if not b.psum]
            if excl:
                writes = list(writes) + pr
            else:
                self._wait(e, self._collect((), pr))
        self._wait(e, self._collect(reads, writes))
        ins = fn(self.eng[e])
        if self.ccnt[e] >= self.LIMIT:
            self.csem[e] = self.free.pop()
            self.ccnt[e] = 0
        self.ccnt[e] += 1
        tok = (self.csem[e], self.ccnt[e])
        ins.then_inc(tok[0], 1)
        self._mark(tok, reads, writes, excl)
        if pr and not excl:
            self._mark(tok, (), pr, True)
        self.ninstr += 1
        return tok

    def dma(self, q, out, in_, reads=(), writes=(), owner=None, excl=True, **kw):
        self._wait(q, self._collect(reads, writes))
        ins = self.eng[q].dma_start(out=out, in_=in_, **kw)
        o = owner or (writes[0] if writes else reads[0])
        if o.dsem is None or o.dcnt + 16 > self.LIMIT:
            o.dsem = self.free.pop()
            o.dcnt = 0
        o.dcnt += 16
        tok = (o.dsem, o.dcnt)
        ins.then_inc(tok[0], 16)
        self.alld[tok[0]] = tok[1]
        self._mark(tok, reads, writes, excl)
        self.ninstr += 1
        return tok

    def barrier(self):
        need = dict(self.alld)
        for e in ("pe", "dve", "act", "pool"):
            if self.ccnt[e] > 0:
                need[self.csem[e]] = self.ccnt[e]
        for e in ("pe", "dve", "act", "pool", "sp"):
            seen = self.seen[e]
            for s_, v in need.items():
                if seen.get(s_, 0) >= v:
                    continue
                self.eng[e].wait_ge(s_, v)
                seen[s_] = v
                self.ninstr += 1

    def wait_all(self, e, bufs):
        self._wait(e, self._collect((), bufs))


class T:
    def __init__(self, t, nchunk=1, name=""):
        self.t = t
        self.bufs = [Buf(f"{name}{i}") for i in range(nchunk)]

    @property
    def buf(self):
        return self.bufs[0]


def build(cfg):
    nseg = cfg.get("nseg", NSEQ)
    tiles = cfg.get("tiles", list(range(NT)))
    dbg = cfg.get("dbg", [])
    nc = bass.Bass("TRN2", target_bir_lowering=False)
    from contextlib import ExitStack
    es = ExitStack()

    def din(name, shape, dt=F32):
        return nc.dram_tensor(name, list(shape), dt, kind="ExternalInput")

    def dint(name, shape, dt):
        return nc.dram_tensor(name, list(shape), dt, kind="Internal")

    def dout(name, shape, dt=F32):
        return nc.dram_tensor(name, list(shape), dt, kind="ExternalOutput")

    xp = din("xp", [NSEQ, SEG + 2 * HALO, D])
    msk = din("msk", [NSEQ * NT, 128, WX])
    wz_f = din("wz", [32, 128, D])
    wglu_f = din("wglu", [32, 128, D])
    wgt_f = din("wgt", [32, 128, D])
    wdo_f = din("wdo", [16, 128, 4096])
    wco_f = din("wco", [16, 128, D])
    wmo_f = din("wmo", [16, 128, D])
    wup_f = din("wup", [88, 128, D])
    wdn_f = din("wdn", [32, 128, DFF // 2])
    vecs = din("vecs", [128, 16, 8])
    cconv = din("cconv", [128, 16, 31])
    fconv = din("fconv", [128, 88, 3])
    dnw = din("dnw", [128, 1])
    ident_d = din("ident", [128, 128])
    o_ext = din("o_ext", [4096, NSEQ, SEG + 2], BF16) if cfg.get("ext_o") else None
    y = dout("y", [NSEQ, SEG, D])

    wz_b = dint("wz_b", [32, 128, D], BF16)
    wglu_b = dint("wglu_b", [32, 128, D], BF16)
    wgt_b = dint("wgt_b", [32, 128, D], BF16)
    wdo_b = dint("wdo_b", [16, 128, 4096], BF16)
    wco_b = dint("wco_b", [16, 128, D], BF16)
    wmo_b = dint("wmo_b", [16, 128, D], BF16)
    wup_b = dint("wup_b", [88, 128, D], BF16)
    wdn_b = dint("wdn_b", [32, 128, DFF // 2], BF16)

    dbg_out = {}
    dbg_bufs = []
    outb = Buf("out")
    o_buf_dram = Buf("o_dram")
    if o_ext is not None:
        o_src = lambda m, s_, c0: o_ext[m * 128:(m + 1) * 128, s_, c0:c0 + WX]

    sems = [es.enter_context(nc.semaphore(f"s{i}")) for i in range(100)]
    fw = FW(nc, sems)

    def sb(name, shape, dt, nchunk=1):
        return T(es.enter_context(nc.sbuf_tensor(name, list(shape), dt)), nchunk, name)

    def ps(name, shape, dt):
        t_ = T(es.enter_context(nc.psum_tensor(name, list(shape), dt)), 1, name)
        t_.bufs[0].psum = True
        return t_

    F1 = sb("F1", [128, 16, WX], F32, 16)
    F2 = sb("F2", [128, 16, WX], F32, 16)
    B1 = sb("B1", [128, 16, WH], BF16, 16)
    B2 = sb("B2", [128, 16, WH], BF16, 16)
    B3 = sb("B3", [128, 16, WX], BF16, 16)
    B4 = sb("B4", [128, 16, WX], BF16, 16)
    BF = sb("BF", [128, 44, WX], BF16, 44)
    xt = [sb(f"xt{i}", [128, D], F32) for i in range(2)]
    xn = sb("xn", [128, D], BF16)
    NWS = 3
    wsl = [sb(f"w{i}", [128, 4096], BF16) for i in range(NWS)]
    tmpA = [sb(f"tA{i}", [128, WH], F32) for i in range(3)]
    tmpB = [sb(f"tB{i}", [128, WX], F32) for i in range(3)]
    obuf = [sb(f"ob{i}", [128, WX], BF16) for i in range(2)]
    rowA = sb("rowA", [128, WH], F32)
    rowB = sb("rowB", [128, WH], F32)
    mskt = sb("mskt", [128, WX], F32)
    ss = sb("ss", [128, 4], F32)
    identf = sb("identf", [128, 128], F32)
    identb = sb("identb", [128, 128], BF16)
    onesf = sb("onesf", [128, 128], F32)
    vec = sb("vec", [128, 16, 8], F32)
    gpre = sb("gpre", [128, 16], F32)
    gpost = sb("gpost", [128, 16], F32)
    gffn = sb("gffn", [128, 16], F32)
    gfpost = sb("gfpost", [128, 16], F32)
    cbias = sb("cbias", [128, 16], F32)
    lnw = sb("lnw", [128, 16], F32)
    lnb = sb("lnb", [128, 16], F32)
    ccw = sb("ccw", [128, 16, 31], F32)
    fcw = sb("fcw", [128, 88, 3], F32)

    NPS = 4
    mmps = [ps(f"mm{i}", [128, 512], F32) for i in range(NPS)]
    tpb = ps("tpb", [128, 8, 128], BF16)
    tpf = [ps(f"tpf{i}", [128, 4, 128], F32) for i in range(2)]
    stp = ps("stp", [128, 512], F32)

    st = {"wi": 0, "pi": 0, "ti": 0, "xi": 0, "oi": 0, "fi": 0}

    fw.dma("sp", identf.t[:, :], ident_d[:, :], writes=[identf.buf])
    fw.dma("sp", vec.t[:, :, :], vecs[:, :, :], writes=[vec.buf])
    fw.dma("sp", ccw.t[:, :, :], cconv[:, :, :], writes=[ccw.buf])
    fw.dma("sp", fcw.t[:, :, :], fconv[:, :, :], writes=[fcw.buf])
    fw.op("dve", lambda e: e.tensor_copy(out=identb.t[:, :], in_=identf.t[:, :]),
          reads=[identf.buf], writes=[identb.buf])
    fw.op("dve", lambda e: e.memset(onesf.t[:, :], 1.0), writes=[onesf.buf])
    for i, gt_ in enumerate((gpre, gpost, gffn, gfpost, cbias, lnw, lnb)):
        fw.op("dve", lambda e, i=i, gt_=gt_: e.tensor_copy(out=gt_.t[:, :], in_=vec.t[:, :, i]),
              reads=[vec.buf], writes=[gt_.buf])

    wbufs = {}
    for name, src, dst, mcn in (("wglu", wglu_f, wglu_b, 32), ("wgt", wgt_f, wgt_b, 32), ("wz", wz_f, wz_b, 32),
                                ("wdo", wdo_f, wdo_b, 16), ("wco", wco_f, wco_b, 16), ("wmo", wmo_f, wmo_b, 16),
                                ("wup", wup_f, wup_b, 88), ("wdn", wdn_f, wdn_b, 32)):
        wb_ = Buf(name)
        wbufs[name] = [wb_] * mcn
        if cfg.get("skip_" + name):
            continue
        G = 4
        for m0 in range(0, mcn, G):
            fw.dma("pool", dst[m0:m0 + G, :, :], src[m0:m0 + G, :, :], writes=[wb_], excl=False)

    def dump(name, ap, shape, bufs, dt=F32):
        if name not in dbg:
            return
        if name not in dbg_out:
            dbg_out[name] = dout("dbg_" + name, shape, dt)
        b = Buf("dbg")
        dbg_bufs.append(b)
        fw.dma("sp", dbg_out[name][tuple(slice(None) for _ in shape)], ap, reads=bufs, writes=[b])

    def linear(wd, wname, order, KC, rhs, rbufs, N, epi, ksplit=1):
        KP = KC // ksplit
        for mc in order:
            pt = mmps[st["pi"] % NPS]
            st["pi"] += 1
            for part in range(ksplit):
                slot = wsl[st["wi"] % NWS]
                st["wi"] += 1
                wi = mc * ksplit + part
                fw.dma("sp", slot.t[:, 0:KP * 128], wd[wi, :, :], reads=[wbufs[wname][wi]], writes=[slot.buf])
                for k in range(KP):
                    kc = part * KP + k
                    fw.op("pe", lambda e, kc=kc, k=k, slot=slot: e.matmul(
                        pt.t[:, 0:N], lhsT=slot.t[:, k * 128:(k + 1) * 128], rhs=rhs(kc),
                        start=(kc == 0), stop=(kc == KC - 1)),
                          reads=[slot.buf, rbufs(kc)], writes=[pt.buf])
            epi(mc, pt)

    def bcast_sum(src_ap_fn, src_bufs, nch, N):
        for c in range(nch):
            fw.op("pe", lambda e, c=c: e.matmul(stp.t[:, 0:N], lhsT=onesf.t[:, :], rhs=src_ap_fn(c),
                                                start=(c == 0), stop=(c == nch - 1)),
                  reads=[onesf.buf, src_bufs(c)], writes=[stp.buf])

    def tile(s, ti):
        a = min(TO * ti, SEG - TO)
        r0 = 0
        while r0 < WH:
            n = min(128, WH - r0)
            xb = xt[st["xi"] % 2]
            st["xi"] += 1
            fw.dma("sp", xb.t[0:n, :], xp[s, a + r0:a + r0 + n, :], writes=[xb.buf])
            fw.op("dve", lambda e: e.memset(ss.t[:, 0:1], 0.0), writes=[ss.buf])
            fw.op("act", lambda e: e.activation(out=xn.t[0:n, :], in_=xb.t[0:n, :], func=AF.Square,
                                                accum_out=ss.t[0:n, 0:1]),
                  reads=[xb.buf], writes=[xn.buf, ss.buf])
            fw.op("act", lambda e: e.activation(out=ss.t[0:n, 1:2], in_=ss.t[0:n, 0:1], func=AF.Sqrt,
                                                bias=RMS_EPS, scale=1.0 / D),
                  reads=[ss.buf], writes=[ss.buf])
            fw.op("dve", lambda e: e.reciprocal(out=ss.t[0:n, 1:2], in_=ss.t[0:n, 1:2]),
                  reads=[ss.buf], writes=[ss.buf])
            fw.op("dve", lambda e: e.tensor_scalar(out=xn.t[0:n, :], in0=xb.t[0:n, :], scalar1=ss.t[0:n, 1:2],
                                                   scalar2=None, op0=ALU.mult),
                  reads=[xb.buf, ss.buf], writes=[xn.buf])
            for k4 in range(4):
                for j in range(4):
                    kc = k4 * 4 + j
                    fw.op("pe", lambda e, kc=kc, j=j: e.transpose(tpb.t[:, j, 0:n], xn.t[0:n, kc * 128:(kc + 1) * 128],
                                                                  identb.t[0:n, 0:n]),
                          reads=[xn.buf, identb.buf], writes=[tpb.buf])
                for j in range(4):
                    kc = k4 * 4 + j
                    fw.op("act", lambda e, kc=kc, j=j: e.activation(out=B1.t[:, kc, r0:r0 + n], in_=tpb.t[:, j, 0:n],
                                                                    func=AF.Copy, scale=gpre.t[:, kc:kc + 1]),
                          reads=[tpb.buf, gpre.buf], writes=[B1.bufs[kc]], excl=(r0 == 0))
            lo = max(r0, HALO - 1)
            hi = min(r0 + n, HALO - 1 + WX)
            if hi > lo:
                for k4 in range(4):
                    tp = tpf[k4 % 2]
                    for j in range(4):
                        kc = k4 * 4 + j
                        fw.op("pe", lambda e, kc=kc, j=j, tp=tp: e.transpose(tp.t[:, j, 0:n],
                                                                             xb.t[0:n, kc * 128:(kc + 1) * 128],
                                                                             identf.t[0:n, 0:n]),
                              reads=[xb.buf, identf.buf], writes=[tp.buf])
                    fw.op("dve", lambda e, k4=k4, tp=tp: e.tensor_copy(
                        out=F1.t[:, k4 * 4:k4 * 4 + 4, lo - (HALO - 1):hi - (HALO - 1)],
                        in_=tp.t[:, :, lo - r0:hi - r0]),
                          reads=[tp.buf], writes=F1.bufs[k4 * 4:k4 * 4 + 4], excl=(lo == HALO - 1))
            r0 += n
        dump("hT", B1.t[:, :, :], [128, 16, WH], B1.bufs, BF16)
        dump("xT", F1.t[:, :, :], [128, 16, WX], F1.bufs)

        hrhs = lambda kc: B1.t[:, kc, 0:WH]
        hbuf = lambda kc: B1.bufs[kc]
        gl_order = []
        for i in range(16):
            gl_order += [16 + i, i]
        sgt = {}

        def epi_glu(mc, pt):
            if mc >= 16:
                tb = tmpA[st["ti"] % 3]
                st["ti"] += 1
                fw.op("act", lambda e: e.activation(out=tb.t[:, 0:WH], in_=pt.t[:, 0:WH], func=AF.Sigmoid),
                      reads=[pt.buf], writes=[tb.buf])
                sgt["t"] = tb
            else:
                tb = sgt["t"]
                fw.op("dve", lambda e: e.tensor_tensor(out=B2.t[:, mc, 0:WH], in0=pt.t[:, 0:WH], in1=tb.t[:, 0:WH],
                                                       op=ALU.mult),
                      reads=[pt.buf, tb.buf], writes=[B2.bufs[mc]])
        linear(wglu_b, "wglu", gl_order, 16, hrhs, hbuf, WH, epi_glu)
        dump("hcv", B2.t[:, :, :], [128, 16, WH], B2.bufs, BF16)

        for kc in range(16):
            ce = "dve"
            fw.op(ce, lambda e, kc=kc: e.tensor_scalar(out=F2.t[:, kc, 0:WX], in0=B2.t[:, kc, 0:WX],
                                                       scalar1=ccw.t[:, kc, 0:1], scalar2=cbias.t[:, kc:kc + 1],
                                                       op0=ALU.mult, op1=ALU.add),
                  reads=[B2.bufs[kc], ccw.buf, cbias.buf], writes=[F2.bufs[kc]])
            for j in range(1, 31):
                fw.op(ce, lambda e, kc=kc, j=j: e.scalar_tensor_tensor(out=F2.t[:, kc, 0:WX], in0=B2.t[:, kc, j:j + WX],
                                                                       scalar=ccw.t[:, kc, j:j + 1], in1=F2.t[:, kc, 0:WX],
                                                                       op0=ALU.mult, op1=ALU.add),
                      reads=[B2.bufs[kc], ccw.buf], writes=[F2.bufs[kc]])
        dump("cv", F2.t[:, :, :], [128, 16, WX], F2.bufs)
        bcast_sum(lambda c: F2.t[:, c, 0:WX], lambda c: F2.bufs[c], 16, WX)
        fw.op("act", lambda e: e.activation(out=rowA.t[:, 0:WX], in_=stp.t[:, 0:WX], func=AF.Copy, scale=1.0 / D),
              reads=[stp.buf], writes=[rowA.buf])
        sqb = []
        for kc in range(16):
            tb = tmpA[st["ti"] % 3]
            st["ti"] += 1
            fw.op("act", lambda e, kc=kc, tb=tb: e.activation(out=tb.t[:, 0:WX], in_=F2.t[:, kc, 0:WX], func=AF.Square),
                  reads=[F2.bufs[kc]], writes=[tb.buf])
            fw.op("pe", lambda e, kc=kc, tb=tb: e.matmul(stp.t[:, 0:WX], lhsT=onesf.t[:, :], rhs=tb.t[:, 0:WX],
                                                         start=(kc == 0), stop=(kc == 15)),
                  reads=[onesf.buf, tb.buf], writes=[stp.buf])
        tb = tmpB[0]
        fw.op("dve", lambda e: e.tensor_tensor(out=tb.t[:, 0:WX], in0=rowA.t[:, 0:WX], in1=rowA.t[:, 0:WX], op=ALU.mult),
              reads=[rowA.buf], writes=[tb.buf])
        fw.op("dve", lambda e: e.scalar_tensor_tensor(out=rowB.t[:, 0:WX], in0=stp.t[:, 0:WX], scalar=1.0 / D,
                                                      in1=tb.t[:, 0:WX], op0=ALU.mult, op1=ALU.subtract),
              reads=[stp.buf, tb.buf], writes=[rowB.buf])
        fw.op("act", lambda e: e.activation(out=rowB.t[:, 0:WX], in_=rowB.t[:, 0:WX], func=AF.Sqrt, bias=LN_EPS, scale=1.0),
              reads=[rowB.buf], writes=[rowB.buf])
        fw.op("dve", lambda e: e.reciprocal(out=rowB.t[:, 0:WX], in_=rowB.t[:, 0:WX]), reads=[rowB.buf], writes=[rowB.buf])
        for kc in range(16):
            tb = tmpB[1 + kc % 2]
            ce = "dve"
            fw.op(ce, lambda e, kc=kc, tb=tb: e.tensor_tensor(out=tb.t[:, 0:WX], in0=F2.t[:, kc, 0:WX], in1=rowA.t[:, 0:WX],
                                                              op=ALU.subtract),
                  reads=[F2.bufs[kc], rowA.buf], writes=[tb.buf])
            fw.op(ce, lambda e, kc=kc, tb=tb: e.tensor_tensor(out=tb.t[:, 0:WX], in0=tb.t[:, 0:WX], in1=rowB.t[:, 0:WX],
                                                              op=ALU.mult),
                  reads=[tb.buf, rowB.buf], writes=[tb.buf])
            fw.op("act", lambda e, kc=kc, tb=tb: e.activation(out=B3.t[:, kc, 0:WX], in_=tb.t[:, 0:WX], func=AF.Silu,
                                                              bias=lnb.t[:, kc:kc + 1], scale=lnw.t[:, kc:kc + 1]),
                  reads=[tb.buf, lnb.buf, lnw.buf], writes=[B3.bufs[kc]])
        dump("cb", B3.t[:, :, :], [128, 16, WX], B3.bufs, BF16)

        xrhs = lambda kc: B1.t[:, kc, HALO - 1:HALO - 1 + WX]
        cbrhs = lambda kc: B3.t[:, kc, 0:WX]
        cbbuf = lambda kc: B3.bufs[kc]
        for mc in range(16):
            def epi_gb(m, pt):
                tb = tmpA[st["ti"] % 3]
                st["ti"] += 1
                fw.op("act", lambda e: e.activation(out=tb.t[:, 0:WX], in_=pt.t[:, 0:WX], func=AF.Sigmoid),
                      reads=[pt.buf], writes=[tb.buf])
                sgt["t"] = tb
            linear(wgt_b, "wgt", [16 + mc], 16, xrhs, hbuf, WX, epi_gb)

            def epi_yb(m, pt):
                tb = sgt["t"]
                fw.op("dve", lambda e: e.tensor_tensor(out=B2.t[:, m, 0:WX], in0=pt.t[:, 0:WX], in1=tb.t[:, 0:WX],
                                                       op=ALU.mult),
                      reads=[pt.buf, tb.buf], writes=[B2.bufs[m]])
            linear(wco_b, "wco", [mc], 16, cbrhs, cbbuf, WX, epi_yb)

        def epi_ga(m, pt):
            fw.op("act", lambda e: e.activation(out=B4.t[:, m, 0:WX], in_=pt.t[:, 0:WX], func=AF.Sigmoid),
                  reads=[pt.buf], writes=[B4.bufs[m]])
        linear(wgt_b, "wgt", list(range(16)), 16, xrhs, hbuf, WX, epi_ga)
        dump("gbyb", B2.t[:, :, :], [128, 16, WH], B2.bufs, BF16)
        dump("ga", B4.t[:, :, :], [128, 16, WX], B4.bufs, BF16)

        tok0 = a - 1 + 1
        def epi_z(m, pt):
            tb = tmpA[st["ti"] % 3]
            st["ti"] += 1
            ob = obuf[st["oi"] % 2]
            st["oi"] += 1
            fw.dma("sp", ob.t[:, 0:WX], o_src(m, s, tok0), reads=[o_buf_dram], writes=[ob.buf])
            fw.op("act", lambda e: e.activation(out=tb.t[:, 0:WX], in_=pt.t[:, 0:WX], func=AF.Silu),
                  reads=[pt.buf], writes=[tb.buf])
            fw.op("dve", lambda e: e.tensor_tensor(out=BF.t[:, m, 0:WX], in0=tb.t[:, 0:WX], in1=ob.t[:, 0:WX], op=ALU.mult),
                  reads=[tb.buf, ob.buf], writes=[BF.bufs[m]])
        linear(wz_b, "wz", list(range(32)), 16, xrhs, hbuf, WX, epi_z)
        dump("og", BF.t[:, 0:32, :], [128, 32, WX], BF.bufs[0:32], BF16)

        def epi_ya(m, pt):
            tb = tmpA[st["ti"] % 3]
            st["ti"] += 1
            fw.op("dve", lambda e: e.tensor_tensor(out=tb.t[:, 0:WX], in0=pt.t[:, 0:WX], in1=B4.t[:, m, 0:WX], op=ALU.mult),
                  reads=[pt.buf, B4.bufs[m]], writes=[tb.buf])
            fw.op("pool", lambda e: e.tensor_tensor(out=B3.t[:, m, 0:WX], in0=tb.t[:, 0:WX], in1=B2.t[:, m, 0:WX], op=ALU.add),
                  reads=[tb.buf, B2.bufs[m]], writes=[B3.bufs[m]])
        linear(wdo_b, "wdo", list(range(16)), 32, lambda kc: BF.t[:, kc, 0:WX], lambda kc: BF.bufs[kc], WX, epi_ya)
        dump("merged", B3.t[:, :, :], [128, 16, WX], B3.bufs, BF16)

        def epi_m2(m, pt):
            fw.op("act", lambda e: e.activation(out=F2.t[:, m, 0:WX], in_=pt.t[:, 0:WX], func=AF.Copy),
                  reads=[pt.buf], writes=[F2.bufs[m]])
        linear(wmo_b, "wmo", list(range(16)), 16, lambda kc: B3.t[:, kc, 0:WX], lambda kc: B3.bufs[kc], WX, epi_m2)

        def rms_row(src, n, dst_row):
            for kc in range(16):
                tb = tmpA[st["ti"] % 3]
                st["ti"] += 1
                fw.op("act", lambda e, kc=kc, tb=tb: e.activation(out=tb.t[:, 0:n], in_=src(kc), func=AF.Square),
                      reads=[srcb(src, kc)], writes=[tb.buf])
                fw.op("pe", lambda e, kc=kc, tb=tb: e.matmul(stp.t[:, 0:n], lhsT=onesf.t[:, :], rhs=tb.t[:, 0:n],
                                                             start=(kc == 0), stop=(kc == 15)),
                      reads=[onesf.buf, tb.buf], writes=[stp.buf])
            fw.op("act", lambda e: e.activation(out=dst_row.t[:, 0:n], in_=stp.t[:, 0:n], func=AF.Sqrt, bias=RMS_EPS,
                                                scale=1.0 / D),
                  reads=[stp.buf], writes=[dst_row.buf])
            fw.op("dve", lambda e: e.reciprocal(out=dst_row.t[:, 0:n], in_=dst_row.t[:, 0:n]),
                  reads=[dst_row.buf], writes=[dst_row.buf])

        srcmap = {}

        def srcb(src, kc):
            return srcmap[src][kc]
        m2src = lambda kc: F2.t[:, kc, 0:WX]
        srcmap[m2src] = F2.bufs
        rms_row(m2src, WX, rowA)
        for kc in range(16):
            tb = tmpB[kc % 3]
            ce = "dve"
            fw.op(ce, lambda e, kc=kc, tb=tb: e.tensor_tensor(out=tb.t[:, 0:WX], in0=F2.t[:, kc, 0:WX], in1=rowA.t[:, 0:WX],
                                                              op=ALU.mult),
                  reads=[F2.bufs[kc], rowA.buf], writes=[tb.buf])
            fw.op(ce, lambda e, kc=kc, tb=tb: e.scalar_tensor_tensor(out=F1.t[:, kc, 0:WX], in0=tb.t[:, 0:WX],
                                                                     scalar=gpost.t[:, kc:kc + 1], in1=F1.t[:, kc, 0:WX],
                                                                     op0=ALU.mult, op1=ALU.add),
                  reads=[tb.buf, gpost.buf], writes=[F1.bufs[kc]])
        dump("x1", F1.t[:, :, :], [128, 16, WX], F1.bufs)

        x1src = lambda kc: F1.t[:, kc, 0:WX]
        srcmap[x1src] = F1.bufs
        rms_row(x1src, WX, rowB)
        fw.dma("sp", mskt.t[:, :], msk[s * NT + ti, :, :], writes=[mskt.buf])
        fw.op("dve", lambda e: e.tensor_tensor(out=rowB.t[:, 0:WX], in0=rowB.t[:, 0:WX], in1=mskt.t[:, 0:WX], op=ALU.mult),
              reads=[mskt.buf], writes=[rowB.buf])
        for kc in range(16):
            ce = "dve"
            fw.op(ce, lambda e, kc=kc: e.scalar_tensor_tensor(out=B1.t[:, kc, 0:WX], in0=F1.t[:, kc, 0:WX],
                                                              scalar=gffn.t[:, kc:kc + 1], in1=rowB.t[:, 0:WX],
                                                              op0=ALU.mult, op1=ALU.mult),
                  reads=[F1.bufs[kc], gffn.buf, rowB.buf], writes=[B1.bufs[kc]])
        dump("h2", B1.t[:, :, :], [128, 16, WH], B1.bufs, BF16)

        h2rhs = lambda kc: B1.t[:, kc, 0:WX]
        cres = {}

        def conv3(m, pt, dst):
            fw.op("act", lambda e: e.activation(out=dst.t[:, 0:TO], in_=pt.t[:, 0:TO], func=AF.Copy,
                                                scale=fcw.t[:, m, 0:1]),
                  reads=[pt.buf, fcw.buf], writes=[dst.buf])
            for j in (1, 2):
                fw.op("dve", lambda e, j=j: e.scalar_tensor_tensor(out=dst.t[:, 0:TO], in0=pt.t[:, j:j + TO],
                                                                   scalar=fcw.t[:, m, j:j + 1], in1=dst.t[:, 0:TO],
                                                                   op0=ALU.mult, op1=ALU.add),
                      reads=[pt.buf, fcw.buf], writes=[dst.buf])

        def epi_up(m, pt):
            if m < 44:
                tb = tmpA[st["ti"] % 3]
                st["ti"] += 1
                conv3(m, pt, tb)
                fw.op("act", lambda e: e.activation(out=tb.t[:, 0:TO], in_=tb.t[:, 0:TO], func=AF.Silu),
                      reads=[tb.buf], writes=[tb.buf])
                cres["g"] = tb
            else:
                tb = tmpB[st["fi"] % 3]
                st["fi"] += 1
                conv3(m, pt, tb)
                tg = cres["g"]
                fw.op("pool", lambda e: e.tensor_tensor(out=BF.t[:, m - 44, 0:TO], in0=tb.t[:, 0:TO], in1=tg.t[:, 0:TO],
                                                        op=ALU.mult),
                      reads=[tb.buf, tg.buf], writes=[BF.bufs[m - 44]])
        up_order = []
        for i in range(44):
            up_order += [i, 44 + i]
        linear(wup_b, "wup", up_order, 16, h2rhs, hbuf, WX, epi_up)
        dump("f", BF.t[:, :, :], [128, 44, WX], BF.bufs, BF16)

        def epi_d(m, pt):
            fw.op("act", lambda e: e.activation(out=F2.t[:, m, 0:TO], in_=pt.t[:, 0:TO], func=AF.Copy),
                  reads=[pt.buf], writes=[F2.bufs[m]])
        linear(wdn_b, "wdn", list(range(16)), 44, lambda kc: BF.t[:, kc, 0:TO], lambda kc: BF.bufs[kc], TO, epi_d, ksplit=2)
        dsrc = lambda kc: F2.t[:, kc, 0:TO]
        srcmap[dsrc] = F2.bufs
        rms_row(dsrc, TO, rowA)
        for kc in range(16):
            tb = tmpB[kc % 3]
            ce = "dve"
            fw.op(ce, lambda e, kc=kc, tb=tb: e.tensor_tensor(out=tb.t[:, 0:TO], in0=F2.t[:, kc, 0:TO], in1=rowA.t[:, 0:TO],
                                                              op=ALU.mult),
                  reads=[F2.bufs[kc], rowA.buf], writes=[tb.buf])
            fw.op(ce, lambda e, kc=kc, tb=tb: e.scalar_tensor_tensor(out=F2.t[:, kc, 0:TO], in0=tb.t[:, 0:TO],
                                                                     scalar=gfpost.t[:, kc:kc + 1], in1=F1.t[:, kc, 1:1 + TO],
                                                                     op0=ALU.mult, op1=ALU.add),
                  reads=[tb.buf, gfpost.buf, F1.bufs[kc]], writes=[F2.bufs[kc]])
        dump("yT", F2.t[:, :, :], [128, 16, WX], F2.bufs)
        t0 = 0
        while t0 < TO:
            n = min(128, TO - t0)
            xb = xt[st["xi"] % 2]
            st["xi"] += 1
            for k4 in range(4):
                tp = tpf[k4 % 2]
                for j in range(4):
                    kc = k4 * 4 + j
                    fw.op("pe", lambda e, kc=kc, j=j, tp=tp: e.transpose(tp.t[0:n, j, :], F2.t[:, kc, t0:t0 + n],
                                                                         identf.t[:, :]),
                          reads=[F2.bufs[kc], identf.buf], writes=[tp.buf])
                ce = "dve" if k4 % 2 == 0 else "act"
                if ce == "dve":
                    fw.op("dve", lambda e, k4=k4, tp=tp: e.tensor_copy(out=xb.t[0:n, k4 * 512:(k4 + 1) * 512].rearrange("p (j c) -> p j c", c=128),
                                                                       in_=tp.t[0:n, :, :]),
                          reads=[tp.buf], writes=[xb.buf], excl=(k4 == 0))
                else:
                    fw.op("act", lambda e, k4=k4, tp=tp: e.activation(out=xb.t[0:n, k4 * 512:(k4 + 1) * 512].rearrange("p (j c) -> p j c", c=128),
                                                                      in_=tp.t[0:n, :, :], func=AF.Copy),
                          reads=[tp.buf], writes=[xb.buf], excl=(k4 == 0))
            fw.dma("act", y[s, a + t0:a + t0 + n, :], xb.t[0:n, :], reads=[xb.buf], writes=[outb], excl=False)
            t0 += n

    for s in range(nseg):
        for ti in tiles:
            tile(s, ti)

    fw.wait_all("sp", [outb] + dbg_bufs)
    es.close()
    return nc, fw, dbg_out


def _wtile(W):
    K, M = W.shape
    KC, MC = K // 128, M // 128
    return np.ascontiguousarray(W.reshape(KC, 128, MC, 128).transpose(2, 1, 0, 3).reshape(MC, 128, KC * 128))


def _fm(v):
    return np.ascontiguousarray(v.reshape(-1, 128).T)


def prep_common(inp):
    f = lambda k: np.asarray(inp[k], dtype=np.float32)[0]
    w_in = f("w_in")
    com = {}
    com["wz"] = _wtile(w_in[:, 8192:12288])
    com["wglu"] = _wtile(w_in[:, 12416:16512])
    com["wgt"] = _wtile(w_in[:, 16512:20608])
    com["wdo"] = _wtile(f("w_delta_out"))
    com["wco"] = _wtile(f("w_conf_out"))
    com["wmo"] = _wtile(f("w_mix_out"))
    com["wup"] = _wtile(f("w_up"))
    wd = _wtile(f("w_down"))
    com["wdn"] = np.ascontiguousarray(wd.reshape(16, 128, 2, DFF // 2).transpose(0, 2, 1, 3).reshape(32, 128, DFF // 2))
    vecs = np.zeros((128, 16, 8), np.float32)
    for i, k in enumerate(("mix_norm_pre", "mix_norm_post", "ffn_norm_pre", "ffn_norm_post", "conf_conv_b",
                           "conf_ln_w", "conf_ln_b")):
        vecs[:, :, i] = _fm(f(k))
    com["vecs"] = vecs
    cw = f("conf_conv_w")
    com["cconv"] = np.ascontiguousarray(cw.T.reshape(16, 128, 31).transpose(1, 0, 2))
    fc = f("ffn_conv_w")
    com["fconv"] = np.ascontiguousarray(fc.T.reshape(88, 128, 3).transpose(1, 0, 2))
    com["dnw"] = np.ascontiguousarray(f("delta_norm_w").reshape(128, 1))
    com["ident"] = np.eye(128, dtype=np.float32)
    return com


def prep_core(inp, c, com):
    xs = [np.asarray(inp["x_prompt"], np.float32)[0], np.asarray(inp["x_sample"], np.float32)[0],
          np.asarray(inp["x_sample"], np.float32)[1]]
    xp = np.zeros((NSEQ, SEG + 2 * HALO, D), np.float32)
    for s in range(NSEQ):
        lo, hi = c * SEG - HALO, (c + 1) * SEG + HALO
        l2, h2 = max(lo, 0), min(hi, SEQ)
        xp[s, l2 - lo:h2 - lo] = xs[s][l2:h2]
    msk = np.ones((NSEQ * NT, 128, WX), np.float32)
    for s in range(NSEQ):
        for ti in range(NT):
            a = min(TO * ti, SEG - TO)
            g = c * SEG + a - 1 + np.arange(WX)
            msk[s * NT + ti, :, :] = ((g >= 0) & (g < SEQ)).astype(np.float32)[None, :]
    d = dict(com)
    d["xp"] = xp
    d["msk"] = msk
    return d


WIN = 256
NWIN = SEQ // WIN
NBLK = SEQ // 128
L2_EPS = 1e-6


def build_p1(cfg):
    nseq = cfg.get("nseq", NSEQ)
    nwin = cfg.get("nwin", NWIN)
    dbg = cfg.get("dbg", [])
    do_scan = cfg.get("scan", True)
    nc = bass.Bass("TRN2", target_bir_lowering=False)
    from contextlib import ExitStack
    es = ExitStack()
    nblk = nwin * 2

    def din(name, shape, dt=F32):
        return nc.dram_tensor(name, list(shape), dt, kind="ExternalInput")

    def dint(name, shape, dt):
        return nc.dram_tensor(name, list(shape), dt, kind="Internal")

    def dout(name, shape, dt=F32):
        return nc.dram_tensor(name, list(shape), dt, kind="ExternalOutput")

    xq = din("xq", [NSEQ, SEQ + 4, D])
    wqkv_f = din("wqkv_in", [8, 128, D])
    wab_f = din("wab_in", [128, 16, 16])
    gpre_d = din("gpre_in", [128, 16])
    scw_d = din("scw_in", [128, 8, 5])
    abv_d = din("abv_in", [16, 2])
    dnw_d = din("dnwb", [128, 128])
    ident_d = din("ident", [128, 128])
    masks_d = din("masks_in", [128, 8, 128])
    ind_d = din("ind_in", [128, 2, 128])
    esel_d = din("esel_in", [4, 8, 128])
    oT = dout("oT", [512, NSEQ, SEQ + 2], BF16)

    base = dint("base", [nseq, nblk, 128, 10, 128], F32)
    gbd = dint("gbd", [nseq, nblk, 128, 16], F32)
    ofw = dint("ofw", [nseq, nblk, 128, 4, 128], F32)

    dbg_out = {}
    dbg_bufs = []
    outb = Buf("out")
    sems = [es.enter_context(nc.semaphore(f"s{i}")) for i in range(100)]
    fw = FW(nc, sems)

    def sb(name, shape, dt, nchunk=1):
        return T(es.enter_context(nc.sbuf_tensor(name, list(shape), dt)), nchunk, name)

    def ps(name, shape, dt):
        t_ = T(es.enter_context(nc.psum_tensor(name, list(shape), dt)), 1, name)
        t_.bufs[0].psum = True
        return t_

    def dump(name, ap, shape, bufs, dt=F32):
        if name not in dbg:
            return
        if name not in dbg_out:
            dbg_out[name] = dout("dbg_" + name, shape, dt)
        b = Buf("dbg")
        dbg_bufs.append(b)
        fw.dma("sp", dbg_out[name][tuple(slice(None) for _ in shape)], ap, reads=bufs, writes=[b])

    identf = sb("identf", [128, 128], F32)
    identb = sb("identb", [128, 128], BF16)
    onesf = sb("onesf", [128, 128], F32)
    gpre = sb("gpre", [128, 16], F32)
    scw = sb("scw", [128, 8, 5], F32)
    abv = sb("abv", [16, 4], F32)
    dnw = sb("dnw", [128, 128], F32)
    masks = sb("masks", [128, 8, 128], F32)
    ind = sb("ind", [128, 2, 128], F32)
    esel = sb("esel", [4, 8, 128], F32)
    wqkv = sb("wqkv", [128, 8, D], BF16)
    wab = sb("wab", [128, 16, 16], BF16)
    for t_, d_ in ((identf, ident_d), (gpre, gpre_d), (dnw, dnw_d)):
        fw.dma("sp", t_.t[:, :], d_[:, :], writes=[t_.buf])
    fw.dma("sp", abv.t[:, 0:2], abv_d[:, :], writes=[abv.buf])
    for t_, d_ in ((scw, scw_d), (masks, masks_d), (ind, ind_d), (esel, esel_d)):
        fw.dma("sp", t_.t[:, :, :], d_[:, :, :], writes=[t_.buf])
    for mc in range(8):
        fw.dma("pool", wqkv.t[:, mc, :], wqkv_f[mc, :, :], writes=[wqkv.buf], excl=False)
    fw.dma("pool", wab.t[:, :, :], wab_f[:, :, :], writes=[wab.buf])
    fw.op("dve", lambda e: e.tensor_copy(out=identb.t[:, :], in_=identf.t[:, :]), reads=[identf.buf], writes=[identb.buf])
    fw.op("dve", lambda e: e.memset(onesf.t[:, :], 1.0), writes=[onesf.buf])
    fw.op("act", lambda e: e.activation(out=abv.t[:, 2:3], in_=abv.t[:, 1:2], func=AF.Exp), reads=[abv.buf], writes=[abv.buf])
    fw.op("dve", lambda e: e.tensor_scalar(out=abv.t[:, 2:3], in0=abv.t[:, 2:3], scalar1=-1.0, scalar2=None, op0=ALU.mult),
          reads=[abv.buf], writes=[abv.buf])

    WW = WIN + 4
    xt = [sb(f"xt{i}", [128, D], F32) for i in range(2)]
    xn = sb("xn", [128, D], BF16)
    ss = sb("ss", [128, 4], F32)
    hT = sb("hT", [128, 16, WW], BF16, 16)
    pre = sb("pre", [128, 8, WW], F32, 8)
    qkvc = [sb(f"qkvc{i}", [128, 8, WIN], F32, 8) for i in range(2)]
    tmpw = [sb(f"tw{i}", [128, WIN], F32) for i in range(2)]
    rows = sb("rows", [16, 2, WIN], F32, 2)
    stg = [sb(f"stg{i}", [128, 6, 128], F32) for i in range(2)]
    gbt = [sb(f"gbt{i}", [128, 16], F32) for i in range(2)]
    mm = [ps(f"mm{i}", [128, 512], F32) for i in range(3)]
    tpb = ps("tpb", [128, 8, 128], BF16)
    tpf = [ps(f"tpf{i}", [128, 512], F32) for i in range(2)]
    stp = ps("stp", [128, 512], F32)
    rwp = ps("rwp", [128, 512], F32)
    st = {"xi": 0, "pi": 0, "qi": 0, "ti": 0, "si": 0}
    base_bufs = [[Buf(f"base{s}_{b}") for b in range(nblk)] for s in range(nseq)]
    gb_bufs = [[Buf(f"gb{s}_{b}") for b in range(nblk)] for s in range(nseq)]

    def window(s, w):
        t0 = w * WIN
        r0 = 0
        while r0 < WW:
            n = min(128, WW - r0)
            xb = xt[st["xi"] % 2]
            st["xi"] += 1
            fw.dma("sp", xb.t[0:n, :], xq[s, t0 + r0:t0 + r0 + n, :], writes=[xb.buf])
            fw.op("dve", lambda e: e.memset(ss.t[:, 0:1], 0.0), writes=[ss.buf])
            fw.op("act", lambda e: e.activation(out=xn.t[0:n, :], in_=xb.t[0:n, :], func=AF.Square, accum_out=ss.t[0:n, 0:1]),
                  reads=[xb.buf], writes=[xn.buf, ss.buf])
            fw.op("act", lambda e: e.activation(out=ss.t[0:n, 1:2], in_=ss.t[0:n, 0:1], func=AF.Sqrt, bias=RMS_EPS, scale=1.0 / D),
                  reads=[ss.buf], writes=[ss.buf])
            fw.op("dve", lambda e: e.reciprocal(out=ss.t[0:n, 1:2], in_=ss.t[0:n, 1:2]), reads=[ss.buf], writes=[ss.buf])
            fw.op("dve", lambda e: e.tensor_scalar(out=xn.t[0:n, :], in0=xb.t[0:n, :], scalar1=ss.t[0:n, 1:2], scalar2=None,
                                                   op0=ALU.mult), reads=[xb.buf, ss.buf], writes=[xn.buf])
            for k4 in range(4):
                for j in range(4):
                    kc = k4 * 4 + j
                    fw.op("pe", lambda e, kc=kc, j=j: e.transpose(tpb.t[:, j, 0:n], xn.t[0:n, kc * 128:(kc + 1) * 128],
                                                                  identb.t[0:n, 0:n]),
                          reads=[xn.buf, identb.buf], writes=[tpb.buf])
                for j in range(4):
                    kc = k4 * 4 + j
                    fw.op("act", lambda e, kc=kc, j=j: e.activation(out=hT.t[:, kc, r0:r0 + n], in_=tpb.t[:, j, 0:n],
                                                                    func=AF.Copy, scale=gpre.t[:, kc:kc + 1]),
                          reads=[tpb.buf, gpre.buf], writes=[hT.bufs[kc]], excl=(r0 == 0))
            r0 += n
        for mc in range(8):
            pt = mm[st["pi"] % 3]
            st["pi"] += 1
            for kc in range(16):
                fw.op("pe", lambda e, kc=kc, mc=mc: e.matmul(pt.t[:, 0:WW], lhsT=wqkv.t[:, mc, kc * 128:(kc + 1) * 128],
                                                             rhs=hT.t[:, kc, 0:WW], start=(kc == 0), stop=(kc == 15)),
                      reads=[wqkv.buf, hT.bufs[kc]], writes=[pt.buf])
            fw.op("act", lambda e, mc=mc: e.activation(out=pre.t[:, mc, 0:WW], in_=pt.t[:, 0:WW], func=AF.Copy),
                  reads=[pt.buf], writes=[pre.bufs[mc]])
        for kc in range(16):
            fw.op("pe", lambda e, kc=kc: e.matmul(rwp.t[0:16, 0:WIN], lhsT=wab.t[:, kc, :], rhs=hT.t[:, kc, 2:2 + WIN],
                                                  start=(kc == 0), stop=(kc == 15)),
                  reads=[wab.buf, hT.bufs[kc]], writes=[rwp.buf])
        fw.op("act", lambda e: e.activation(out=rows.t[:, 0, :], in_=rwp.t[0:16, 0:WIN], func=AF.Exp, bias=abv.t[:, 0:1], scale=1.0),
              reads=[rwp.buf, abv.buf], writes=[rows.bufs[0]])
        fw.op("act", lambda e: e.activation(out=rows.t[:, 0, :], in_=rows.t[:, 0, :], func=AF.Ln, bias=1.0, scale=1.0),
              reads=[rows.bufs[0]], writes=[rows.bufs[0]])
        fw.op("dve", lambda e: e.tensor_scalar(out=rows.t[:, 0, :], in0=rows.t[:, 0, :], scalar1=abv.t[:, 2:3], scalar2=None,
                                               op0=ALU.mult), reads=[rows.bufs[0], abv.buf], writes=[rows.bufs[0]])
        fw.op("act", lambda e: e.activation(out=rows.t[:, 1, :], in_=rwp.t[0:16, 0:WIN], func=AF.Sigmoid),
              reads=[rwp.buf], writes=[rows.bufs[1]])
        qc = qkvc[st["qi"] % 2]
        st["qi"] += 1
        for mc in range(8):
            tw = tmpw[st["ti"] % 2]
            st["ti"] += 1
            fw.op("dve", lambda e, mc=mc, tw=tw: e.tensor_scalar(out=tw.t[:, 0:WIN], in0=pre.t[:, mc, 0:WIN], scalar1=scw.t[:, mc, 0:1],
                                                                scalar2=None, op0=ALU.mult),
                  reads=[pre.bufs[mc], scw.buf], writes=[tw.buf])
            for j in range(1, 5):
                fw.op("dve", lambda e, mc=mc, j=j, tw=tw: e.scalar_tensor_tensor(out=tw.t[:, 0:WIN], in0=pre.t[:, mc, j:j + WIN],
                                                                                 scalar=scw.t[:, mc, j:j + 1], in1=tw.t[:, 0:WIN],
                                                                                 op0=ALU.mult, op1=ALU.add),
                      reads=[pre.bufs[mc], scw.buf], writes=[tw.buf])
            fw.op("act", lambda e, mc=mc, tw=tw: e.activation(out=qc.t[:, mc, :], in_=tw.t[:, 0:WIN], func=AF.Silu),
                  reads=[tw.buf], writes=[qc.bufs[mc]])
        for mc in range(4):
            tw = tmpw[st["ti"] % 2]
            st["ti"] += 1
            fw.op("act", lambda e, mc=mc, tw=tw: e.activation(out=tw.t[:, 0:WIN], in_=qc.t[:, mc, :], func=AF.Square),
                  reads=[qc.bufs[mc]], writes=[tw.buf])
            fw.op("pe", lambda e, tw=tw: e.matmul(stp.t[:, 0:WIN], lhsT=onesf.t[:, :], rhs=tw.t[:, 0:WIN], start=True, stop=True),
                  reads=[onesf.buf, tw.buf], writes=[stp.buf])
            fw.op("act", lambda e, tw=tw: e.activation(out=tw.t[:, 0:WIN], in_=stp.t[:, 0:WIN], func=AF.Sqrt, bias=L2_EPS, scale=1.0),
                  reads=[stp.buf], writes=[tw.buf])
            fw.op("dve", lambda e, tw=tw: e.reciprocal(out=tw.t[:, 0:WIN], in_=tw.t[:, 0:WIN]), reads=[tw.buf], writes=[tw.buf])
            sc = (128.0 ** -0.5) if mc < 2 else 1.0
            fw.op("dve", lambda e, mc=mc, tw=tw, sc=sc: e.scalar_tensor_tensor(out=qc.t[:, mc, :], in0=qc.t[:, mc, :], scalar=sc,
                                                                               in1=tw.t[:, 0:WIN], op0=ALU.mult, op1=ALU.mult),
                  reads=[tw.buf], writes=[qc.bufs[mc]])
        dump("qkvc", qc.t[:, :, :], [128, 8, WIN], qc.bufs)
        dump("rows", rows.t[:, :, :], [16, 2, WIN], rows.bufs)
        for hb in range(2):
            blk = w * 2 + hb
            sg = stg[st["si"] % 2]
            gt_ = gbt[st["si"] % 2]
            st["si"] += 1
            c0 = hb * 128
            tp = tpf[0]
            for i, mc in enumerate((2, 3, 4, 5)):
                fw.op("pe", lambda e, i=i, mc=mc: e.transpose(tp.t[:, i * 128:(i + 1) * 128], qc.t[:, mc, c0:c0 + 128], identf.t[:, :]),
                      reads=[qc.bufs[mc], identf.buf], writes=[tp.buf])
            fw.op("act", lambda e: e.activation(out=sg.t[:, 0:4, :], in_=tp.t[:, 0:512].rearrange("p (a b) -> p a b", b=128), func=AF.Copy), reads=[tp.buf], writes=[sg.buf])
            tp2 = tpf[1]
            for i, mc in enumerate((6, 7)):
                fw.op("pe", lambda e, i=i, mc=mc: e.transpose(tp2.t[:, i * 128:(i + 1) * 128], qc.t[:, mc, c0:c0 + 128], identf.t[:, :]),
                      reads=[qc.bufs[mc], identf.buf], writes=[tp2.buf])
            for i in range(2):
                fw.op("pe", lambda e, i=i: e.transpose(tp2.t[:, 256 + i * 128:256 + i * 128 + 16], rows.t[:, i, c0:c0 + 128], identf.t[0:16, 0:16]),
                      reads=[rows.bufs[i], identf.buf], writes=[tp2.buf])
            fw.op("dve", lambda e: e.tensor_copy(out=sg.t[:, 4:6, :], in_=tp2.t[:, 0:256].rearrange("p (a b) -> p a b", b=128)), reads=[tp2.buf], writes=[sg.buf], excl=False)
            fw.op("dve", lambda e: e.tensor_copy(out=gt_.t[:, 0:8], in_=tp2.t[:, 256:264]), reads=[tp2.buf], writes=[gt_.buf])
            fw.op("dve", lambda e: e.tensor_copy(out=gt_.t[:, 8:16], in_=tp2.t[:, 384 + 8:384 + 16]), reads=[tp2.buf], writes=[gt_.buf], excl=False)
            fw.dma("act", base[s, blk, :, 0:4, :], qc.t[:, 0:4, c0:c0 + 128], reads=qc.bufs[0:4], writes=[base_bufs[s][blk]],
                   owner=base_bufs[s][0], excl=False)
            fw.dma("act", base[s, blk, :, 4:10, :], sg.t[:, :, :], reads=[sg.buf], writes=[base_bufs[s][blk]],
                   owner=base_bufs[s][0], excl=False)
            fw.dma("act", gbd[s, blk, :, :], gt_.t[:, :], reads=[gt_.buf], writes=[gb_bufs[s][blk]], owner=base_bufs[s][0])
            if "stg" in dbg:
                dump("stg", sg.t[:, :, :], [128, 6, 128], [sg.buf])
                dump("gbt", gt_.t[:, :], [128, 16], [gt_.buf])

    for s in range(nseq):
        for w in range(nwin):
            window(s, w)

    if do_scan:
        fw.barrier()
        _scan_pass(nc, fw, es, locals())

    fw.wait_all("sp", [outb] + dbg_bufs)
    fw.wait_all("act", [outb] + dbg_bufs)
    es.close()
    return nc, fw, dbg_out


def prep_p1_common(inp):
    f = lambda k: np.asarray(inp[k], dtype=np.float32)[0]
    com = {}
    com["gpre_in"] = _fm(f("mix_norm_pre"))
    com["dnwb"] = np.ascontiguousarray(np.broadcast_to(f("delta_norm_w")[None, :], (128, 128))).astype(np.float32)
    com["ident"] = np.eye(128, dtype=np.float32)
    t = np.arange(128)
    same = (t[:, None] // 64) == (t[None, :] // 64)
    s_, c_ = t[:, None], t[None, :]
    m = np.zeros((128, 8, 128), np.float32)
    m[:, 0] = (same & (c_ >= s_))
    m[:, 1] = (same & (c_ <= s_))
    m[:, 2] = (same & (c_ > s_))
    m[:, 3] = (same & (c_ < s_))
    m[:, 4] = np.where(m[:, 0] > 0, 0.0, -30000.0)
    m[:, 5] = np.where(m[:, 1] > 0, 0.0, -30000.0)
    m[:, 6] = same
    com["masks_in"] = m
    ind = np.zeros((128, 2, 128), np.float32)
    ind[0:64, 0, :] = 1.0
    ind[64:128, 1, :] = 1.0
    com["ind_in"] = ind
    es_ = np.zeros((4, 8, 128), np.float32)
    for j in range(4):
        es_[j, j, :] = 1.0
        es_[j, 4 + j, :] = -1.0
    com["esel_in"] = es_
    return com


def prep_p1_core(inp, c, com, xq):
    f = lambda k: np.asarray(inp[k], dtype=np.float32)[0]
    w_in = f("w_in")
    cols = np.concatenate([np.arange(256 * c, 256 * c + 256), 2048 + np.arange(256 * c, 256 * c + 256),
                           4096 + np.arange(512 * c, 512 * c + 512)])
    d = dict(com)
    d["wqkv_in"] = _wtile(w_in[:, cols])
    acols = np.concatenate([12288 + dd * 32 + 4 * c + np.arange(4) for dd in range(2)])
    bcols = acols + 64
    wab = w_in[:, np.concatenate([acols, bcols])]
    d["wab_in"] = np.ascontiguousarray(wab.reshape(16, 128, 16).transpose(1, 0, 2))
    scw = f("short_conv_w")[:, cols]
    d["scw_in"] = np.ascontiguousarray(scw.T.reshape(8, 128, 5).transpose(1, 0, 2))
    abv = np.zeros((16, 2), np.float32)
    abv[0:8, 0] = f("dt_bias")[:, 4 * c:4 * c + 4].reshape(-1)
    abv[0:8, 1] = f("a_log")[:, 4 * c:4 * c + 4].reshape(-1)
    d["abv_in"] = abv
    d["xq"] = xq
    return d


def prep_xq(inp):
    xs = [np.asarray(inp["x_prompt"], np.float32)[0], np.asarray(inp["x_sample"], np.float32)[0],
          np.asarray(inp["x_sample"], np.float32)[1]]
    xq = np.zeros((NSEQ, SEQ + 4, D), np.float32)
    for s in range(NSEQ):
        xq[s, 2:2 + SEQ] = xs[s]
    return xq


def _scan_pass(nc, fw, es, L):
    sb, ps, dump = L["sb"], L["ps"], L["dump"]
    masks, ind, esel, identf, identb, dnw = L["masks"], L["ind"], L["esel"], L["identf"], L["identb"], L["dnw"]
    base, gbd, ofw, oT = L["base"], L["gbd"], L["ofw"], L["oT"]
    base_bufs, gb_bufs, nseq, nblk, outb, dbg = L["base_bufs"], L["gb_bufs"], L["nseq"], L["nblk"], L["outb"], L["dbg"]
    mmp, stp, rwp, tpf, tpb = L["mm"], L["stp"], L["rwp"], L["tpf"], L["tpb"]
    lvl = float(L["cfg"].get("lvl", 4))

    class PS_:
        def __init__(self, ap, bank):
            self.ap = ap
            self.buf = bank.buf
    pG = PS_(mmp[0].t[:, 0:64], mmp[0])
    pGr = PS_(mmp[0].t[0:4, 128:256], mmp[0])
    pGram = [PS_(mmp[1].t[:, i * 128:(i + 1) * 128], mmp[1]) for i in range(4)]
    UB = [mmp[2], stp, rwp, tpf[0]]
    pDiff = [PS_(b_.t[:, 0:128], b_) for b_ in UB]
    pT = [PS_(b_.t[:, 128:256], b_) for b_ in UB]
    pPow = [PS_(b_.t[:, 0:256], b_) for b_ in UB]
    pApp = [PS_(b_.t[:, 256:512], b_) for b_ in UB]
    pS = [[PS_(b_.t[:, k * 128:(k + 1) * 128], b_) for k in range(4)] for b_ in UB]

    def run_rr(gens):
        gens = list(gens)
        while gens:
            for g_ in list(gens):
                try:
                    next(g_)
                except StopIteration:
                    gens.remove(g_)
    pO = PS_(tpb.t[:, 0:4, :], tpb)

    T32 = [sb(f"T32_{i}", [128, 10, 128], F32) for i in range(2)]
    QB = [sb(f"QB{i}", [128, 2, 128], BF16) for i in range(2)]
    GB = [sb(f"GB{i}", [128, 16], F32) for i in range(2)]
    sm = [sb(f"sm{i}", [128, 64], F32) for i in range(2)]
    sm2 = [sb(f"sm2_{i}", [128, 16], F32) for i in range(2)]
    Grs = [sb(f"Grs{i}", [4, 128], F32) for i in range(2)]
    KKs = [sb(f"KKs{i}", [128, 2, 128], F32, 2) for i in range(2)]
    KQs = [sb(f"KQs{i}", [128, 2, 128], F32, 2) for i in range(2)]

    class Unit:
        pass
    units = []
    for k in range(8):
        U = Unit()
        U.Dm = sb(f"uDm{k}", [128, 128], F32)
        U.Y = [sb(f"uY{k}_{i}", [128, 128], BF16) for i in range(2)]
        U.W = sb(f"uW{k}", [128, 512], BF16)
        U.At = sb(f"uAt{k}", [128, 128], BF16)
        U.u = sb(f"uu{k}", [128, 128], F32)
        U.w = sb(f"uw{k}", [128, 128], BF16)
        U.wT = sb(f"uwT{k}", [128, 128], BF16)
        U.kd = sb(f"ukd{k}", [128, 128], BF16)
        U.vn = sb(f"uvn{k}", [128, 128], BF16)
        U.p3s = sb(f"up3{k}", [128, 128], F32)
        units.append(U)
    S = [sb(f"S{j}", [128, 128], F32) for j in range(4)]
    Sb = [sb(f"Sb{j}", [128, 128], BF16) for j in range(4)]
    oacc = [sb(f"oacc{i}", [128, 4, 128], F32, 4) for i in range(2)]
    of32 = [sb(f"of32_{i}", [128, 4, 128], F32) for i in range(2)]
    onb = sb("onb", [128, 4, 128], BF16)
    ostg = [sb(f"ostg{i}", [128, 4, 130], BF16) for i in range(2)]
    ssq = sb("ssq", [128, 8], F32)
    ofw_bufs = [[Buf(f"ofw{s}_{b}") for b in range(nblk)] for s in range(nseq)]
    ofw_own = Buf("ofw_own")
    ld_own = [Buf("ld0"), Buf("ld1")]

    def mmf(pt, lhsT, rhs, reads, start=True, stop=True):
        fw.op("pe", lambda e: e.matmul(pt.ap, lhsT=lhsT, rhs=rhs, start=start, stop=stop), reads=reads, writes=[pt.buf])

    zf = sb("zf", [128, 4, 2], F32)
    fw.op("dve", lambda e: e.memset(zf.t[:, :, :], 0.0), writes=[zf.buf])
    for og0 in ostg:
        for cc in (0, 129):
            fw.op("dve", lambda e, og0=og0, cc=cc: e.tensor_copy(out=og0.t[:, :, cc:cc + 1], in_=zf.t[:, :, 0:1]), reads=[zf.buf],
                  writes=[og0.buf], excl=False)
    cnt = {"u": 0, "e": 0}
    for s in range(nseq):
        for dirn in (0, 1):
            for j in range(4):
                fw.op("dve", lambda e, j=j: e.memset(S[j].t[:, :], 0.0), writes=[S[j].buf])
                fw.op("dve", lambda e, j=j: e.tensor_copy(out=Sb[j].t[:, :], in_=S[j].t[:, :]), reads=[S[j].buf], writes=[Sb[j].buf])
            order = list(range(nblk)) if dirn == 0 else list(reversed(range(nblk)))
            for bi, blk in enumerate(order):
                par = bi % 2
                t32, qb, gb = T32[par], QB[par], GB[par]
                fw.dma("sp", t32.t[:, :, :], base[s, blk, :, :, :], reads=[base_bufs[s][blk]], writes=[t32.buf], owner=ld_own[par])
                fw.dma("pool", qb.t[:, :, :], base[s, blk, :, 0:2, :], reads=[base_bufs[s][blk]], writes=[qb.buf])
                fw.dma("sp", gb.t[:, :], gbd[s, blk, :, :], reads=[gb_bufs[s][blk]], writes=[gb.buf])
                g4 = gb.t[:, dirn * 4:dirn * 4 + 4]
                b4 = gb.t[:, 8 + dirn * 4:8 + dirn * 4 + 4]
                Mi, Ms, Mn, same = masks.t[:, dirn, :], masks.t[:, 2 + dirn, :], masks.t[:, 4 + dirn, :], masks.t[:, 6, :]
                a_, a2, gr = sm[par], sm2[par], Grs[par]
                gall = gb.t[:, 0:16]
                d4 = dirn * 4
                if lvl <= 0:
                    continue
                fw.op("pe", lambda e: e.matmul(mmp[0].t[:, 0:16], lhsT=Mi, rhs=gall, start=True, stop=True),
                      reads=[masks.buf, gb.buf], writes=[pG.buf])
                fw.op("pe", lambda e: e.matmul(mmp[0].t[:, 16:32], lhsT=same, rhs=gall, start=True, stop=True),
                      reads=[masks.buf, gb.buf], writes=[pG.buf])
                for h in range(2):
                    fw.op("pe", lambda e, h=h: e.matmul(mmp[0].t[:, 32 + 16 * h:48 + 16 * h], lhsT=ind.t[:, h, :], rhs=gall, start=True, stop=True),
                          reads=[ind.buf, gb.buf], writes=[pG.buf])
                mmf(pGr, g4, Mi, [masks.buf, gb.buf])
                fw.op("act", lambda e: e.activation(out=a_.t[:, :], in_=pG.ap, func=AF.Copy), reads=[pG.buf], writes=[a_.buf])
                fw.op("act", lambda e: e.activation(out=a2.t[:, 0:4], in_=a_.t[:, d4:d4 + 4], func=AF.Exp), reads=[a_.buf], writes=[a2.buf])
                fw.op("dve", lambda e: e.tensor_tensor(out=a2.t[:, 4:8], in0=a_.t[:, 16 + d4:20 + d4], in1=a_.t[:, d4:d4 + 4], op=ALU.subtract),
                      reads=[a_.buf], writes=[a2.buf], excl=False)
                fw.op("act", lambda e: e.activation(out=a2.t[:, 4:8], in_=a2.t[:, 4:8], func=AF.Exp), reads=[a2.buf], writes=[a2.buf])
                for h in range(2):
                    fw.op("act", lambda e, h=h: e.activation(out=a2.t[:, 8 + 4 * h:12 + 4 * h], in_=a_.t[:, 32 + 16 * h + d4:36 + 16 * h + d4], func=AF.Exp),
                          reads=[a_.buf], writes=[a2.buf], excl=False)
                fw.op("dve", lambda e: e.tensor_copy(out=gr.t[:, :], in_=pGr.ap), reads=[pGr.buf], writes=[gr.buf])
                if lvl <= 0.5:
                    continue
                Gam = lambda j: a2.t[:, j:j + 1]
                E2 = lambda j: a2.t[:, 4 + j:5 + j]
                gbc = lambda h, j: a2.t[:, 8 + 4 * h + j:9 + 4 * h + j]
                kks, kqs = KKs[par], KQs[par]
                for q in range(2):
                    mmf(pGram[2 * q], t32.t[:, 2 + q, :], t32.t[:, 2 + q, :], [t32.buf])
                    mmf(pGram[2 * q + 1], t32.t[:, 2 + q, :], t32.t[:, q, :], [t32.buf])
                    fw.op("dve", lambda e, q=q: e.tensor_tensor(out=kks.t[:, q, :], in0=pGram[2 * q].ap, in1=Ms, op=ALU.mult),
                          reads=[pGram[2 * q].buf, masks.buf], writes=[kks.bufs[q]])
                    fw.op("act", lambda e, q=q: e.activation(out=kqs.t[:, q, :], in_=pGram[2 * q + 1].ap, func=AF.Copy),
                          reads=[pGram[2 * q + 1].buf], writes=[kqs.bufs[q]])
                us = [units[par * 4 + j] for j in range(4)]
                oa = oacc[par]
                chunks = (0, 1) if dirn == 0 else (1, 0)

                def unit_gen(j, U, t32=t32, qb=qb, gb=gb, b4=b4, a2=a2, gr=gr, kks=kks, kqs=kqs, Mn=Mn, oa=oa, chunks=chunks,
                             Gam=Gam, E2=E2, gbc=gbc):
                    q = j // 2
                    bank = UB[j]
                    pd = PS_(bank.t[:, 0:128], bank)
                    pZ1 = PS_(tpb.t[:, 4 + j, :], tpb)
                    W = U.W
                    Rr = W.t[:, 128:384]
                    mmf(pd, esel.t[:, j, :], gr.t[:, :], [esel.buf, gr.buf], True, False)
                    mmf(pd, gr.t[:, :], esel.t[:, 4 + j, :], [esel.buf, gr.buf], False, True)
                    yield
                    fw.op("dve", lambda e: e.scalar_tensor_tensor(out=U.Dm.t[:, :], in0=pd.ap, scalar=0.0, in1=Mn, op0=ALU.min, op1=ALU.add),
                          reads=[pd.buf, masks.buf], writes=[U.Dm.buf])
                    yield
                    fw.op("act", lambda e: e.activation(out=U.Dm.t[:, :], in_=U.Dm.t[:, :], func=AF.Exp), reads=[U.Dm.buf], writes=[U.Dm.buf])
                    yield
                    Yt = U.Y
                    fw.op("dve", lambda e: e.scalar_tensor_tensor(out=Yt[0].t[:, :], in0=kks.t[:, q, :], scalar=b4[:, j:j + 1], in1=U.Dm.t[:, :],
                                                                  op0=ALU.mult, op1=ALU.mult),
                          reads=[kks.bufs[q], gb.buf, U.Dm.buf], writes=[Yt[0].buf])
                    fw.op("pool", lambda e: e.tensor_tensor(out=U.At.t[:, :], in0=kqs.t[:, q, :], in1=U.Dm.t[:, :], op=ALU.mult),
                          reads=[kqs.bufs[q], U.Dm.buf], writes=[U.At.buf])
                    fw.op("act", lambda e: e.activation(out=W.t[:, 128:256], in_=t32.t[:, 6 + j, :], func=AF.Copy), reads=[t32.buf], writes=[W.buf])
                    fw.op("act", lambda e: e.activation(out=W.t[:, 256:384], in_=t32.t[:, 4 + q, :], func=AF.Copy, scale=Gam(j)),
                          reads=[t32.buf, a2.buf], writes=[W.buf], excl=False)
                    fw.op("act", lambda e: e.activation(out=U.kd.t[:, :], in_=t32.t[:, 4 + q, :], func=AF.Copy, scale=E2(j)),
                          reads=[t32.buf, a2.buf], writes=[U.kd.buf])
                    yield
                    fw.op("pe", lambda e: e.transpose(pZ1.ap, Yt[0].t[:, :], identb.t[:, :]), reads=[Yt[0].buf, identb.buf], writes=[pZ1.buf])
                    yield
                    fw.op("act", lambda e: e.activation(out=W.t[:, 0:128], in_=pZ1.ap, func=AF.Copy), reads=[pZ1.buf], writes=[W.buf], excl=False)
                    yield
                    for lv in range(6):
                        Yc, Yn = Yt[lv % 2], Yt[(lv + 1) % 2]
                        even = (lv % 2 == 0)
                        if lv < 5:
                            rhs1 = W.t[:, 0:384] if even else W.t[:, 128:512]
                            zc = W.t[:, 0:128] if even else W.t[:, 384:512]
                            fw.op("pe", lambda e: e.matmul(bank.t[:, 0:384], lhsT=Yc.t[:, :], rhs=rhs1, start=True, stop=True),
                                  reads=[Yc.buf, W.buf], writes=[bank.buf])
                            fw.op("pe", lambda e: e.matmul(bank.t[:, 384:512], lhsT=zc, rhs=Yc.t[:, :], start=True, stop=True),
                                  reads=[Yc.buf, W.buf], writes=[bank.buf])
                            app = bank.t[:, 128:384] if even else bank.t[:, 0:256]
                            zn_ps = bank.t[:, 0:128] if even else bank.t[:, 256:384]
                            zn_sb = W.t[:, 384:512] if even else W.t[:, 0:128]
                        else:
                            fw.op("pe", lambda e: e.matmul(bank.t[:, 0:256], lhsT=Yc.t[:, :], rhs=Rr, start=True, stop=True),
                                  reads=[Yc.buf, W.buf], writes=[bank.buf])
                            app = bank.t[:, 0:256]
                        yield
                        if lv < 4:
                            fw.op("act", lambda e: e.activation(out=zn_sb, in_=zn_ps, func=AF.Copy), reads=[bank.buf], writes=[W.buf], excl=False)
                        if lv < 5:
                            fw.op("dve", lambda e: e.tensor_copy(out=Yn.t[:, :], in_=bank.t[:, 384:512]), reads=[bank.buf], writes=[Yn.buf])
                        fw.op("dve", lambda e: e.tensor_tensor(out=Rr, in0=Rr, in1=app, op=(ALU.subtract if lv == 0 else ALU.add)),
                              reads=[bank.buf], writes=[W.buf], excl=False)
                        yield
                    fw.op("dve", lambda e: e.tensor_scalar(out=U.u.t[:, :], in0=W.t[:, 128:256], scalar1=b4[:, j:j + 1], scalar2=None, op0=ALU.mult),
                          reads=[W.buf, gb.buf], writes=[U.u.buf])
                    fw.op("act", lambda e: e.activation(out=U.w.t[:, :], in_=W.t[:, 256:384], func=AF.Copy, scale=b4[:, j:j + 1]),
                          reads=[W.buf, gb.buf], writes=[U.w.buf])
                    yield
                    fw.op("pe", lambda e: e.transpose(pZ1.ap, U.w.t[:, :], identb.t[:, :]), reads=[U.w.buf, identb.buf], writes=[pZ1.buf])
                    yield
                    fw.op("act", lambda e: e.activation(out=U.wT.t[:, :], in_=pZ1.ap, func=AF.Copy), reads=[pZ1.buf], writes=[U.wT.buf])
                    yield
                    P = pS[j]
                    for h in chunks:
                        r = slice(64 * h, 64 * h + 64)
                        mmf(P[0], U.wT.t[:, :], Sb[j].t[:, :], [U.wT.buf, Sb[j].buf])
                        mmf(P[1], qb.t[:, q, :], Sb[j].t[:, :], [qb.buf, Sb[j].buf])
                        yield
                        fw.op("dve", lambda e: e.tensor_tensor(out=U.vn.t[r, :], in0=U.u.t[r, :], in1=P[0].ap[r, :], op=ALU.subtract),
                              reads=[U.u.buf, P[0].buf], writes=[U.vn.buf])
                        yield
                        mmf(P[2], U.At.t[r, :], U.vn.t[r, :], [U.At.buf, U.vn.buf])
                        mmf(P[3], U.kd.t[r, :], U.vn.t[r, :], [U.kd.buf, U.vn.buf])
                        yield
                        fw.op("act", lambda e: e.activation(out=U.p3s.t[r, :], in_=P[2].ap[r, :], func=AF.Copy),
                              reads=[P[2].buf], writes=[U.p3s.buf])
                        fw.op("dve", lambda e: e.scalar_tensor_tensor(out=S[j].t[:, :], in0=S[j].t[:, :], scalar=gbc(h, j), in1=P[3].ap,
                                                                      op0=ALU.mult, op1=ALU.add),
                              reads=[a2.buf, P[3].buf], writes=[S[j].buf])
                        yield
                        fw.op("act", lambda e: e.activation(out=Sb[j].t[:, :], in_=S[j].t[:, :], func=AF.Copy), reads=[S[j].buf], writes=[Sb[j].buf])
                        fw.op("dve", lambda e: e.scalar_tensor_tensor(out=oa.t[r, j, :], in0=P[1].ap[r, :], scalar=a2.t[r, j:j + 1],
                                                                      in1=U.p3s.t[r, :], op0=ALU.mult, op1=ALU.add),
                              reads=[P[1].buf, a2.buf, U.p3s.buf], writes=[oa.bufs[j]])
                        yield

                run_rr([unit_gen(j, us[j]) for j in range(4)])
                if dirn == 0:
                    fw.dma("act", ofw[s, blk, :, :, :], oa.t[:, :, :], reads=oa.bufs, writes=[ofw_bufs[s][blk]], owner=ofw_own)
                else:
                    of = of32[par]
                    fw.dma("sp", of.t[:, :, :], ofw[s, blk, :, :, :], reads=[ofw_bufs[s][blk]], writes=[of.buf])
                    fw.op("dve", lambda e: e.tensor_tensor(out=oa.t[:, :, :], in0=oa.t[:, :, :], in1=of.t[:, :, :], op=ALU.add),
                          reads=[of.buf], writes=oa.bufs)
                    if "osum" in dbg and blk == 0:
                        dump("osum", oa.t[:, :, :], [128, 4, 128], oa.bufs)
                    fw.op("dve", lambda e: e.memset(ssq.t[:, 0:4], 0.0), writes=[ssq.buf])
                    for j in range(4):
                        fw.op("act", lambda e, j=j: e.activation(out=of.t[:, j, :], in_=oa.t[:, j, :], func=AF.Square, accum_out=ssq.t[:, j:j + 1]),
                              reads=[oa.bufs[j]], writes=[of.buf, ssq.buf])
                    fw.op("act", lambda e: e.activation(out=ssq.t[:, 4:8], in_=ssq.t[:, 0:4], func=AF.Sqrt, bias=RMS_EPS, scale=1.0 / 128),
                          reads=[ssq.buf], writes=[ssq.buf])
                    fw.op("dve", lambda e: e.reciprocal(out=ssq.t[:, 4:8], in_=ssq.t[:, 4:8]), reads=[ssq.buf], writes=[ssq.buf])
                    for j in range(4):
                        fw.op("dve", lambda e, j=j: e.scalar_tensor_tensor(out=onb.t[:, j, :], in0=oa.t[:, j, :], scalar=ssq.t[:, 4 + j:5 + j],
                                                                           in1=dnw.t[:, :], op0=ALU.mult, op1=ALU.mult),
                              reads=[oa.bufs[j], ssq.buf, dnw.buf], writes=[onb.buf], excl=(j == 0))
                    for j in range(4):
                        fw.op("pe", lambda e, j=j: e.transpose(tpb.t[:, j, :], onb.t[:, j, :], identb.t[:, :]),
                              reads=[onb.buf, identb.buf], writes=[pO.buf])
                    og_ = ostg[par]
                    fw.op("act", lambda e: e.activation(out=og_.t[:, :, 1:129], in_=tpb.t[:, 0:4, :], func=AF.Copy), reads=[pO.buf], writes=[og_.buf])
                    c0 = 1 + blk * 128
                    lo_ = 0 if blk == 0 else 1
                    hi_ = 130 if blk == nblk - 1 else 129
                    fw.dma("act", oT[:, s, c0 - 1 + lo_:c0 - 1 + hi_].rearrange("(j p) t -> p j t", p=128), og_.t[:, :, lo_:hi_], reads=[og_.buf],
                           writes=[outb], excl=False)


def kernel(**inputs):
    import ml_dtypes
    inp = {k: np.asarray(v) for k, v in inputs.items()}
    cores = list(range(NCORE))
    com1 = prep_p1_common(inp)
    xq = prep_xq(inp)
    maps1 = [prep_p1_core(inp, c, com1, xq) for c in cores]
    nc1, _, _ = build_p1({})
    res1 = run_bass_kernel_spmd(nc1, maps1, core_ids=cores)
    o_full = np.concatenate([np.asarray(res1.results[c]["oT"]) for c in cores], axis=0)
    del maps1, xq, res1
    com2 = prep_common(inp)
    maps2 = []
    for c in cores:
        d = prep_core(inp, c, com2)
        d["o_ext"] = np.ascontiguousarray(o_full[:, :, c * SEG:c * SEG + SEG + 2])
        maps2.append(d)
    nc2, _, _ = build({"ext_o": True})
    res2 = run_bass_kernel_spmd(nc2, maps2, core_ids=cores)
    ys = [np.asarray(res2.results[c]["y"]) for c in cores]
    full = np.concatenate(ys, axis=1)
    y_prompt = np.ascontiguousarray(full[0:1]).astype(np.float32)
    y_sample = np.ascontiguousarray(full[1:3]).astype(np.float32)
    return (y_prompt, y_sample)
```

```python
import math
import numpy as np
import concourse.bass as bass
import concourse.mybir as mybir
from concourse.bass_utils import run_bass_kernel_spmd

F32 = mybir.dt.float32
BF16 = mybir.dt.bfloat16
AF = mybir.ActivationFunctionType
ALU = mybir.AluOpType
AX = mybir.AxisListType

D = 2048
NCORE = 8
SEQ = 16384
NSEQ = 3
SEG = SEQ // NCORE
TO = 410
WX = TO + 2
WH = TO + 32
NT = 5
HALO = 16
DFF = 5632
RMS_EPS = 1e-6
LN_EPS = 1e-5


class Buf:
    __slots__ = ("name", "w", "r", "dsem", "dcnt", "psum")

    def __init__(self, name="", psum=False):
        self.name = name
        self.psum = psum
        self.w = {}
        self.r = {}
        self.dsem = None
        self.dcnt = 0


class FW:
    LIMIT = 30000
    SAME_ENGINE_SYNC = True

    def __init__(self, nc, sems):
        self.nc = nc
        self.free = list(sems)
        self.eng = {"pe": nc.tensor, "dve": nc.vector, "act": nc.scalar,
                    "pool": nc.gpsimd, "sp": nc.sync}
        self.csem = {}
        self.ccnt = {}
        for e in ("pe", "dve", "act", "pool"):
            self.csem[e] = self.free.pop()
            self.ccnt[e] = 0
        self.seen = {e: {} for e in self.eng}
        self.ninstr = 0
        self.alld = {}
        self.pend = {e: [] for e in ("pe", "dve", "act", "pool")}

    def _collect(self, reads, writes):
        need = {}
        for b in reads:
            for s, v in b.w.items():
                if need.get(s, 0) < v:
                    need[s] = v
        for b in writes:
            for s, v in b.w.items():
                if need.get(s, 0) < v:
                    need[s] = v
            for s, v in b.r.items():
                if need.get(s, 0) < v:
                    need[s] = v
        return need

    def _wait(self, e, need):
        seen = self.seen[e]
        own = self.csem.get(e)
        for s, v in need.items():
            if seen.get(s, 0) >= v:
                continue
            if s is own and (e == "pe" or not self.SAME_ENGINE_SYNC):
                continue
            self.eng[e].wait_ge(s, v)
            seen[s] = v
            self.ninstr += 1

    def _mark(self, tok, reads, writes, excl=True):
        s, v = tok
        for b in reads:
            if b.r.get(s, 0) < v:
                b.r[s] = v
        for b in writes:
            if excl:
                b.w = {s: v}
                b.r = {}
            else:
                if b.w.get(s, 0) < v:
                    b.w[s] = v

    def op(self, e, fn, reads=(), writes=(), excl=True, sig=True):
        for e2, pl in self.pend.items():
            if pl and e2 != e:
                raise RuntimeError(f"unsignalled ops pending on {e2} while emitting on {e}")
        pr = [b for b in reads if b.psum]
        if pr:
            reads = [b for b in reads if not b.psum]
            if excl:
                writes = list(writes) + pr
            else:
                self._wait(e, self._collect((), pr))
        self._wait(e, self._collect(reads, writes))
        ins = fn(self.eng[e])
        self.ninstr += 1
        if not sig:
            self.pend[e].append((list(reads), list(writes), excl, pr))
            return None
        if self.ccnt[e] >= self.LIMIT:
            self.csem[e] = self.free.pop()
            self.ccnt[e] = 0
        self.ccnt[e] += 1
        tok = (self.csem[e], self.ccnt[e])
        ins.then_inc(tok[0], 1)
        for (r_, w_, x_, p_) in self.pend[e]:
            self._mark(tok, r_, w_, x_)
            if p_ and not x_:
                self._mark(tok, (), p_, True)
        self.pend[e] = []
        self._mark(tok, reads, writes, excl)
        if pr and not excl:
            self._mark(tok, (), pr, True)
        return tok

    def dma(self, q, out, in_, reads=(), writes=(), owner=None, excl=True, **kw):
        for e2, pl in self.pend.items():
            if pl:
                raise RuntimeError(f"unsignalled ops pending on {e2} while emitting dma")
        self._wait(q, self._collect(reads, writes))
        ins = self.eng[q].dma_start(out=out, in_=in_, **kw)
        o = owner or (writes[0] if writes else reads[0])
        if o.dsem is None or o.dcnt + 16 > self.LIMIT:
            o.dsem = self.free.pop()
            o.dcnt = 0
        o.dcnt += 16
        tok = (o.dsem, o.dcnt)
        ins.then_inc(tok[0], 16)
        self.alld[tok[0]] = tok[1]
        self._mark(tok, reads, writes, excl)
        self.ninstr += 1
        return tok

    def barrier(self):
        need = dict(self.alld)
        for e in ("pe", "dve", "act", "pool"):
            if self.ccnt[e] > 0:
                need[self.csem[e]] = self.ccnt[e]
        for e in ("pe", "dve", "act", "pool", "sp"):
            seen = self.seen[e]
            for s_, v in need.items():
                if seen.get(s_, 0) >= v:
                    continue
                self.eng[e].wait_ge(s_, v)
                seen[s_] = v
                self.ninstr += 1

    def wait_all(self, e, bufs):
        self._wait(e, self._collect((), bufs))


class T:
    def __init__(self, t, nchunk=1, name=""):
        self.t = t
        self.bufs = [Buf(f"{name}{i}") for i in range(nchunk)]

    @property
    def buf(self):
        return self.bufs[0]


def build(cfg):
    nseg = cfg.get("nseg", NSEQ)
    tiles = cfg.get("tiles", list(range(NT)))
    dbg = cfg.get("dbg", [])
    nc = bass.Bass("TRN2", target_bir_lowering=False)
    from contextlib import ExitStack
    es = ExitStack()

    def din(name, shape, dt=F32):
        return nc.dram_tensor(name, list(shape), dt, kind="ExternalInput")

    def dint(name, shape, dt):
        return nc.dram_tensor(name, list(shape), dt, kind="Internal")

    def dout(name, shape, dt=F32):
        return nc.dram_tensor(name, list(shape), dt, kind="ExternalOutput")

    xp = din("xp", [NSEQ, SEG + 2 * HALO, D])
    msk = din("msk", [NSEQ * NT, 128, WX])
    wz_f = din("wz", [32, 128, D])
    wglu_f = din("wglu", [32, 128, D])
    wgt_f = din("wgt", [32, 128, D])
    wdo_f = din("wdo", [16, 128, 4096])
    wco_f = din("wco", [16, 128, D])
    wmo_f = din("wmo", [16, 128, D])
    wup_f = din("wup", [88, 128, D])
    wdn_f = din("wdn", [32, 128, DFF // 2])
    vecs = din("vecs", [128, 16, 8])
    cconv = din("cconv", [128, 16, 31])
    fconv = din("fconv", [128, 88, 3])
    dnw = din("dnw", [128, 1])
    ident_d = din("ident", [128, 128])
    o_ext = din("o_ext", [4096, NSEQ, SEG + 2], BF16) if cfg.get("ext_o") else None
    y = dout("y", [NSEQ, SEG, D])

    wz_b = dint("wz_b", [32, 128, D], BF16)
    wglu_b = dint("wglu_b", [32, 128, D], BF16)
    wgt_b = dint("wgt_b", [32, 128, D], BF16)
    wdo_b = dint("wdo_b", [16, 128, 4096], BF16)
    wco_b = dint("wco_b", [16, 128, D], BF16)
    wmo_b = dint("wmo_b", [16, 128, D], BF16)
    wup_b = dint("wup_b", [88, 128, D], BF16)
    wdn_b = dint("wdn_b", [32, 128, DFF // 2], BF16)

    dbg_out = {}
    dbg_bufs = []
    outb = Buf("out")
    o_buf_dram = Buf("o_dram")
    if o_ext is not None:
        o_src = lambda m, s_, c0: o_ext[m * 128:(m + 1) * 128, s_, c0:c0 + WX]

    sems = [es.enter_context(nc.semaphore(f"s{i}")) for i in range(100)]
    fw = FW(nc, sems)

    def sb(name, shape, dt, nchunk=1):
        return T(es.enter_context(nc.sbuf_tensor(name, list(shape), dt)), nchunk, name)

    def ps(name, shape, dt):
        t_ = T(es.enter_context(nc.psum_tensor(name, list(shape), dt)), 1, name)
        t_.bufs[0].psum = True
        return t_

    F1 = sb("F1", [128, 16, WX], F32, 16)
    F2 = sb("F2", [128, 16, WX], F32, 16)
    B1 = sb("B1", [128, 16, WH], BF16, 16)
    B2 = sb("B2", [128, 16, WH], BF16, 16)
    B3 = sb("B3", [128, 16, WX], BF16, 16)
    B4 = sb("B4", [128, 16, WX], BF16, 16)
    BF = sb("BF", [128, 44, WX], BF16, 44)
    xt = [sb(f"xt{i}", [128, D], F32) for i in range(2)]
    xn = sb("xn", [128, D], BF16)
    NWS = 3
    wsl = [sb(f"w{i}", [128, 4096], BF16) for i in range(NWS)]
    tmpA = [sb(f"tA{i}", [128, WH], F32) for i in range(3)]
    tmpB = [sb(f"tB{i}", [128, WX], F32) for i in range(3)]
    obuf = [sb(f"ob{i}", [128, WX], BF16) for i in range(2)]
    rowA = sb("rowA", [128, WH], F32)
    rowB = sb("rowB", [128, WH], F32)
    mskt = sb("mskt", [128, WX], F32)
    ss = sb("ss", [128, 4], F32)
    identf = sb("identf", [128, 128], F32)
    identb = sb("identb", [128, 128], BF16)
    onesf = sb("onesf", [128, 128], F32)
    vec = sb("vec", [128, 16, 8], F32)
    gpre = sb("gpre", [128, 16], F32)
    gpost = sb("gpost", [128, 16], F32)
    gffn = sb("gffn", [128, 16], F32)
    gfpost = sb("gfpost", [128, 16], F32)
    cbias = sb("cbias", [128, 16], F32)
    lnw = sb("lnw", [128, 16], F32)
    lnb = sb("lnb", [128, 16], F32)
    ccw = sb("ccw", [128, 16, 31], F32)
    fcw = sb("fcw", [128, 88, 3], F32)

    NPS = 4
    mmps = [ps(f"mm{i}", [128, 512], F32) for i in range(NPS)]
    tpb = ps("tpb", [128, 8, 128], BF16)
    tpf = [ps(f"tpf{i}", [128, 4, 128], F32) for i in range(2)]
    stp = ps("stp", [128, 512], F32)

    st = {"wi": 0, "pi": 0, "ti": 0, "xi": 0, "oi": 0, "fi": 0}

    fw.dma("sp", identf.t[:, :], ident_d[:, :], writes=[identf.buf])
    fw.dma("sp", vec.t[:, :, :], vecs[:, :, :], writes=[vec.buf])
    fw.dma("sp", ccw.t[:, :, :], cconv[:, :, :], writes=[ccw.buf])
    fw.dma("sp", fcw.t[:, :, :], fconv[:, :, :], writes=[fcw.buf])
    fw.op("dve", lambda e: e.tensor_copy(out=identb.t[:, :], in_=identf.t[:, :]),
          reads=[identf.buf], writes=[identb.buf])
    fw.op("dve", lambda e: e.memset(onesf.t[:, :], 1.0), writes=[onesf.buf])
    for i, gt_ in enumerate((gpre, gpost, gffn, gfpost, cbias, lnw, lnb)):
        fw.op("dve", lambda e, i=i, gt_=gt_: e.tensor_copy(out=gt_.t[:, :], in_=vec.t[:, :, i]),
              reads=[vec.buf], writes=[gt_.buf])

    wbufs = {}
    for name, src, dst, mcn in (("wglu", wglu_f, wglu_b, 32), ("wgt", wgt_f, wgt_b, 32), ("wz", wz_f, wz_b, 32),
                                ("wdo", wdo_f, wdo_b, 16), ("wco", wco_f, wco_b, 16), ("wmo", wmo_f, wmo_b, 16),
                                ("wup", wup_f, wup_b, 88), ("wdn", wdn_f, wdn_b, 32)):
        wb_ = Buf(name)
        wbufs[name] = [wb_] * mcn
        if cfg.get("skip_" + name):
            continue
        G = 4
        for m0 in range(0, mcn, G):
            fw.dma("pool", dst[m0:m0 + G, :, :], src[m0:m0 + G, :, :], writes=[wb_], excl=False)

    def dump(name, ap, shape, bufs, dt=F32):
        if name not in dbg:
            return
        if name not in dbg_out:
            dbg_out[name] = dout("dbg_" + name, shape, dt)
        b = Buf("dbg")
        dbg_bufs.append(b)
        fw.dma("sp", dbg_out[name][tuple(slice(None) for _ in shape)], ap, reads=bufs, writes=[b])

    def linear(wd, wname, order, KC, rhs, rbufs, N, epi, ksplit=1):
        KP = KC // ksplit
        for mc in order:
            pt = mmps[st["pi"] % NPS]
            st["pi"] += 1
            for part in range(ksplit):
                slot = wsl[st["wi"] % NWS]
                st["wi"] += 1
                wi = mc * ksplit + part
                fw.dma("sp", slot.t[:, 0:KP * 128], wd[wi, :, :], reads=[wbufs[wname][wi]], writes=[slot.buf])
                for k in range(KP):
                    kc = part * KP + k
                    fw.op("pe", lambda e, kc=kc, k=k, slot=slot: e.matmul(
                        pt.t[:, 0:N], lhsT=slot.t[:, k * 128:(k + 1) * 128], rhs=rhs(kc),
                        start=(kc == 0), stop=(kc == KC - 1)),
                          reads=[slot.buf, rbufs(kc)], writes=[pt.buf], sig=(k == KP - 1))
            epi(mc, pt)

    def bcast_sum(src_ap_fn, src_bufs, nch, N):
        for c in range(nch):
            fw.op("pe", lambda e, c=c: e.matmul(stp.t[:, 0:N], lhsT=onesf.t[:, :], rhs=src_ap_fn(c),
                                                start=(c == 0), stop=(c == nch - 1)),
                  reads=[onesf.buf, src_bufs(c)], writes=[stp.buf])

    def tile(s, ti):
        a = min(TO * ti, SEG - TO)
        r0 = 0
        while r0 < WH:
            n = min(128, WH - r0)
            xb = xt[st["xi"] % 2]
            st["xi"] += 1
            fw.dma("sp", xb.t[0:n, :], xp[s, a + r0:a + r0 + n, :], writes=[xb.buf])
            fw.op("dve", lambda e: e.memset(ss.t[:, 0:1], 0.0), writes=[ss.buf])
            fw.op("act", lambda e: e.activation(out=xn.t[0:n, :], in_=xb.t[0:n, :], func=AF.Square,
                                                accum_out=ss.t[0:n, 0:1]),
                  reads=[xb.buf], writes=[xn.buf, ss.buf])
            fw.op("act", lambda e: e.activation(out=ss.t[0:n, 1:2], in_=ss.t[0:n, 0:1], func=AF.Sqrt,
                                                bias=RMS_EPS, scale=1.0 / D),
                  reads=[ss.buf], writes=[ss.buf])
            fw.op("dve", lambda e: e.reciprocal(out=ss.t[0:n, 1:2], in_=ss.t[0:n, 1:2]),
                  reads=[ss.buf], writes=[ss.buf])
            fw.op("dve", lambda e: e.tensor_scalar(out=xn.t[0:n, :], in0=xb.t[0:n, :], scalar1=ss.t[0:n, 1:2],
                                                   scalar2=None, op0=ALU.mult),
                  reads=[xb.buf, ss.buf], writes=[xn.buf])
            for k4 in range(4):
                for j in range(4):
                    kc = k4 * 4 + j
                    fw.op("pe", lambda e, kc=kc, j=j: e.transpose(tpb.t[:, j, 0:n], xn.t[0:n, kc * 128:(kc + 1) * 128],
                                                                  identb.t[0:n, 0:n]),
                          reads=[xn.buf, identb.buf], writes=[tpb.buf], sig=(j == 3))
                for j in range(4):
                    kc = k4 * 4 + j
                    fw.op("act", lambda e, kc=kc, j=j: e.activation(out=B1.t[:, kc, r0:r0 + n], in_=tpb.t[:, j, 0:n],
                                                                    func=AF.Copy, scale=gpre.t[:, kc:kc + 1]),
                          reads=[tpb.buf, gpre.buf], writes=[B1.bufs[kc]], excl=(r0 == 0))
            lo = max(r0, HALO - 1)
            hi = min(r0 + n, HALO - 1 + WX)
            if hi > lo:
                for k4 in range(4):
                    tp = tpf[k4 % 2]
                    for j in range(4):
                        kc = k4 * 4 + j
                        fw.op("pe", lambda e, kc=kc, j=j, tp=tp: e.transpose(tp.t[:, j, 0:n],
                                                                             xb.t[0:n, kc * 128:(kc + 1) * 128],
                                                                             identf.t[0:n, 0:n]),
                              reads=[xb.buf, identf.buf], writes=[tp.buf], sig=(j == 3))
                    fw.op("dve", lambda e, k4=k4, tp=tp: e.tensor_copy(
                        out=F1.t[:, k4 * 4:k4 * 4 + 4, lo - (HALO - 1):hi - (HALO - 1)],
                        in_=tp.t[:, :, lo - r0:hi - r0]),
                          reads=[tp.buf], writes=F1.bufs[k4 * 4:k4 * 4 + 4], excl=(lo == HALO - 1))
            r0 += n
        dump("hT", B1.t[:, :, :], [128, 16, WH], B1.bufs, BF16)
        dump("xT", F1.t[:, :, :], [128, 16, WX], F1.bufs)

        hrhs = lambda kc: B1.t[:, kc, 0:WH]
        hbuf = lambda kc: B1.bufs[kc]
        gl_order = []
        for i in range(16):
            gl_order += [16 + i, i]
        sgt = {}

        def epi_glu(mc, pt):
            if mc >= 16:
                tb = tmpA[st["ti"] % 3]
                st["ti"] += 1
                fw.op("act", lambda e: e.activation(out=tb.t[:, 0:WH], in_=pt.t[:, 0:WH], func=AF.Sigmoid),
                      reads=[pt.buf], writes=[tb.buf])
                sgt["t"] = tb
            else:
                tb = sgt["t"]
                fw.op("dve", lambda e: e.tensor_tensor(out=B2.t[:, mc, 0:WH], in0=pt.t[:, 0:WH], in1=tb.t[:, 0:WH],
                                                       op=ALU.mult),
                      reads=[pt.buf, tb.buf], writes=[B2.bufs[mc]])
        linear(wglu_b, "wglu", gl_order, 16, hrhs, hbuf, WH, epi_glu)
        dump("hcv", B2.t[:, :, :], [128, 16, WH], B2.bufs, BF16)

        for kc in range(16):
            ce = "dve"
            fw.op(ce, lambda e, kc=kc: e.tensor_scalar(out=F2.t[:, kc, 0:WX], in0=B2.t[:, kc, 0:WX],
                                                       scalar1=ccw.t[:, kc, 0:1], scalar2=cbias.t[:, kc:kc + 1],
                                                       op0=ALU.mult, op1=ALU.add),
                  reads=[B2.bufs[kc], ccw.buf, cbias.buf], writes=[F2.bufs[kc]])
            for j in range(1, 31):
                fw.op(ce, lambda e, kc=kc, j=j: e.scalar_tensor_tensor(out=F2.t[:, kc, 0:WX], in0=B2.t[:, kc, j:j + WX],
                                                                       scalar=ccw.t[:, kc, j:j + 1], in1=F2.t[:, kc, 0:WX],
                                                                       op0=ALU.mult, op1=ALU.add),
                      reads=[B2.bufs[kc], ccw.buf], writes=[F2.bufs[kc]])
        dump("cv", F2.t[:, :, :], [128, 16, WX], F2.bufs)
        bcast_sum(lambda c: F2.t[:, c, 0:WX], lambda c: F2.bufs[c], 16, WX)
        fw.op("act", lambda e: e.activation(out=rowA.t[:, 0:WX], in_=stp.t[:, 0:WX], func=AF.Copy, scale=1.0 / D),
              reads=[stp.buf], writes=[rowA.buf])
        sqb = []
        for kc in range(16):
            tb = tmpA[st["ti"] % 3]
            st["ti"] += 1
            fw.op("act", lambda e, kc=kc, tb=tb: e.activation(out=tb.t[:, 0:WX], in_=F2.t[:, kc, 0:WX], func=AF.Square),
                  reads=[F2.bufs[kc]], writes=[tb.buf])
            fw.op("pe", lambda e, kc=kc, tb=tb: e.matmul(stp.t[:, 0:WX], lhsT=onesf.t[:, :], rhs=tb.t[:, 0:WX],
                                                         start=(kc == 0), stop=(kc == 15)),
                  reads=[onesf.buf, tb.buf], writes=[stp.buf])
        tb = tmpB[0]
        fw.op("dve", lambda e: e.tensor_tensor(out=tb.t[:, 0:WX], in0=rowA.t[:, 0:WX], in1=rowA.t[:, 0:WX], op=ALU.mult),
              reads=[rowA.buf], writes=[tb.buf])
        fw.op("dve", lambda e: e.scalar_tensor_tensor(out=rowB.t[:, 0:WX], in0=stp.t[:, 0:WX], scalar=1.0 / D,
                                                      in1=tb.t[:, 0:WX], op0=ALU.mult, op1=ALU.subtract),
              reads=[stp.buf, tb.buf], writes=[rowB.buf])
        fw.op("act", lambda e: e.activation(out=rowB.t[:, 0:WX], in_=rowB.t[:, 0:WX], func=AF.Sqrt, bias=LN_EPS, scale=1.0),
              reads=[rowB.buf], writes=[rowB.buf])
        fw.op("dve", lambda e: e.reciprocal(out=rowB.t[:, 0:WX], in_=rowB.t[:, 0:WX]), reads=[rowB.buf], writes=[rowB.buf])
        for kc in range(16):
            tb = tmpB[1 + kc % 2]
            ce = "dve"
            fw.op(ce, lambda e, kc=kc, tb=tb: e.tensor_tensor(out=tb.t[:, 0:WX], in0=F2.t[:, kc, 0:WX], in1=rowA.t[:, 0:WX],
                                                              op=ALU.subtract),
                  reads=[F2.bufs[kc], rowA.buf], writes=[tb.buf])
            fw.op(ce, lambda e, kc=kc, tb=tb: e.tensor_tensor(out=tb.t[:, 0:WX], in0=tb.t[:, 0:WX], in1=rowB.t[:, 0:WX],
                                                              op=ALU.mult),
                  reads=[tb.buf, rowB.buf], writes=[tb.buf])
            fw.op("act", lambda e, kc=kc, tb=tb: e.activation(out=B3.t[:, kc, 0:WX], in_=tb.t[:, 0:WX], func=AF.Silu,
                                                              bias=lnb.t[:, kc:kc + 1], scale=lnw.t[:, kc:kc + 1]),
                  reads=[tb.buf, lnb.buf, lnw.buf], writes=[B3.bufs[kc]])
        dump("cb", B3.t[:, :, :], [128, 16, WX], B3.bufs, BF16)

        xrhs = lambda kc: B1.t[:, kc, HALO - 1:HALO - 1 + WX]
        cbrhs = lambda kc: B3.t[:, kc, 0:WX]
        cbbuf = lambda kc: B3.bufs[kc]
        for mc in range(16):
            def epi_gb(m, pt):
                tb = tmpA[st["ti"] % 3]
                st["ti"] += 1
                fw.op("act", lambda e: e.activation(out=tb.t[:, 0:WX], in_=pt.t[:, 0:WX], func=AF.Sigmoid),
                      reads=[pt.buf], writes=[tb.buf])
                sgt["t"] = tb
            linear(wgt_b, "wgt", [16 + mc], 16, xrhs, hbuf, WX, epi_gb)

            def epi_yb(m, pt):
                tb = sgt["t"]
                fw.op("dve", lambda e: e.tensor_tensor(out=B2.t[:, m, 0:WX], in0=pt.t[:, 0:WX], in1=tb.t[:, 0:WX],
                                                       op=ALU.mult),
                      reads=[pt.buf, tb.buf], writes=[B2.bufs[m]])
            linear(wco_b, "wco", [mc], 16, cbrhs, cbbuf, WX, epi_yb)

        def epi_ga(m, pt):
            fw.op("act", lambda e: e.activation(out=B4.t[:, m, 0:WX], in_=pt.t[:, 0:WX], func=AF.Sigmoid),
                  reads=[pt.buf], writes=[B4.bufs[m]])
        linear(wgt_b, "wgt", list(range(16)), 16, xrhs, hbuf, WX, epi_ga)
        dump("gbyb", B2.t[:, :, :], [128, 16, WH], B2.bufs, BF16)
        dump("ga", B4.t[:, :, :], [128, 16, WX], B4.bufs, BF16)

        tok0 = a - 1 + 1
        def epi_z(m, pt):
            tb = tmpA[st["ti"] % 3]
            st["ti"] += 1
            ob = obuf[st["oi"] % 2]
            st["oi"] += 1
            fw.dma("sp", ob.t[:, 0:WX], o_src(m, s, tok0), reads=[o_buf_dram], writes=[ob.buf])
            fw.op("act", lambda e: e.activation(out=tb.t[:, 0:WX], in_=pt.t[:, 0:WX], func=AF.Silu),
                  reads=[pt.buf], writes=[tb.buf])
            fw.op("dve", lambda e: e.tensor_tensor(out=BF.t[:, m, 0:WX], in0=tb.t[:, 0:WX], in1=ob.t[:, 0:WX], op=ALU.mult),
                  reads=[tb.buf, ob.buf], writes=[BF.bufs[m]])
        linear(wz_b, "wz", list(range(32)), 16, xrhs, hbuf, WX, epi_z)
        dump("og", BF.t[:, 0:32, :], [128, 32, WX], BF.bufs[0:32], BF16)

        def epi_ya(m, pt):
            tb = tmpA[st["ti"] % 3]
            st["ti"] += 1
            fw.op("dve", lambda e: e.tensor_tensor(out=tb.t[:, 0:WX], in0=pt.t[:, 0:WX], in1=B4.t[:, m, 0:WX], op=ALU.mult),
                  reads=[pt.buf, B4.bufs[m]], writes=[tb.buf])
            fw.op("pool", lambda e: e.tensor_tensor(out=B3.t[:, m, 0:WX], in0=tb.t[:, 0:WX], in1=B2.t[:, m, 0:WX], op=ALU.add),
                  reads=[tb.buf, B2.bufs[m]], writes=[B3.bufs[m]])
        linear(wdo_b, "wdo", list(range(16)), 32, lambda kc: BF.t[:, kc, 0:WX], lambda kc: BF.bufs[kc], WX, epi_ya)
        dump("merged", B3.t[:, :, :], [128, 16, WX], B3.bufs, BF16)

        def epi_m2(m, pt):
            fw.op("act", lambda e: e.activation(out=F2.t[:, m, 0:WX], in_=pt.t[:, 0:WX], func=AF.Copy),
                  reads=[pt.buf], writes=[F2.bufs[m]])
        linear(wmo_b, "wmo", list(range(16)), 16, lambda kc: B3.t[:, kc, 0:WX], lambda kc: B3.bufs[kc], WX, epi_m2)

        def rms_row(src, n, dst_row):
            for kc in range(16):
                tb = tmpA[st["ti"] % 3]
                st["ti"] += 1
                fw.op("act", lambda e, kc=kc, tb=tb: e.activation(out=tb.t[:, 0:n], in_=src(kc), func=AF.Square),
                      reads=[srcb(src, kc)], writes=[tb.buf])
                fw.op("pe", lambda e, kc=kc, tb=tb: e.matmul(stp.t[:, 0:n], lhsT=onesf.t[:, :], rhs=tb.t[:, 0:n],
                                                             start=(kc == 0), stop=(kc == 15)),
                      reads=[onesf.buf, tb.buf], writes=[stp.buf])
            fw.op("act", lambda e: e.activation(out=dst_row.t[:, 0:n], in_=stp.t[:, 0:n], func=AF.Sqrt, bias=RMS_EPS,
                                                scale=1.0 / D),
                  reads=[stp.buf], writes=[dst_row.buf])
            fw.op("dve", lambda e: e.reciprocal(out=dst_row.t[:, 0:n], in_=dst_row.t[:, 0:n]),
                  reads=[dst_row.buf], writes=[dst_row.buf])

        srcmap = {}

        def srcb(src, kc):
            return srcmap[src][kc]
        m2src = lambda kc: F2.t[:, kc, 0:WX]
        srcmap[m2src] = F2.bufs
        rms_row(m2src, WX, rowA)
        for kc in range(16):
            tb = tmpB[kc % 3]
            ce = "dve"
            fw.op(ce, lambda e, kc=kc, tb=tb: e.tensor_tensor(out=tb.t[:, 0:WX], in0=F2.t[:, kc, 0:WX], in1=rowA.t[:, 0:WX],
                                                              op=ALU.mult),
                  reads=[F2.bufs[kc], rowA.buf], writes=[tb.buf])
            fw.op(ce, lambda e, kc=kc, tb=tb: e.scalar_tensor_tensor(out=F1.t[:, kc, 0:WX], in0=tb.t[:, 0:WX],
                                                                     scalar=gpost.t[:, kc:kc + 1], in1=F1.t[:, kc, 0:WX],
                                                                     op0=ALU.mult, op1=ALU.add),
                  reads=[tb.buf, gpost.buf], writes=[F1.bufs[kc]])
        dump("x1", F1.t[:, :, :], [128, 16, WX], F1.bufs)

        x1src = lambda kc: F1.t[:, kc, 0:WX]
        srcmap[x1src] = F1.bufs
        rms_row(x1src, WX, rowB)
        fw.dma("sp", mskt.t[:, :], msk[s * NT + ti, :, :], writes=[mskt.buf])
        fw.op("dve", lambda e: e.tensor_tensor(out=rowB.t[:, 0:WX], in0=rowB.t[:, 0:WX], in1=mskt.t[:, 0:WX], op=ALU.mult),
              reads=[mskt.buf], writes=[rowB.buf])
        for kc in range(16):
            ce = "dve"
            fw.op(ce, lambda e, kc=kc: e.scalar_tensor_tensor(out=B1.t[:, kc, 0:WX], in0=F1.t[:, kc, 0:WX],
                                                              scalar=gffn.t[:, kc:kc + 1], in1=rowB.t[:, 0:WX],
                                                              op0=ALU.mult, op1=ALU.mult),
                  reads=[F1.bufs[kc], gffn.buf, rowB.buf], writes=[B1.bufs[kc]])
        dump("h2", B1.t[:, :, :], [128, 16, WH], B1.bufs, BF16)

        h2rhs = lambda kc: B1.t[:, kc, 0:WX]
        cres = {}

        def conv3(m, pt, dst):
            fw.op("act", lambda e: e.activation(out=dst.t[:, 0:TO], in_=pt.t[:, 0:TO], func=AF.Copy,
                                                scale=fcw.t[:, m, 0:1]),
                  reads=[pt.buf, fcw.buf], writes=[dst.buf])
            for j in (1, 2):
                fw.op("dve", lambda e, j=j: e.scalar_tensor_tensor(out=dst.t[:, 0:TO], in0=pt.t[:, j:j + TO],
                                                                   scalar=fcw.t[:, m, j:j + 1], in1=dst.t[:, 0:TO],
                                                                   op0=ALU.mult, op1=ALU.add),
                      reads=[pt.buf, fcw.buf], writes=[dst.buf])

        def epi_up(m, pt):
            if m < 44:
                tb = tmpA[st["ti"] % 3]
                st["ti"] += 1
                conv3(m, pt, tb)
                fw.op("act", lambda e: e.activation(out=tb.t[:, 0:TO], in_=tb.t[:, 0:TO], func=AF.Silu),
                      reads=[tb.buf], writes=[tb.buf])
                cres["g"] = tb
            else:
                tb = tmpB[st["fi"] % 3]
                st["fi"] += 1
                conv3(m, pt, tb)
                tg = cres["g"]
                fw.op("pool", lambda e: e.tensor_tensor(out=BF.t[:, m - 44, 0:TO], in0=tb.t[:, 0:TO], in1=tg.t[:, 0:TO],
                                                        op=ALU.mult),
                      reads=[tb.buf, tg.buf], writes=[BF.bufs[m - 44]])
        up_order = []
        for i in range(44):
            up_order += [i, 44 + i]
        linear(wup_b, "wup", up_order, 16, h2rhs, hbuf, WX, epi_up)
        dump("f", BF.t[:, :, :], [128, 44, WX], BF.bufs, BF16)

        def epi_d(m, pt):
            fw.op("act", lambda e: e.activation(out=F2.t[:, m, 0:TO], in_=pt.t[:, 0:TO], func=AF.Copy),
                  reads=[pt.buf], writes=[F2.bufs[m]])
        linear(wdn_b, "wdn", list(range(16)), 44, lambda kc: BF.t[:, kc, 0:TO], lambda kc: BF.bufs[kc], TO, epi_d, ksplit=2)
        dsrc = lambda kc: F2.t[:, kc, 0:TO]
        srcmap[dsrc] = F2.bufs
        rms_row(dsrc, TO, rowA)
        for kc in range(16):
            tb = tmpB[kc % 3]
            ce = "dve"
            fw.op(ce, lambda e, kc=kc, tb=tb: e.tensor_tensor(out=tb.t[:, 0:TO], in0=F2.t[:, kc, 0:TO], in1=rowA.t[:, 0:TO],
                                                              op=ALU.mult),
                  reads=[F2.bufs[kc], rowA.buf], writes=[tb.buf])
            fw.op(ce, lambda e, kc=kc, tb=tb: e.scalar_tensor_tensor(out=F2.t[:, kc, 0:TO], in0=tb.t[:, 0:TO],
                                                                     scalar=gfpost.t[:, kc:kc + 1], in1=F1.t[:, kc, 1:1 + TO],
                                                                     op0=ALU.mult, op1=ALU.add),
                  reads=[tb.buf, gfpost.buf, F1.bufs[kc]], writes=[F2.bufs[kc]])
        dump("yT", F2.t[:, :, :], [128, 16, WX], F2.bufs)
        t0 = 0
        while t0 < TO:
            n = min(128, TO - t0)
            xb = xt[st["xi"] % 2]
            st["xi"] += 1
            for k4 in range(4):
                tp = tpf[k4 % 2]
                for j in range(4):
                    kc = k4 * 4 + j
                    fw.op("pe", lambda e, kc=kc, j=j, tp=tp: e.transpose(tp.t[0:n, j, :], F2.t[:, kc, t0:t0 + n],
                                                                         identf.t[:, :]),
                          reads=[F2.bufs[kc], identf.buf], writes=[tp.buf], sig=(j == 3))
                ce = "dve" if k4 % 2 == 0 else "act"
                if ce == "dve":
                    fw.op("dve", lambda e, k4=k4, tp=tp: e.tensor_copy(out=xb.t[0:n, k4 * 512:(k4 + 1) * 512].rearrange("p (j c) -> p j c", c=128),
                                                                       in_=tp.t[0:n, :, :]),
                          reads=[tp.buf], writes=[xb.buf], excl=(k4 == 0))
                else:
                    fw.op("act", lambda e, k4=k4, tp=tp: e.activation(out=xb.t[0:n, k4 * 512:(k4 + 1) * 512].rearrange("p (j c) -> p j c", c=128),
                                                                      in_=tp.t[0:n, :, :], func=AF.Copy),
                          reads=[tp.buf], writes=[xb.buf], excl=(k4 == 0))
            fw.dma("act", y[s, a + t0:a + t0 + n, :], xb.t[0:n, :], reads=[xb.buf], writes=[outb], excl=False)
            t0 += n

    for s in range(nseg):
        for ti in tiles:
            tile(s, ti)

    fw.wait_all("sp", [outb] + dbg_bufs)
    es.close()
    return nc, fw, dbg_out


def _wtile(W):
    K, M = W.shape
    KC, MC = K // 128, M // 128
    return np.ascontiguousarray(W.reshape(KC, 128, MC, 128).transpose(2, 1, 0, 3).reshape(MC, 128, KC * 128))


def _fm(v):
    return np.ascontiguousarray(v.reshape(-1, 128).T)


def prep_common(inp):
    f = lambda k: np.asarray(inp[k], dtype=np.float32)[0]
    w_in = f("w_in")
    com = {}
    com["wz"] = _wtile(w_in[:, 8192:12288])
    com["wglu"] = _wtile(w_in[:, 12416:16512])
    com["wgt"] = _wtile(w_in[:, 16512:20608])
    com["wdo"] = _wtile(f("w_delta_out"))
    com["wco"] = _wtile(f("w_conf_out"))
    com["wmo"] = _wtile(f("w_mix_out"))
    com["wup"] = _wtile(f("w_up"))
    wd = _wtile(f("w_down"))
    com["wdn"] = np.ascontiguousarray(wd.reshape(16, 128, 2, DFF // 2).transpose(0, 2, 1, 3).reshape(32, 128, DFF // 2))
    vecs = np.zeros((128, 16, 8), np.float32)
    for i, k in enumerate(("mix_norm_pre", "mix_norm_post", "ffn_norm_pre", "ffn_norm_post", "conf_conv_b",
                           "conf_ln_w", "conf_ln_b")):
        vecs[:, :, i] = _fm(f(k))
    com["vecs"] = vecs
    cw = f("conf_conv_w")
    com["cconv"] = np.ascontiguousarray(cw.T.reshape(16, 128, 31).transpose(1, 0, 2))
    fc = f("ffn_conv_w")
    com["fconv"] = np.ascontiguousarray(fc.T.reshape(88, 128, 3).transpose(1, 0, 2))
    com["dnw"] = np.ascontiguousarray(f("delta_norm_w").reshape(128, 1))
    com["ident"] = np.eye(128, dtype=np.float32)
    return com


def prep_core(inp, c, com):
    xs = [np.asarray(inp["x_prompt"], np.float32)[0], np.asarray(inp["x_sample"], np.float32)[0],
          np.asarray(inp["x_sample"], np.float32)[1]]
    xp = np.zeros((NSEQ, SEG + 2 * HALO, D), np.float32)
    for s in range(NSEQ):
        lo, hi = c * SEG - HALO, (c + 1) * SEG + HALO
        l2, h2 = max(lo, 0), min(hi, SEQ)
        xp[s, l2 - lo:h2 - lo] = xs[s][l2:h2]
    msk = np.ones((NSEQ * NT, 128, WX), np.float32)
    for s in range(NSEQ):
        for ti in range(NT):
            a = min(TO * ti, SEG - TO)
            g = c * SEG + a - 1 + np.arange(WX)
            msk[s * NT + ti, :, :] = ((g >= 0) & (g < SEQ)).astype(np.float32)[None, :]
    d = dict(com)
    d["xp"] = xp
    d["msk"] = msk
    return d


WIN = 256
NWIN = SEQ // WIN
NBLK = SEQ // 128
L2_EPS = 1e-6


def build_p1(cfg):
    nseq = cfg.get("nseq", NSEQ)
    nwin = cfg.get("nwin", NWIN)
    dbg = cfg.get("dbg", [])
    do_scan = cfg.get("scan", True)
    nc = bass.Bass("TRN2", target_bir_lowering=False)
    from contextlib import ExitStack
    es = ExitStack()
    nblk = nwin * 2

    def din(name, shape, dt=F32):
        return nc.dram_tensor(name, list(shape), dt, kind="ExternalInput")

    def dint(name, shape, dt):
        return nc.dram_tensor(name, list(shape), dt, kind="Internal")

    def dout(name, shape, dt=F32):
        return nc.dram_tensor(name, list(shape), dt, kind="ExternalOutput")

    xq = din("xq", [NSEQ, SEQ + 4, D])
    wqkv_f = din("wqkv_in", [8, 128, D])
    wab_f = din("wab_in", [128, 16, 16])
    gpre_d = din("gpre_in", [128, 16])
    scw_d = din("scw_in", [128, 8, 5])
    abv_d = din("abv_in", [16, 2])
    dnw_d = din("dnwb", [128, 128])
    ident_d = din("ident", [128, 128])
    masks_d = din("masks_in", [128, 8, 128])
    ind_d = din("ind_in", [128, 2, 128])
    esel_d = din("esel_in", [4, 8, 128])
    oT = dout("oT", [512, NSEQ, SEQ + 2], BF16)

    base = dint("base", [nseq, nblk, 128, 10, 128], F32)
    gbd = dint("gbd", [nseq, nblk, 128, 16], F32)
    ofw = dint("ofw", [nseq, nblk, 128, 4, 128], F32)

    dbg_out = {}
    dbg_bufs = []
    outb = Buf("out")
    sems = [es.enter_context(nc.semaphore(f"s{i}")) for i in range(100)]
    fw = FW(nc, sems)

    def sb(name, shape, dt, nchunk=1):
        return T(es.enter_context(nc.sbuf_tensor(name, list(shape), dt)), nchunk, name)

    def ps(name, shape, dt):
        t_ = T(es.enter_context(nc.psum_tensor(name, list(shape), dt)), 1, name)
        t_.bufs[0].psum = True
        return t_

    def dump(name, ap, shape, bufs, dt=F32):
        if name not in dbg:
            return
        if name not in dbg_out:
            dbg_out[name] = dout("dbg_" + name, shape, dt)
        b = Buf("dbg")
        dbg_bufs.append(b)
        fw.dma("sp", dbg_out[name][tuple(slice(None) for _ in shape)], ap, reads=bufs, writes=[b])

    identf = sb("identf", [128, 128], F32)
    identb = sb("identb", [128, 128], BF16)
    onesf = sb("onesf", [128, 128], F32)
    gpre = sb("gpre", [128, 16], F32)
    scw = sb("scw", [128, 8, 5], F32)
    abv = sb("abv", [16, 4], F32)
    dnw = sb("dnw", [128, 128], F32)
    masks = sb("masks", [128, 8, 128], F32)
    ind = sb("ind", [128, 2, 128], F32)
    esel = sb("esel", [4, 8, 128], F32)
    wqkv = sb("wqkv", [128, 8, D], BF16)
    wab = sb("wab", [128, 16, 16], BF16)
    for t_, d_ in ((identf, ident_d), (gpre, gpre_d), (dnw, dnw_d)):
        fw.dma("sp", t_.t[:, :], d_[:, :], writes=[t_.buf])
    fw.dma("sp", abv.t[:, 0:2], abv_d[:, :], writes=[abv.buf])
    for t_, d_ in ((scw, scw_d), (masks, masks_d), (ind, ind_d), (esel, esel_d)):
        fw.dma("sp", t_.t[:, :, :], d_[:, :, :], writes=[t_.buf])
    for mc in range(8):
        fw.dma("pool", wqkv.t[:, mc, :], wqkv_f[mc, :, :], writes=[wqkv.buf], excl=False)
    fw.dma("pool", wab.t[:, :, :], wab_f[:, :, :], writes=[wab.buf])
    fw.op("dve", lambda e: e.tensor_copy(out=identb.t[:, :], in_=identf.t[:, :]), reads=[identf.buf], writes=[identb.buf])
    fw.op("dve", lambda e: e.memset(onesf.t[:, :], 1.0), writes=[onesf.buf])
    fw.op("act", lambda e: e.activation(out=abv.t[:, 2:3], in_=abv.t[:, 1:2], func=AF.Exp), reads=[abv.buf], writes=[abv.buf])
    fw.op("dve", lambda e: e.tensor_scalar(out=abv.t[:, 2:3], in0=abv.t[:, 2:3], scalar1=-1.0, scalar2=None, op0=ALU.mult),
          reads=[abv.buf], writes=[abv.buf])

    WW = WIN + 4
    xt = [sb(f"xt{i}", [128, D], F32) for i in range(2)]
    xn = sb("xn", [128, D], BF16)
    ss = sb("ss", [128, 4], F32)
    hT = sb("hT", [128, 16, WW], BF16, 16)
    pre = sb("pre", [128, 8, WW], F32, 8)
    qkvc = [sb(f"qkvc{i}", [128, 8, WIN], F32, 8) for i in range(2)]
    tmpw = [sb(f"tw{i}", [128, WIN], F32) for i in range(2)]
    rows = sb("rows", [16, 2, WIN], F32, 2)
    stg = [sb(f"stg{i}", [128, 6, 128], F32) for i in range(2)]
    gbt = [sb(f"gbt{i}", [128, 16], F32) for i in range(2)]
    mm = [ps(f"mm{i}", [128, 512], F32) for i in range(3)]
    tpb = ps("tpb", [128, 8, 128], BF16)
    tpf = [ps(f"tpf{i}", [128, 512], F32) for i in range(2)]
    stp = ps("stp", [128, 512], F32)
    rwp = ps("rwp", [128, 512], F32)
    st = {"xi": 0, "pi": 0, "qi": 0, "ti": 0, "si": 0}
    base_bufs = [[Buf(f"base{s}_{b}") for b in range(nblk)] for s in range(nseq)]
    gb_bufs = [[Buf(f"gb{s}_{b}") for b in range(nblk)] for s in range(nseq)]

    def window(s, w):
        t0 = w * WIN
        r0 = 0
        while r0 < WW:
            n = min(128, WW - r0)
            xb = xt[st["xi"] % 2]
            st["xi"] += 1
            fw.dma("sp", xb.t[0:n, :], xq[s, t0 + r0:t0 + r0 + n, :], writes=[xb.buf])
            fw.op("dve", lambda e: e.memset(ss.t[:, 0:1], 0.0), writes=[ss.buf])
            fw.op("act", lambda e: e.activation(out=xn.t[0:n, :], in_=xb.t[0:n, :], func=AF.Square, accum_out=ss.t[0:n, 0:1]),
                  reads=[xb.buf], writes=[xn.buf, ss.buf])
            fw.op("act", lambda e: e.activation(out=ss.t[0:n, 1:2], in_=ss.t[0:n, 0:1], func=AF.Sqrt, bias=RMS_EPS, scale=1.0 / D),
                  reads=[ss.buf], writes=[ss.buf])
            fw.op("dve", lambda e: e.reciprocal(out=ss.t[0:n, 1:2], in_=ss.t[0:n, 1:2]), reads=[ss.buf], writes=[ss.buf])
            fw.op("dve", lambda e: e.tensor_scalar(out=xn.t[0:n, :], in0=xb.t[0:n, :], scalar1=ss.t[0:n, 1:2], scalar2=None,
                                                   op0=ALU.mult), reads=[xb.buf, ss.buf], writes=[xn.buf])
            for k4 in range(4):
                for j in range(4):
                    kc = k4 * 4 + j
                    fw.op("pe", lambda e, kc=kc, j=j: e.transpose(tpb.t[:, j, 0:n], xn.t[0:n, kc * 128:(kc + 1) * 128],
                                                                  identb.t[0:n, 0:n]),
                          reads=[xn.buf, identb.buf], writes=[tpb.buf], sig=(j == 3))
                for j in range(4):
                    kc = k4 * 4 + j
                    fw.op("act", lambda e, kc=kc, j=j: e.activation(out=hT.t[:, kc, r0:r0 + n], in_=tpb.t[:, j, 0:n],
                                                                    func=AF.Copy, scale=gpre.t[:, kc:kc + 1]),
                          reads=[tpb.buf, gpre.buf], writes=[hT.bufs[kc]], excl=(r0 == 0))
            r0 += n
        for mc in range(8):
            pt = mm[st["pi"] % 3]
            st["pi"] += 1
            for kc in range(16):
                fw.op("pe", lambda e, kc=kc, mc=mc: e.matmul(pt.t[:, 0:WW], lhsT=wqkv.t[:, mc, kc * 128:(kc + 1) * 128],
                                                             rhs=hT.t[:, kc, 0:WW], start=(kc == 0), stop=(kc == 15)),
                      reads=[wqkv.buf, hT.bufs[kc]], writes=[pt.buf], sig=(kc == 15))
            fw.op("act", lambda e, mc=mc: e.activation(out=pre.t[:, mc, 0:WW], in_=pt.t[:, 0:WW], func=AF.Copy),
                  reads=[pt.buf], writes=[pre.bufs[mc]])
        for kc in range(16):
            fw.op("pe", lambda e, kc=kc: e.matmul(rwp.t[0:16, 0:WIN], lhsT=wab.t[:, kc, :], rhs=hT.t[:, kc, 2:2 + WIN],
                                                  start=(kc == 0), stop=(kc == 15)),
                  reads=[wab.buf, hT.bufs[kc]], writes=[rwp.buf], sig=(kc == 15))
        fw.op("act", lambda e: e.activation(out=rows.t[:, 0, :], in_=rwp.t[0:16, 0:WIN], func=AF.Exp, bias=abv.t[:, 0:1], scale=1.0),
              reads=[rwp.buf, abv.buf], writes=[rows.bufs[0]])
        fw.op("act", lambda e: e.activation(out=rows.t[:, 0, :], in_=rows.t[:, 0, :], func=AF.Ln, bias=1.0, scale=1.0),
              reads=[rows.bufs[0]], writes=[rows.bufs[0]])
        fw.op("dve", lambda e: e.tensor_scalar(out=rows.t[:, 0, :], in0=rows.t[:, 0, :], scalar1=abv.t[:, 2:3], scalar2=None,
                                               op0=ALU.mult), reads=[rows.bufs[0], abv.buf], writes=[rows.bufs[0]])
        fw.op("act", lambda e: e.activation(out=rows.t[:, 1, :], in_=rwp.t[0:16, 0:WIN], func=AF.Sigmoid),
              reads=[rwp.buf], writes=[rows.bufs[1]])
        qc = qkvc[st["qi"] % 2]
        st["qi"] += 1
        for mc in range(8):
            tw = tmpw[st["ti"] % 2]
            st["ti"] += 1
            fw.op("dve", lambda e, mc=mc, tw=tw: e.tensor_scalar(out=tw.t[:, 0:WIN], in0=pre.t[:, mc, 0:WIN], scalar1=scw.t[:, mc, 0:1],
                                                                scalar2=None, op0=ALU.mult),
                  reads=[pre.bufs[mc], scw.buf], writes=[tw.buf])
            for j in range(1, 5):
                fw.op("dve", lambda e, mc=mc, j=j, tw=tw: e.scalar_tensor_tensor(out=tw.t[:, 0:WIN], in0=pre.t[:, mc, j:j + WIN],
                                                                                 scalar=scw.t[:, mc, j:j + 1], in1=tw.t[:, 0:WIN],
                                                                                 op0=ALU.mult, op1=ALU.add),
                      reads=[pre.bufs[mc], scw.buf], writes=[tw.buf])
            fw.op("act", lambda e, mc=mc, tw=tw: e.activation(out=qc.t[:, mc, :], in_=tw.t[:, 0:WIN], func=AF.Silu),
                  reads=[tw.buf], writes=[qc.bufs[mc]])
        for mc in range(4):
            tw = tmpw[st["ti"] % 2]
            st["ti"] += 1
            fw.op("act", lambda e, mc=mc, tw=tw: e.activation(out=tw.t[:, 0:WIN], in_=qc.t[:, mc, :], func=AF.Square),
                  reads=[qc.bufs[mc]], writes=[tw.buf])
            fw.op("pe", lambda e, tw=tw: e.matmul(stp.t[:, 0:WIN], lhsT=onesf.t[:, :], rhs=tw.t[:, 0:WIN], start=True, stop=True),
                  reads=[onesf.buf, tw.buf], writes=[stp.buf])
            fw.op("act", lambda e, tw=tw: e.activation(out=tw.t[:, 0:WIN], in_=stp.t[:, 0:WIN], func=AF.Sqrt, bias=L2_EPS, scale=1.0),
                  reads=[stp.buf], writes=[tw.buf])
            fw.op("dve", lambda e, tw=tw: e.reciprocal(out=tw.t[:, 0:WIN], in_=tw.t[:, 0:WIN]), reads=[tw.buf], writes=[tw.buf])
            sc = (128.0 ** -0.5) if mc < 2 else 1.0
            fw.op("dve", lambda e, mc=mc, tw=tw, sc=sc: e.scalar_tensor_tensor(out=qc.t[:, mc, :], in0=qc.t[:, mc, :], scalar=sc,
                                                                               in1=tw.t[:, 0:WIN], op0=ALU.mult, op1=ALU.mult),
                  reads=[tw.buf], writes=[qc.bufs[mc]])
        dump("qkvc", qc.t[:, :, :], [128, 8, WIN], qc.bufs)
        dump("rows", rows.t[:, :, :], [16, 2, WIN], rows.bufs)
        for hb in range(2):
            blk = w * 2 + hb
            sg = stg[st["si"] % 2]
            gt_ = gbt[st["si"] % 2]
            st["si"] += 1
            c0 = hb * 128
            tp = tpf[0]
            for i, mc in enumerate((2, 3, 4, 5)):
                fw.op("pe", lambda e, i=i, mc=mc: e.transpose(tp.t[:, i * 128:(i + 1) * 128], qc.t[:, mc, c0:c0 + 128], identf.t[:, :]),
                      reads=[qc.bufs[mc], identf.buf], writes=[tp.buf])
            fw.op("act", lambda e: e.activation(out=sg.t[:, 0:4, :], in_=tp.t[:, 0:512].rearrange("p (a b) -> p a b", b=128), func=AF.Copy), reads=[tp.buf], writes=[sg.buf])
            tp2 = tpf[1]
            for i, mc in enumerate((6, 7)):
                fw.op("pe", lambda e, i=i, mc=mc: e.transpose(tp2.t[:, i * 128:(i + 1) * 128], qc.t[:, mc, c0:c0 + 128], identf.t[:, :]),
                      reads=[qc.bufs[mc], identf.buf], writes=[tp2.buf])
            for i in range(2):
                fw.op("pe", lambda e, i=i: e.transpose(tp2.t[:, 256 + i * 128:256 + i * 128 + 16], rows.t[:, i, c0:c0 + 128], identf.t[0:16, 0:16]),
                      reads=[rows.bufs[i], identf.buf], writes=[tp2.buf])
            fw.op("dve", lambda e: e.tensor_copy(out=sg.t[:, 4:6, :], in_=tp2.t[:, 0:256].rearrange("p (a b) -> p a b", b=128)), reads=[tp2.buf], writes=[sg.buf], excl=False)
            fw.op("dve", lambda e: e.tensor_copy(out=gt_.t[:, 0:8], in_=tp2.t[:, 256:264]), reads=[tp2.buf], writes=[gt_.buf])
            fw.op("dve", lambda e: e.tensor_copy(out=gt_.t[:, 8:16], in_=tp2.t[:, 384 + 8:384 + 16]), reads=[tp2.buf], writes=[gt_.buf], excl=False)
            fw.dma("act", base[s, blk, :, 0:4, :], qc.t[:, 0:4, c0:c0 + 128], reads=qc.bufs[0:4], writes=[base_bufs[s][blk]],
                   owner=base_bufs[s][0], excl=False)
            fw.dma("act", base[s, blk, :, 4:10, :], sg.t[:, :, :], reads=[sg.buf], writes=[base_bufs[s][blk]],
                   owner=base_bufs[s][0], excl=False)
            fw.dma("act", gbd[s, blk, :, :], gt_.t[:, :], reads=[gt_.buf], writes=[gb_bufs[s][blk]], owner=base_bufs[s][0])
            if "stg" in dbg:
                dump("stg", sg.t[:, :, :], [128, 6, 128], [sg.buf])
                dump("gbt", gt_.t[:, :], [128, 16], [gt_.buf])

    for s in range(nseq):
        for w in range(nwin):
            window(s, w)

    if do_scan:
        fw.barrier()
        _scan_pass(nc, fw, es, locals())

    fw.wait_all("sp", [outb] + dbg_bufs)
    fw.wait_all("act", [outb] + dbg_bufs)
    es.close()
    return nc, fw, dbg_out


def prep_p1_common(inp):
    f = lambda k: np.asarray(inp[k], dtype=np.float32)[0]
    com = {}
    com["gpre_in"] = _fm(f("mix_norm_pre"))
    com["dnwb"] = np.ascontiguousarray(np.broadcast_to(f("delta_norm_w")[None, :], (128, 128))).astype(np.float32)
    com["ident"] = np.eye(128, dtype=np.float32)
    t = np.arange(128)
    same = (t[:, None] // 64) == (t[None, :] // 64)
    s_, c_ = t[:, None], t[None, :]
    m = np.zeros((128, 8, 128), np.float32)
    m[:, 0] = (same & (c_ >= s_))
    m[:, 1] = (same & (c_ <= s_))
    m[:, 2] = (same & (c_ > s_))
    m[:, 3] = (same & (c_ < s_))
    m[:, 4] = np.where(m[:, 0] > 0, 0.0, -30000.0)
    m[:, 5] = np.where(m[:, 1] > 0, 0.0, -30000.0)
    m[:, 6] = same
    com["masks_in"] = m
    ind = np.zeros((128, 2, 128), np.float32)
    ind[0:64, 0, :] = 1.0
    ind[64:128, 1, :] = 1.0
    com["ind_in"] = ind
    es_ = np.zeros((4, 8, 128), np.float32)
    for j in range(4):
        es_[j, j, :] = 1.0
        es_[j, 4 + j, :] = -1.0
    com["esel_in"] = es_
    return com


def prep_p1_core(inp, c, com, xq):
    f = lambda k: np.asarray(inp[k], dtype=np.float32)[0]
    w_in = f("w_in")
    cols = np.concatenate([np.arange(256 * c, 256 * c + 256), 2048 + np.arange(256 * c, 256 * c + 256),
                           4096 + np.arange(512 * c, 512 * c + 512)])
    d = dict(com)
    d["wqkv_in"] = _wtile(w_in[:, cols])
    acols = np.concatenate([12288 + dd * 32 + 4 * c + np.arange(4) for dd in range(2)])
    bcols = acols + 64
    wab = w_in[:, np.concatenate([acols, bcols])]
    d["wab_in"] = np.ascontiguousarray(wab.reshape(16, 128, 16).transpose(1, 0, 2))
    scw = f("short_conv_w")[:, cols]
    d["scw_in"] = np.ascontiguousarray(scw.T.reshape(8, 128, 5).transpose(1, 0, 2))
    abv = np.zeros((16, 2), np.float32)
    abv[0:8, 0] = f("dt_bias")[:, 4 * c:4 * c + 4].reshape(-1)
    abv[0:8, 1] = f("a_log")[:, 4 * c:4 * c + 4].reshape(-1)
    d["abv_in"] = abv
    d["xq"] = xq
    return d


def prep_xq(inp):
    xs = [np.asarray(inp["x_prompt"], np.float32)[0], np.asarray(inp["x_sample"], np.float32)[0],
          np.asarray(inp["x_sample"], np.float32)[1]]
    xq = np.zeros((NSEQ, SEQ + 4, D), np.float32)
    for s in range(NSEQ):
        xq[s, 2:2 + SEQ] = xs[s]
    return xq


def _scan_pass(nc, fw, es, L):
    sb, ps, dump = L["sb"], L["ps"], L["dump"]
    masks, ind, esel, identf, identb, dnw = L["masks"], L["ind"], L["esel"], L["identf"], L["identb"], L["dnw"]
    base, gbd, ofw, oT = L["base"], L["gbd"], L["ofw"], L["oT"]
    base_bufs, gb_bufs, nseq, nblk, outb, dbg = L["base_bufs"], L["gb_bufs"], L["nseq"], L["nblk"], L["outb"], L["dbg"]
    mmp, stp, rwp, tpf, tpb = L["mm"], L["stp"], L["rwp"], L["tpf"], L["tpb"]
    lvl = float(L["cfg"].get("lvl", 4))

    class PS_:
        def __init__(self, ap, bank):
            self.ap = ap
            self.buf = bank.buf
    pG = PS_(mmp[0].t[:, 0:64], mmp[0])
    pGr = PS_(mmp[0].t[0:4, 128:256], mmp[0])
    pGram = [PS_(mmp[1].t[:, i * 128:(i + 1) * 128], mmp[1]) for i in range(4)]
    UB = [mmp[2], stp, rwp, tpf[0]]
    pDiff = [PS_(b_.t[:, 0:128], b_) for b_ in UB]
    pT = [PS_(b_.t[:, 128:256], b_) for b_ in UB]
    pPow = [PS_(b_.t[:, 0:256], b_) for b_ in UB]
    pApp = [PS_(b_.t[:, 256:512], b_) for b_ in UB]
    pS = [[PS_(b_.t[:, k * 128:(k + 1) * 128], b_) for k in range(4)] for b_ in UB]

    def run_rr(gens):
        gens = list(gens)
        while gens:
            for g_ in list(gens):
                try:
                    next(g_)
                except StopIteration:
                    gens.remove(g_)
    pO = PS_(tpb.t[:, 0:4, :], tpb)

    T32 = [sb(f"T32_{i}", [128, 10, 128], F32) for i in range(2)]
    QB = [sb(f"QB{i}", [128, 2, 128], BF16) for i in range(2)]
    GB = [sb(f"GB{i}", [128, 16], F32) for i in range(2)]
    sm = [sb(f"sm{i}", [128, 64], F32) for i in range(2)]
    sm2 = [sb(f"sm2_{i}", [128, 16], F32) for i in range(2)]
    Grs = [sb(f"Grs{i}", [4, 128], F32) for i in range(2)]
    KKs = [sb(f"KKs{i}", [128, 2, 128], F32, 2) for i in range(2)]
    KQs = [sb(f"KQs{i}", [128, 2, 128], F32, 2) for i in range(2)]

    class Unit:
        pass
    units = []
    for k in range(8):
        U = Unit()
        U.Dm = sb(f"uDm{k}", [128, 128], F32)
        U.Y = [sb(f"uY{k}_{i}", [128, 128], BF16) for i in range(2)]
        U.W = sb(f"uW{k}", [128, 512], BF16)
        U.At = sb(f"uAt{k}", [128, 128], BF16)
        U.u = sb(f"uu{k}", [128, 128], F32)
        U.w = sb(f"uw{k}", [128, 128], BF16)
        U.wT = sb(f"uwT{k}", [128, 128], BF16)
        U.kd = sb(f"ukd{k}", [128, 128], BF16)
        U.vn = sb(f"uvn{k}", [128, 128], BF16)
        U.p3s = sb(f"up3{k}", [128, 128], F32)
        units.append(U)
    S = [sb(f"S{j}", [128, 128], F32) for j in range(4)]
    Sb = [sb(f"Sb{j}", [128, 128], BF16) for j in range(4)]
    oacc = [sb(f"oacc{i}", [128, 4, 128], F32, 4) for i in range(2)]
    of32 = [sb(f"of32_{i}", [128, 4, 128], F32) for i in range(2)]
    onb = sb("onb", [128, 4, 128], BF16)
    ostg = [sb(f"ostg{i}", [128, 4, 130], BF16) for i in range(2)]
    ssq = sb("ssq", [128, 8], F32)
    ofw_bufs = [[Buf(f"ofw{s}_{b}") for b in range(nblk)] for s in range(nseq)]
    ofw_own = Buf("ofw_own")
    ld_own = [Buf("ld0"), Buf("ld1")]

    def mmf(pt, lhsT, rhs, reads, start=True, stop=True, sig=True):
        fw.op("pe", lambda e: e.matmul(pt.ap, lhsT=lhsT, rhs=rhs, start=start, stop=stop), reads=reads, writes=[pt.buf], sig=sig)

    zf = sb("zf", [128, 4, 2], F32)
    fw.op("dve", lambda e: e.memset(zf.t[:, :, :], 0.0), writes=[zf.buf])
    for og0 in ostg:
        for cc in (0, 129):
            fw.op("dve", lambda e, og0=og0, cc=cc: e.tensor_copy(out=og0.t[:, :, cc:cc + 1], in_=zf.t[:, :, 0:1]), reads=[zf.buf],
                  writes=[og0.buf], excl=False)
    cnt = {"u": 0, "e": 0}
    for s in range(nseq):
        for dirn in (0, 1):
            for j in range(4):
                fw.op("dve", lambda e, j=j: e.memset(S[j].t[:, :], 0.0), writes=[S[j].buf])
                fw.op("dve", lambda e, j=j: e.tensor_copy(out=Sb[j].t[:, :], in_=S[j].t[:, :]), reads=[S[j].buf], writes=[Sb[j].buf])
            order = list(range(nblk)) if dirn == 0 else list(reversed(range(nblk)))
            for bi, blk in enumerate(order):
                par = bi % 2
                t32, qb, gb = T32[par], QB[par], GB[par]
                fw.dma("sp", t32.t[:, :, :], base[s, blk, :, :, :], reads=[base_bufs[s][blk]], writes=[t32.buf], owner=ld_own[par])
                fw.op("act", lambda e: e.activation(out=qb.t[:, :, :], in_=t32.t[:, 0:2, :], func=AF.Copy), reads=[t32.buf], writes=[qb.buf])
                fw.dma("sp", gb.t[:, :], gbd[s, blk, :, :], reads=[gb_bufs[s][blk]], writes=[gb.buf])
                g4 = gb.t[:, dirn * 4:dirn * 4 + 4]
                b4 = gb.t[:, 8 + dirn * 4:8 + dirn * 4 + 4]
                Mi, Ms, Mn, same = masks.t[:, dirn, :], masks.t[:, 2 + dirn, :], masks.t[:, 4 + dirn, :], masks.t[:, 6, :]
                a_, a2, gr = sm[par], sm2[par], Grs[par]
                gall = gb.t[:, 0:16]
                d4 = dirn * 4
                if lvl <= 0:
                    continue
                fw.op("pe", lambda e: e.matmul(mmp[0].t[:, 0:16], lhsT=Mi, rhs=gall, start=True, stop=True),
                      reads=[masks.buf, gb.buf], writes=[pG.buf], sig=False)
                fw.op("pe", lambda e: e.matmul(mmp[0].t[:, 16:32], lhsT=same, rhs=gall, start=True, stop=True),
                      reads=[masks.buf, gb.buf], writes=[pG.buf], sig=False)
                for h in range(2):
                    fw.op("pe", lambda e, h=h: e.matmul(mmp[0].t[:, 32 + 16 * h:48 + 16 * h], lhsT=ind.t[:, h, :], rhs=gall, start=True, stop=True),
                          reads=[ind.buf, gb.buf], writes=[pG.buf], sig=False)
                mmf(pGr, g4, Mi, [masks.buf, gb.buf])
                fw.op("act", lambda e: e.activation(out=a_.t[:, :], in_=pG.ap, func=AF.Copy), reads=[pG.buf], writes=[a_.buf])
                fw.op("act", lambda e: e.activation(out=a2.t[:, 0:4], in_=a_.t[:, d4:d4 + 4], func=AF.Exp), reads=[a_.buf], writes=[a2.buf])
                fw.op("dve", lambda e: e.tensor_tensor(out=a2.t[:, 4:8], in0=a_.t[:, 16 + d4:20 + d4], in1=a_.t[:, d4:d4 + 4], op=ALU.subtract),
                      reads=[a_.buf], writes=[a2.buf], excl=False)
                fw.op("act", lambda e: e.activation(out=a2.t[:, 4:8], in_=a2.t[:, 4:8], func=AF.Exp), reads=[a2.buf], writes=[a2.buf])
                for h in range(2):
                    fw.op("act", lambda e, h=h: e.activation(out=a2.t[:, 8 + 4 * h:12 + 4 * h], in_=a_.t[:, 32 + 16 * h + d4:36 + 16 * h + d4], func=AF.Exp),
                          reads=[a_.buf], writes=[a2.buf], excl=False)
                fw.op("dve", lambda e: e.tensor_copy(out=gr.t[:, :], in_=pGr.ap), reads=[pGr.buf], writes=[gr.buf])
                if lvl <= 0.5:
                    continue
                Gam = lambda j: a2.t[:, j:j + 1]
                E2 = lambda j: a2.t[:, 4 + j:5 + j]
                gbc = lambda h, j: a2.t[:, 8 + 4 * h + j:9 + 4 * h + j]
                kks, kqs = KKs[par], KQs[par]
                for q in range(2):
                    mmf(pGram[2 * q], t32.t[:, 2 + q, :], t32.t[:, 2 + q, :], [t32.buf])
                    mmf(pGram[2 * q + 1], t32.t[:, 2 + q, :], t32.t[:, q, :], [t32.buf])
                    fw.op("dve", lambda e, q=q: e.tensor_tensor(out=kks.t[:, q, :], in0=pGram[2 * q].ap, in1=Ms, op=ALU.mult),
                          reads=[pGram[2 * q].buf, masks.buf], writes=[kks.bufs[q]])
                    fw.op("act", lambda e, q=q: e.activation(out=kqs.t[:, q, :], in_=pGram[2 * q + 1].ap, func=AF.Copy),
                          reads=[pGram[2 * q + 1].buf], writes=[kqs.bufs[q]])
                us = [units[par * 4 + j] for j in range(4)]
                oa = oacc[par]
                chunks = (0, 1) if dirn == 0 else (1, 0)

                def unit_gen(j, U, t32=t32, qb=qb, gb=gb, b4=b4, a2=a2, gr=gr, kks=kks, kqs=kqs, Mn=Mn, oa=oa, chunks=chunks,
                             Gam=Gam, E2=E2, gbc=gbc):
                    q = j // 2
                    bank = UB[j]
                    pd = PS_(bank.t[:, 0:128], bank)
                    pZ1 = PS_(tpb.t[:, 4 + j, :], tpb)
                    W = U.W
                    Rr = W.t[:, 128:384]
                    mmf(pd, esel.t[:, j, :], gr.t[:, :], [esel.buf, gr.buf], True, False, sig=False)
                    mmf(pd, gr.t[:, :], esel.t[:, 4 + j, :], [esel.buf, gr.buf], False, True)
                    yield
                    fw.op("dve", lambda e: e.scalar_tensor_tensor(out=U.Dm.t[:, :], in0=pd.ap, scalar=0.0, in1=Mn, op0=ALU.min, op1=ALU.add),
                          reads=[pd.buf, masks.buf], writes=[U.Dm.buf])
                    yield
                    fw.op("act", lambda e: e.activation(out=U.Dm.t[:, :], in_=U.Dm.t[:, :], func=AF.Exp), reads=[U.Dm.buf], writes=[U.Dm.buf])
                    yield
                    Yt = U.Y
                    fw.op("dve", lambda e: e.scalar_tensor_tensor(out=Yt[0].t[:, :], in0=kks.t[:, q, :], scalar=b4[:, j:j + 1], in1=U.Dm.t[:, :],
                                                                  op0=ALU.mult, op1=ALU.mult),
                          reads=[kks.bufs[q], gb.buf, U.Dm.buf], writes=[Yt[0].buf])
                    fw.op("pool", lambda e: e.tensor_tensor(out=U.At.t[:, :], in0=kqs.t[:, q, :], in1=U.Dm.t[:, :], op=ALU.mult),
                          reads=[kqs.bufs[q], U.Dm.buf], writes=[U.At.buf])
                    fw.op("act", lambda e: e.activation(out=W.t[:, 128:256], in_=t32.t[:, 6 + j, :], func=AF.Copy), reads=[t32.buf], writes=[W.buf])
                    fw.op("act", lambda e: e.activation(out=W.t[:, 256:384], in_=t32.t[:, 4 + q, :], func=AF.Copy, scale=Gam(j)),
                          reads=[t32.buf, a2.buf], writes=[W.buf], excl=False)
                    fw.op("act", lambda e: e.activation(out=U.kd.t[:, :], in_=t32.t[:, 4 + q, :], func=AF.Copy, scale=E2(j)),
                          reads=[t32.buf, a2.buf], writes=[U.kd.buf])
                    yield
                    fw.op("pe", lambda e: e.transpose(pZ1.ap, Yt[0].t[:, :], identb.t[:, :]), reads=[Yt[0].buf, identb.buf], writes=[pZ1.buf])
                    yield
                    fw.op("act", lambda e: e.activation(out=W.t[:, 0:128], in_=pZ1.ap, func=AF.Copy), reads=[pZ1.buf], writes=[W.buf], excl=False)
                    yield
                    for lv in range(6):
                        Yc, Yn = Yt[lv % 2], Yt[(lv + 1) % 2]
                        even = (lv % 2 == 0)
                        if lv < 5:
                            rhs1 = W.t[:, 0:384] if even else W.t[:, 128:512]
                            zc = W.t[:, 0:128] if even else W.t[:, 384:512]
                            fw.op("pe", lambda e: e.matmul(bank.t[:, 0:384], lhsT=Yc.t[:, :], rhs=rhs1, start=True, stop=True),
                                  reads=[Yc.buf, W.buf], writes=[bank.buf], sig=False)
                            fw.op("pe", lambda e: e.matmul(bank.t[:, 384:512], lhsT=zc, rhs=Yc.t[:, :], start=True, stop=True),
                                  reads=[Yc.buf, W.buf], writes=[bank.buf])
                            app = bank.t[:, 128:384] if even else bank.t[:, 0:256]
                            zn_ps = bank.t[:, 0:128] if even else bank.t[:, 256:384]
                            zn_sb = W.t[:, 384:512] if even else W.t[:, 0:128]
                        else:
                            fw.op("pe", lambda e: e.matmul(bank.t[:, 0:256], lhsT=Yc.t[:, :], rhs=Rr, start=True, stop=True),
                                  reads=[Yc.buf, W.buf], writes=[bank.buf])
                            app = bank.t[:, 0:256]
                        yield
                        if lv < 4:
                            fw.op("act", lambda e: e.activation(out=zn_sb, in_=zn_ps, func=AF.Copy), reads=[bank.buf], writes=[W.buf], excl=False)
                        if lv < 5:
                            fw.op("dve", lambda e: e.tensor_copy(out=Yn.t[:, :], in_=bank.t[:, 384:512]), reads=[bank.buf], writes=[Yn.buf])
                        fw.op("dve", lambda e: e.tensor_tensor(out=Rr, in0=Rr, in1=app, op=(ALU.subtract if lv == 0 else ALU.add)),
                              reads=[bank.buf], writes=[W.buf], excl=False)
                        yield
                    fw.op("dve", lambda e: e.tensor_scalar(out=U.u.t[:, :], in0=W.t[:, 128:256], scalar1=b4[:, j:j + 1], scalar2=None, op0=ALU.mult),
                          reads=[W.buf, gb.buf], writes=[U.u.buf])
                    fw.op("act", lambda e: e.activation(out=U.w.t[:, :], in_=W.t[:, 256:384], func=AF.Copy, scale=b4[:, j:j + 1]),
                          reads=[W.buf, gb.buf], writes=[U.w.buf])
                    yield
                    fw.op("pe", lambda e: e.transpose(pZ1.ap, U.w.t[:, :], identb.t[:, :]), reads=[U.w.buf, identb.buf], writes=[pZ1.buf])
                    yield
                    fw.op("act", lambda e: e.activation(out=U.wT.t[:, :], in_=pZ1.ap, func=AF.Copy), reads=[pZ1.buf], writes=[U.wT.buf])
                    yield
                    P = pS[j]
                    for h in chunks:
                        r = slice(64 * h, 64 * h + 64)
                        mmf(P[0], U.wT.t[:, :], Sb[j].t[:, :], [U.wT.buf, Sb[j].buf], sig=False)
                        mmf(P[1], qb.t[:, q, :], Sb[j].t[:, :], [qb.buf, Sb[j].buf])
                        yield
                        fw.op("dve", lambda e: e.tensor_tensor(out=U.vn.t[r, :], in0=U.u.t[r, :], in1=P[0].ap[r, :], op=ALU.subtract),
                              reads=[U.u.buf, P[0].buf], writes=[U.vn.buf])
                        yield
                        mmf(P[2], U.At.t[r, :], U.vn.t[r, :], [U.At.buf, U.vn.buf], sig=False)
                        mmf(P[3], U.kd.t[r, :], U.vn.t[r, :], [U.kd.buf, U.vn.buf])
                        yield
                        fw.op("act", lambda e: e.activation(out=U.p3s.t[r, :], in_=P[2].ap[r, :], func=AF.Copy),
                              reads=[P[2].buf], writes=[U.p3s.buf])
                        fw.op("dve", lambda e: e.scalar_tensor_tensor(out=S[j].t[:, :], in0=S[j].t[:, :], scalar=gbc(h, j), in1=P[3].ap,
                                                                      op0=ALU.mult, op1=ALU.add),
                              reads=[a2.buf, P[3].buf], writes=[S[j].buf])
                        yield
                        fw.op("act", lambda e: e.activation(out=Sb[j].t[:, :], in_=S[j].t[:, :], func=AF.Copy), reads=[S[j].buf], writes=[Sb[j].buf])
                        fw.op("dve", lambda e: e.scalar_tensor_tensor(out=oa.t[r, j, :], in0=P[1].ap[r, :], scalar=a2.t[r, j:j + 1],
                                                                      in1=U.p3s.t[r, :], op0=ALU.mult, op1=ALU.add),
                              reads=[P[1].buf, a2.buf, U.p3s.buf], writes=[oa.bufs[j]])
                        yield

                run_rr([unit_gen(j, us[j]) for j in range(4)])
                if dirn == 0:
                    fw.dma("act", ofw[s, blk, :, :, :], oa.t[:, :, :], reads=oa.bufs, writes=[ofw_bufs[s][blk]], owner=ofw_own)
                else:
                    of = of32[par]
                    fw.dma("sp", of.t[:, :, :], ofw[s, blk, :, :, :], reads=[ofw_bufs[s][blk]], writes=[of.buf])
                    fw.op("dve", lambda e: e.tensor_tensor(out=oa.t[:, :, :], in0=oa.t[:, :, :], in1=of.t[:, :, :], op=ALU.add),
                          reads=[of.buf], writes=oa.bufs)
                    if "osum" in dbg and blk == 0:
                        dump("osum", oa.t[:, :, :], [128, 4, 128], oa.bufs)
                    fw.op("dve", lambda e: e.memset(ssq.t[:, 0:4], 0.0), writes=[ssq.buf])
                    for j in range(4):
                        fw.op("act", lambda e, j=j: e.activation(out=of.t[:, j, :], in_=oa.t[:, j, :], func=AF.Square, accum_out=ssq.t[:, j:j + 1]),
                              reads=[oa.bufs[j]], writes=[of.buf, ssq.buf])
                    fw.op("act", lambda e: e.activation(out=ssq.t[:, 4:8], in_=ssq.t[:, 0:4], func=AF.Sqrt, bias=RMS_EPS, scale=1.0 / 128),
                          reads=[ssq.buf], writes=[ssq.buf])
                    fw.op("dve", lambda e: e.reciprocal(out=ssq.t[:, 4:8], in_=ssq.t[:, 4:8]), reads=[ssq.buf], writes=[ssq.buf])
                    for j in range(4):
                        fw.op("dve", lambda e, j=j: e.scalar_tensor_tensor(out=onb.t[:, j, :], in0=oa.t[:, j, :], scalar=ssq.t[:, 4 + j:5 + j],
                                                                           in1=dnw.t[:, :], op0=ALU.mult, op1=ALU.mult),
                              reads=[oa.bufs[j], ssq.buf, dnw.buf], writes=[onb.buf], excl=(j == 0))
                    for j in range(4):
                        fw.op("pe", lambda e, j=j: e.transpose(tpb.t[:, j, :], onb.t[:, j, :], identb.t[:, :]),
                              reads=[onb.buf, identb.buf], writes=[pO.buf])
                    og_ = ostg[par]
                    fw.op("act", lambda e: e.activation(out=og_.t[:, :, 1:129], in_=tpb.t[:, 0:4, :], func=AF.Copy), reads=[pO.buf], writes=[og_.buf])
                    c0 = 1 + blk * 128
                    lo_ = 0 if blk == 0 else 1
                    hi_ = 130 if blk == nblk - 1 else 129
                    fw.dma("act", oT[:, s, c0 - 1 + lo_:c0 - 1 + hi_].rearrange("(j p) t -> p j t", p=128), og_.t[:, :, lo_:hi_], reads=[og_.buf],
                           writes=[outb], excl=False)


def kernel(**inputs):
    import ml_dtypes
    inp = {k: np.asarray(v) for k, v in inputs.items()}
    cores = list(range(NCORE))
    com1 = prep_p1_common(inp)
    xq = prep_xq(inp)
    maps1 = [prep_p1_core(inp, c, com1, xq) for c in cores]
    nc1, _, _ = build_p1({})
    res1 = run_bass_kernel_spmd(nc1, maps1, core_ids=cores)
    o_full = np.concatenate([np.asarray(res1.results[c]["oT"]) for c in cores], axis=0)
    del maps1, xq, res1
    com2 = prep_common(inp)
    maps2 = []
    for c in cores:
        d = prep_core(inp, c, com2)
        d["o_ext"] = np.ascontiguousarray(o_full[:, :, c * SEG:c * SEG + SEG + 2])
        maps2.append(d)
    nc2, _, _ = build({"ext_o": True})
    res2 = run_bass_kernel_spmd(nc2, maps2, core_ids=cores)
    ys = [np.asarray(res2.results[c]["y"]) for c in cores]
    full = np.concatenate(ys, axis=1)
    y_prompt = np.ascontiguousarray(full[0:1]).astype(np.float32)
    y_sample = np.ascontiguousarray(full[1:3]).astype(np.float32)
    return (y_prompt, y_sample)
```
